# Optimizing a Trainium2 kernel written in Bass

```python
import math
import jax, jax.numpy as jnp
from jax import lax
import numpy as np

D_MODEL = 1024
BATCH = 4
SEQ = 8192
DEPTH = 4

N_MIXERS = 3
N_A = (DEPTH + 2) // 3
N_B = (DEPTH + 1) // 3
N_C = DEPTH // 3

CONV_KERNEL = 31

DA_HEADS = 8
DA_HEAD_DIM = 64
Q_BLOCK = 128

GMLP_FFN = 6 * D_MODEL
GMLP_HALF = GMLP_FFN // 2
GMLP_GROUPS = 4
CHUNK = 128

FFN_DIM = ((8 * D_MODEL // 3 + 127) // 128) * 128
FFN_CONV = 3

ALPHA = (2 * DEPTH) ** 0.25
BETA = (8 * DEPTH) ** -0.25
LN_EPS = 1e-5
RMS_EPS = 1e-5

kernel_name = "hybrid_conv_diffattn_gmlp_deepnorm"


def layer_norm(x, g, b):
    xf = x.astype(jnp.float32)
    mu = jnp.mean(xf, axis=-1, keepdims=True)
    var = jnp.mean(jnp.square(xf - mu), axis=-1, keepdims=True)
    y = (xf - mu) * lax.rsqrt(var + LN_EPS) * g.astype(jnp.float32) + b.astype(jnp.float32)
    return y.astype(x.dtype)


def causal_dwconv(x, w, b):
    k_width, ch = w.shape
    y = lax.conv_general_dilated(
        x, w[:, None, :].astype(x.dtype), window_strides=(1,),
        padding=[(k_width - 1, 0)],
        dimension_numbers=("NWC", "WIO", "NWC"),
        feature_group_count=ch)
    return y + b


def conformer_conv(x, w_in, b_in, w_dw, b_dw, ln_g, ln_b, w_out, b_out):
    h = x @ w_in + b_in
    a, g = jnp.split(h, 2, axis=-1)
    h = a * jax.nn.sigmoid(g)
    h = causal_dwconv(h, w_dw, b_dw)
    h = jax.nn.silu(layer_norm(h, ln_g, ln_b))
    return h @ w_out + b_out


def diff_attention(x, w_qkv, lq1, lk1, lq2, lk2, subln_g, w_o, lambda_init):
    bsz, seq, _ = x.shape
    f32 = jnp.float32
    qkv = x @ w_qkv
    q, k, v = jnp.split(qkv, 3, axis=-1)
    q = q.reshape(bsz, seq, DA_HEADS, 2, DA_HEAD_DIM)
    k = k.reshape(bsz, seq, DA_HEADS, 2, DA_HEAD_DIM)
    v = v.reshape(bsz, seq, DA_HEADS, 2 * DA_HEAD_DIM)
    lam = (jnp.exp(jnp.sum(lq1.astype(f32) * lk1.astype(f32)))
           - jnp.exp(jnp.sum(lq2.astype(f32) * lk2.astype(f32))) + lambda_init)
    n_blk = seq // Q_BLOCK
    q_blocks = jnp.moveaxis(
        q.reshape(bsz, n_blk, Q_BLOCK, DA_HEADS, 2, DA_HEAD_DIM), 1, 0)
    k_pos = jnp.arange(seq)
    scale = DA_HEAD_DIM ** -0.5

    def attend(args):
        qb, blk = args
        s = jnp.einsum("bqhcd,bkhcd->bhcqk", qb, k,
                       preferred_element_type=f32) * scale
        q_pos = blk * Q_BLOCK + jnp.arange(Q_BLOCK)
        mask = k_pos[None, :] <= q_pos[:, None]
        p = jax.nn.softmax(jnp.where(mask, s, -jnp.inf), axis=-1)
        a = p[:, :, 0] - lam * p[:, :, 1]
        return jnp.einsum("bhqk,bkhe->bqhe", a.astype(v.dtype), v)

    o = lax.map(attend, (q_blocks, jnp.arange(n_blk)))
    o = jnp.moveaxis(o, 0, 1).reshape(bsz, seq, DA_HEADS, 2 * DA_HEAD_DIM)
    of = o.astype(f32)
    of = (of * lax.rsqrt(jnp.mean(of * of, axis=-1, keepdims=True) + RMS_EPS)
          * subln_g.astype(f32) * (1.0 - lambda_init))
    return of.astype(x.dtype).reshape(bsz, seq, DA_HEADS * 2 * DA_HEAD_DIM) @ w_o


def chunked_gmlp(x, w_in, b_in, ln_g, ln_b, w_s, b_s, w_out, b_out):
    bsz, seq, _ = x.shape
    z = jax.nn.gelu(x @ w_in + b_in, approximate=False)
    u, v = jnp.split(z, 2, axis=-1)
    v = layer_norm(v, ln_g, ln_b)
    v = v.reshape(bsz, seq // CHUNK, CHUNK, GMLP_GROUPS, GMLP_HALF // GMLP_GROUPS)
    tril = jnp.tril(jnp.ones((CHUNK, CHUNK), dtype=bool))
    w = jnp.where(tril[None], w_s, jnp.zeros_like(w_s))
    sv = jnp.einsum("gts,bnsgc->bntgc", w, v) + b_s.T[:, :, None]
    out = u * sv.reshape(bsz, seq, GMLP_HALF)
    return out @ w_out + b_out


def conv_ffn(x, w_up, b_up, w_dw, b_dw, w_down, b_down):
    h = x @ w_up + b_up
    h = causal_dwconv(h, w_dw, b_dw)
    g, val = jnp.split(h, 2, axis=-1)
    return (jax.nn.silu(g) * val) @ w_down + b_down


def _normal(k, shape, scale):
    return jax.random.normal(k, shape, jnp.float32) * scale


def setup_inputs(seed: int = 0) -> dict:
    key = jax.random.key(seed)
    ks = iter(jax.random.split(key, 48))
    D = D_MODEL
    F = FFN_DIM
    qkv_w = 3 * DA_HEADS * 2 * DA_HEAD_DIM
    attn_w = DA_HEADS * 2 * DA_HEAD_DIM
    return {
        "x": _normal(next(ks), (BATCH, SEQ, D), 1.0),
        "a_w_in": _normal(next(ks), (N_A, D, 2 * D), D ** -0.5),
        "a_b_in": _normal(next(ks), (N_A, 2 * D), 0.01),
        "a_w_dw": _normal(next(ks), (N_A, CONV_KERNEL, D), CONV_KERNEL ** -0.5),
        "a_b_dw": _normal(next(ks), (N_A, D), 0.01),
        "a_ln_g": 1.0 + _normal(next(ks), (N_A, D), 0.01),
        "a_ln_b": _normal(next(ks), (N_A, D), 0.01),
        "a_w_out": _normal(next(ks), (N_A, D, D), BETA * D ** -0.5),
        "a_b_out": _normal(next(ks), (N_A, D), 0.01),
        "b_w_qkv": _normal(next(ks), (N_B, D, qkv_w), D ** -0.5),
        "b_lq1": _normal(next(ks), (N_B, DA_HEAD_DIM), 0.1),
        "b_lk1": _normal(next(ks), (N_B, DA_HEAD_DIM), 0.1),
        "b_lq2": _normal(next(ks), (N_B, DA_HEAD_DIM), 0.1),
        "b_lk2": _normal(next(ks), (N_B, DA_HEAD_DIM), 0.1),
        "b_subln_g": 1.0 + _normal(next(ks), (N_B, 2 * DA_HEAD_DIM), 0.01),
        "b_w_o": _normal(next(ks), (N_B, attn_w, D), BETA * attn_w ** -0.5),
        "c_w_in": _normal(next(ks), (N_C, D, GMLP_FFN), D ** -0.5),
        "c_b_in": _normal(next(ks), (N_C, GMLP_FFN), 0.01),
        "c_ln_g": 1.0 + _normal(next(ks), (N_C, GMLP_HALF), 0.01),
        "c_ln_b": _normal(next(ks), (N_C, GMLP_HALF), 0.01),
        "c_w_s": _normal(next(ks), (N_C, GMLP_GROUPS, CHUNK, CHUNK), CHUNK ** -0.5),
        "c_b_s": 1.0 + _normal(next(ks), (N_C, GMLP_GROUPS, CHUNK), 0.01),
        "c_w_out": _normal(next(ks), (N_C, GMLP_HALF, D), BETA * GMLP_HALF ** -0.5),
        "c_b_out": _normal(next(ks), (N_C, D), 0.01),
        "f_w_up": _normal(next(ks), (DEPTH, D, 2 * F), D ** -0.5),
        "f_b_up": _normal(next(ks), (DEPTH, 2 * F), 0.01),
        "f_w_dw": _normal(next(ks), (DEPTH, FFN_CONV, 2 * F), FFN_CONV ** -0.5),
        "f_b_dw": _normal(next(ks), (DEPTH, 2 * F), 0.01),
        "f_w_down": _normal(next(ks), (DEPTH, F, D), BETA * F ** -0.5),
        "f_b_down": _normal(next(ks), (DEPTH, D), 0.01),
        "ln_mix_g": 1.0 + _normal(next(ks), (DEPTH, D), 0.01),
        "ln_mix_b": _normal(next(ks), (DEPTH, D), 0.01),
        "ln_ffn_g": 1.0 + _normal(next(ks), (DEPTH, D), 0.01),
        "ln_ffn_b": _normal(next(ks), (DEPTH, D), 0.01),
    }


def reference(x,
              a_w_in, a_b_in, a_w_dw, a_b_dw, a_ln_g, a_ln_b, a_w_out, a_b_out,
              b_w_qkv, b_lq1, b_lk1, b_lq2, b_lk2, b_subln_g, b_w_o,
              c_w_in, c_b_in, c_ln_g, c_ln_b, c_w_s, c_b_s, c_w_out, c_b_out,
              f_w_up, f_b_up, f_w_dw, f_b_dw, f_w_down, f_b_down,
              ln_mix_g, ln_mix_b, ln_ffn_g, ln_ffn_b):
    for i in range(DEPTH):
        kind = i % N_MIXERS
        j = i // N_MIXERS
        if kind == 0:
            y = conformer_conv(x, a_w_in[j], a_b_in[j], a_w_dw[j], a_b_dw[j],
                               a_ln_g[j], a_ln_b[j], a_w_out[j], a_b_out[j])
        elif kind == 1:
            lambda_init = 0.8 - 0.6 * math.exp(-0.3 * i)
            y = diff_attention(x, b_w_qkv[j], b_lq1[j], b_lk1[j], b_lq2[j], b_lk2[j],
                               b_subln_g[j], b_w_o[j], lambda_init)
        else:
            y = chunked_gmlp(x, c_w_in[j], c_b_in[j], c_ln_g[j], c_ln_b[j],
                             c_w_s[j], c_b_s[j], c_w_out[j], c_b_out[j])
        x = layer_norm(ALPHA * x + y, ln_mix_g[i], ln_mix_b[i])
        f = conv_ffn(x, f_w_up[i], f_b_up[i], f_w_dw[i], f_b_dw[i],
                     f_w_down[i], f_b_down[i])
        x = layer_norm(ALPHA * x + f, ln_ffn_g[i], ln_ffn_b[i])
    return x
```

```python
import contextlib
import math
import numpy as np
import ml_dtypes
import concourse.bass as bass
import concourse.mybir as mybir
from concourse.bass_utils import run_bass_kernel_spmd

F32 = mybir.dt.float32
BF16 = mybir.dt.bfloat16
BF16_NP = ml_dtypes.bfloat16
AF = mybir.ActivationFunctionType
ALU = mybir.AluOpType

D = 1024
DC = 8
DEPTH = 4
HALO = 256
NOWN_FULL = 4096
FF = 2816
FC = 22
CONVK = 31
GH = 3072
GC = 24
ALPHA = (2 * DEPTH) ** 0.25
LN_EPS = 1e-5
RMS_EPS = 1e-5
SEM_LIMIT = 60000
SELF_SYNC = True


class Tok:
    __slots__ = ("sem", "val")

    def __init__(self, sem, val):
        self.sem = sem
        self.val = val


class Trk:
    __slots__ = ("w", "r")

    def __init__(self):
        self.w = []
        self.r = []


class DSem:
    __slots__ = ("sem", "cnt")

    def __init__(self, sem):
        self.sem = sem
        self.cnt = 0


class Prog:
    def __init__(self, nc, es):
        self.nc = nc
        self.es = es
        self.engs = {"pe": nc.tensor, "act": nc.scalar, "dve": nc.vector, "pool": nc.gpsimd, "sp": nc.sync}
        self.q = {k: [] for k in self.engs}
        self.pool_sems = []
        n = 0
        while n < 96:
            try:
                self.pool_sems.append(es.enter_context(nc.semaphore("s%d" % n)))
            except Exception:
                break
            n += 1
        self.esem = {k: self.pool_sems.pop() for k in self.engs}
        self.dpool = [DSem(x) for x in self.pool_sems[8:]]
        self.pool_sems = self.pool_sems[:8]
        self.stage_ds = []
        self.ecnt = {k: 0 for k in self.engs}
        self.unsig = {k: False for k in self.engs}
        self.seen = {k: {} for k in self.engs}
        self.stage_dma = {}
        self.dsems = []
        self.ninst = 0

    def dsem(self):
        d = self.dpool.pop()
        self.stage_ds.append(d)
        return d

    def release_dsems(self):
        self.dpool.extend(self.stage_ds)
        self.stage_ds = []

    def _collect(self, eng, reads, writes, extra):
        need = {}

        def add(t):
            if t is None:
                return
            k = id(t.sem)
            if k not in need or need[k].val < t.val:
                need[k] = t
        for b in reads:
            for t in b.w:
                add(t)
        for b in writes:
            for t in b.w:
                add(t)
            for t in b.r:
                add(t)
        for t in extra:
            add(t)
        waits = []
        seen = self.seen[eng]
        own = self.esem[eng]
        for k, t in need.items():
            if t.sem is own and (eng == "pe" or not SELF_SYNC):
                continue
            if seen.get(k, 0) >= t.val:
                continue
            seen[k] = t.val
            waits.append(t)
        return waits

    @staticmethod
    def _commit(tok, reads, writes):
        for b in reads:
            b.r.append(tok)
            if len(b.r) > 64:
                best = {}
                for t in b.r:
                    k = id(t.sem)
                    if k not in best or best[k].val < t.val:
                        best[k] = t
                b.r = list(best.values())
        for b in writes:
            b.w = [tok]
            b.r = []

    def op(self, eng, name, kw, reads=(), writes=(), extra=(), signal=True):
        waits = self._collect(eng, reads, writes, extra)
        sem = self.esem[eng]
        if SELF_SYNC and eng != "pe":
            signal = True
        self.unsig[eng] = not signal
        if signal:
            self.ecnt[eng] += 1
            tok = Tok(sem, self.ecnt[eng])
        else:
            tok = Tok(sem, self.ecnt[eng] + 1)

        def emit(e):
            for t in waits:
                e.wait_ge(t.sem, t.val)
            ins = getattr(e, name)(**kw)
            if signal:
                ins.then_inc(sem, 1)
        self.q[eng].append(emit)
        self.ninst += 1
        self._commit(tok, reads, writes)
        if signal and self.ecnt[eng] >= SEM_LIMIT:
            self.esem[eng] = self.pool_sems.pop()
            self.ecnt[eng] = 0
        return tok

    def dma(self, eng, pairs, ds, reads=(), writes=(), extra=()):
        if ds.cnt + 16 * len(pairs) >= SEM_LIMIT:
            ds.sem = self.pool_sems.pop()
            ds.cnt = 0
        waits = self._collect(eng, reads, writes, extra)
        ds.cnt += 16 * len(pairs)
        sem = ds.sem
        tok = Tok(sem, ds.cnt)

        def emit(e):
            for t in waits:
                e.wait_ge(t.sem, t.val)
            for (o, i) in pairs:
                e.dma_start(out=o, in_=i).then_inc(sem, 16)
        self.q[eng].append(emit)
        self.ninst += len(pairs)
        self._commit(tok, reads, writes)
        self.stage_dma[id(sem)] = tok
        return tok

    def barrier(self):
        toks = []
        for k in self.engs:
            assert not self.unsig[k], k
            if self.ecnt[k] > 0:
                toks.append(Tok(self.esem[k], self.ecnt[k]))
        toks += list(self.stage_dma.values())
        self.stage_dma = {}
        for k in self.engs:
            waits = self._collect(k, (), (), toks)
            if waits:
                def emit(e, waits=waits):
                    for t in waits:
                        e.wait_ge(t.sem, t.val)
                self.q[k].append(emit)

    def finish(self):
        self.barrier()
        block = self.es.enter_context(self.nc.Block())
        q = self.q

        @block.tensor
        def _(e):
            for f in q["pe"]:
                f(e)

        @block.scalar
        def _(e):
            for f in q["act"]:
                f(e)

        @block.vector
        def _(e):
            for f in q["dve"]:
                f(e)

        @block.gpsimd
        def _(e):
            for f in q["pool"]:
                f(e)

        @block.sync
        def _(e):
            for f in q["sp"]:
                f(e)


class Cfg:
    def __init__(self, nown):
        self.nown = nown
        self.nt = HALO + nown
        self.tiles = [(0, HALO)] + [(HALO + 512 * i, 512) for i in range(nown // 512)]


class Ctx:
    def __init__(self, nc, es, cfg, ext_in, ext_out):
        self.nc = nc
        self.es = es
        self.cfg = cfg
        self.P = Prog(nc, es)
        self.ext_in = ext_in
        self.ext_out = ext_out
        self.dram = {}
        self.dtrk = {}
        self.kinds = {}
        self.internal = set()
        self.views = {}

    def dt(self, name, shape=None, dtype=None):
        if name in self.views:
            return self.views[name]
        if name not in self.dram:
            kind = "Internal"
            if name in self.ext_out:
                kind = "ExternalOutput"
            elif self.ext_in is None:
                if name not in self.internal:
                    kind = "ExternalInput"
            elif name in self.ext_in:
                kind = "ExternalInput"
            self.kinds[name] = kind
            self.dram[name] = self.nc.dram_tensor(name, list(shape), dtype, kind=kind).ap()
        return self.dram[name]

    def trk(self, name, idx):
        key = (name, idx)
        if key not in self.dtrk:
            self.dtrk[key] = Trk()
        return self.dtrk[key]


class Stage:
    def __init__(self, cx, name):
        self.cx = cx
        self.name = name
        self.es = contextlib.ExitStack()
        self.n = 0

    def __enter__(self):
        self.es.__enter__()
        return self

    def __exit__(self, *a):
        self.cx.P.barrier()
        self.cx.P.release_dsems()
        return self.es.__exit__(*a)

    def sb(self, shape, dtype, tag="t"):
        self.n += 1
        return self.es.enter_context(self.cx.nc.sbuf_tensor("%s_%s%d" % (self.name, tag, self.n), list(shape), dtype))

    def ps(self, shape, dtype, tag="p"):
        self.n += 1
        return self.es.enter_context(self.cx.nc.psum_tensor("%s_%s%d" % (self.name, tag, self.n), list(shape), dtype))


def load_consts(cx, st, specs, eng="sp"):
    P = cx.P
    ds = P.dsem()
    trk = Trk()
    ts = []
    pairs = []
    for (name, shape) in specs:
        src = cx.dt(name, shape, F32)
        t = st.sb(shape, F32, tag=name)
        ts.append(t)
        pairs.append((t[:], src))
    P.dma(eng, pairs, ds, writes=[trk])
    return ts, trk


class Epi:
    def __init__(self, cx, st, prefix, li, xres_in, xres_out, xT_out, final_out=None):
        P = cx.P
        self.cx, self.st = cx, st
        cfg = cx.cfg
        self.xres_in = cx.dt(xres_in, [cfg.nt, D], F32)
        self.xres_in_name = xres_in
        self.final_out = final_out
        if final_out is None:
            self.xres_out = cx.dt(xres_out, [cfg.nt, D], F32)
            self.xT_out = cx.dt(xT_out, [D, cfg.nt], BF16)
        else:
            self.xres_out = cx.dt(final_out, [cfg.nown, D], F32)
            self.xT_out = None
        self.xres_out_name = xres_out if final_out is None else final_out
        self.xT_out_name = xT_out
        (self.bb, self.gam, self.bet, self.ident), ct = load_consts(
            cx, st, [(prefix + "_bb", [128, D]), (prefix + "_g", [128, D]), (prefix + "_b", [128, D]),
                     ("ident_f32", [128, 128])])
        self.bb_t = self.gam_t = self.bet_t = self.ident_t = ct
        self.mh = st.sb([128, 1], F32, "mh")
        self.mh_t = Trk()
        P.op("pool", "memset", dict(ap=self.mh[:], constant=-0.5), writes=[self.mh_t])
        NR = 2
        self.xold = [st.sb([128, 4, D], F32, "xold") for _ in range(NR)]
        self.xold_t = [Trk() for _ in range(NR)]
        self.xold_ds = [P.dsem() for _ in range(NR)]
        self.s_ = [st.sb([128, D], F32, "s") for _ in range(2)]
        self.s_t = [Trk() for _ in range(2)]
        self.xn = [st.sb([128, D], F32, "xn") for _ in range(2)]
        self.xn_t = [Trk() for _ in range(2)]
        self.xnew = [st.sb([128, D], F32, "xnew") for _ in range(2)]
        self.xnew_t = [Trk() for _ in range(2)]
        self.xnew_ds = [P.dsem() for _ in range(2)]
        self.stat = [st.sb([128, 16], F32, "stat") for _ in range(2)]
        self.stat_t = [Trk() for _ in range(2)]
        self.xTo = [st.sb([128, DC, 512], BF16, "xTo") for _ in range(2)]
        self.xTo_t = [Trk() for _ in range(2)]
        self.xTo_ds = [P.dsem() for _ in range(2)]
        self.tp = [st.ps([128, D], F32, "tp") for _ in range(1)]
        self.tp_t = [Trk() for _ in range(1)]
        self.cnt = 0
        self.tile_cnt = 0

    def load_xold(self, ti):
        P = self.cx.P
        t0, W = self.cx.cfg.tiles[ti]
        slot = ti % len(self.xold)
        src = self.xres_in[t0:t0 + W, :].rearrange("(s p) d -> p s d", p=128)
        P.dma("sp", [(self.xold[slot][:, 0:W // 128, :], src)], self.xold_ds[slot],
              reads=[self.cx.trk(self.xres_in_name, (ti, s)) for s in range(W // 128)], writes=[self.xold_t[slot]])

    def sub(self, ti, s, y_ps, y_t, with_resid=True):
        cx, P = self.cx, self.cx.P
        t0, W = cx.cfg.tiles[ti]
        slot = ti % len(self.xold)
        i = self.cnt % 2
        self.cnt += 1
        xold = self.xold[slot][:, s, :]
        if y_ps is not None:
            xnew = self.xnew[i]
            s_ = self.s_[i]
            P.op("dve", "scalar_tensor_tensor", dict(out=s_[:], in0=xold, scalar=float(ALPHA), in1=self.bb[:],
                                                         op0=ALU.mult, op1=ALU.add),
                 reads=[self.xold_t[slot], self.bb_t], writes=[self.s_t[i]])
            for h in range(2):
                P.op("dve", "tensor_tensor", dict(out=s_[:, h * 512:(h + 1) * 512], in0=y_ps[:, h * 512:(h + 1) * 512],
                                                          in1=s_[:, h * 512:(h + 1) * 512], op=ALU.add),
                     reads=[y_t], writes=[self.s_t[i]], signal=(h == 1))
            stt = self.stat[i]
            for h in range(2):
                P.op("dve", "bn_stats", dict(out=stt[:, h * 6:(h + 1) * 6], in_=s_[:, h * 512:(h + 1) * 512]),
                     reads=[self.s_t[i]], writes=[self.stat_t[i]], signal=False)
            P.op("dve", "bn_aggr", dict(out=stt[:, 12:14], in_=stt[:, 0:12]), reads=[self.stat_t[i]], writes=[self.stat_t[i]])
            P.op("pool", "tensor_scalar", dict(out=stt[:, 14:15], in0=stt[:, 13:14], scalar1=float(LN_EPS), scalar2=None,
                                                   op0=ALU.add), reads=[self.stat_t[i]], writes=[self.stat_t[i]], signal=False)
            P.op("pool", "tensor_tensor", dict(out=stt[:, 14:15], in0=stt[:, 14:15], in1=self.mh[:], op=ALU.pow),
                 reads=[self.stat_t[i], self.mh_t], writes=[self.stat_t[i]])
            xn = self.xn[i]
            P.op("dve", "tensor_scalar", dict(out=xn[:], in0=s_[:], scalar1=stt[:, 12:13], scalar2=stt[:, 14:15],
                                                  op0=ALU.subtract, op1=ALU.mult),
                 reads=[self.s_t[i], self.stat_t[i]], writes=[self.xn_t[i]])
            P.op("pool", "tensor_tensor", dict(out=xn[:], in0=xn[:], in1=self.gam[:], op=ALU.mult),
                 reads=[self.xn_t[i], self.gam_t], writes=[self.xn_t[i]])
            P.op("pool", "tensor_tensor", dict(out=xnew[:], in0=xn[:], in1=self.bet[:], op=ALU.add),
                 reads=[self.xn_t[i], self.bet_t], writes=[self.xnew_t[i]])
            src_for_T = xnew
            src_t = self.xnew_t[i]
            if self.final_out is None:
                dst = self.xres_out[t0 + s * 128:t0 + (s + 1) * 128, :]
                P.dma("sp", [(dst, xnew[:])], self.xnew_ds[i], reads=[self.xnew_t[i]],
                      writes=[cx.trk(self.xres_out_name, (ti, s))])
            else:
                if t0 >= HALO:
                    o0 = t0 - HALO + s * 128
                    dst = self.xres_out[o0:o0 + 128, :]
                    P.dma("sp", [(dst, xnew[:])], self.xnew_ds[i], reads=[self.xnew_t[i]],
                          writes=[cx.trk(self.xres_out_name, (ti, s))])
            src_ap = xnew
        else:
            src_ap = None
            src_t = self.xold_t[slot]
        if self.xT_out is None:
            return
        j = self.tile_cnt % 2
        tp = self.tp[0]
        for k in range(DC):
            if src_ap is not None:
                inp = src_ap[:, k * 128:(k + 1) * 128]
            else:
                inp = self.xold[slot][:, s, k * 128:(k + 1) * 128]
            P.op("pe", "transpose", dict(out=tp[:, k * 128:(k + 1) * 128], in_=inp, identity=self.ident[:]),
                 reads=[src_t, self.ident_t], writes=[self.tp_t[0]], signal=(k == DC - 1))
        xTo = self.xTo[j]
        P.op("act", "copy", dict(out=xTo[:, :, s * 128:(s + 1) * 128], in_=tp[:].rearrange("p (k t) -> p k t", k=DC)),
             reads=[self.tp_t[0]], writes=[self.xTo_t[j]])
        if s == W // 128 - 1:
            dst = self.xT_out[:, t0:t0 + W].rearrange("(k p) t -> p k t", p=128)
            P.dma("sp", [(dst, xTo[:, :, 0:W])], self.xTo_ds[j], reads=[self.xTo_t[j]],
                  writes=[cx.trk(self.xT_out_name, ti)])
            self.tile_cnt += 1


def stage_prep(cx, x_in, xT_out):
    cfg = cx.cfg
    with Stage(cx, "prep") as st:
        ep = Epi.__new__(Epi)
        P = cx.P
        ep.cx, ep.st = cx, st
        ep.xres_in = cx.dt(x_in, [cfg.nt, D], F32)
        ep.xres_in_name = x_in
        ep.final_out = None
        ep.xT_out = cx.dt(xT_out, [D, cfg.nt], BF16)
        ep.xT_out_name = xT_out
        (ep.ident,), ep.ident_t = load_consts(cx, st, [("ident_f32", [128, 128])])
        ep.xold = [st.sb([128, 4, D], F32, "xold") for _ in range(2)]
        ep.xold_t = [Trk() for _ in range(2)]
        ep.xold_ds = [P.dsem() for _ in range(2)]
        ep.xTo = [st.sb([128, DC, 512], BF16, "xTo") for _ in range(2)]
        ep.xTo_t = [Trk() for _ in range(2)]
        ep.xTo_ds = [P.dsem() for _ in range(2)]
        ep.tp = [st.ps([128, D], F32, "tp")]
        ep.tp_t = [Trk()]
        ep.cnt = 0
        ep.tile_cnt = 0
        for ti, (t0, W) in enumerate(cfg.tiles):
            ep.load_xold(ti)
            for s in range(W // 128):
                ep.sub(ti, s, None, None)


def load_xT_resident(cx, st, xT_in):
    cfg = cx.cfg
    P = cx.P
    src = cx.dt(xT_in, [D, cfg.nt], BF16)
    xT = []
    trks = []
    for ti, (t0, W) in enumerate(cfg.tiles):
        ds = P.dsem()
        trk = Trk()
        xt = st.sb([128, DC, W], BF16, "xTres")
        P.dma("sp", [(xt[:], src[:, t0:t0 + W].rearrange("(k p) t -> p k t", p=128))], ds,
              reads=[cx.trk(xT_in, ti)], writes=[trk])
        trks.append(trk)
        xT.append(xt)
    return xT, trks


def stage_f1(cx, li, xT_in, uT_out):
    cfg = cx.cfg
    P = cx.P
    with Stage(cx, "f1_%d" % li) as st:
        uT = cx.dt(uT_out, [FF, cfg.nt], BF16)
        (cols, hm), cols_t = load_consts(cx, st, [("f%d_cols" % li, [128, 2 * FC, 5]), ("hm", [128, 1])])
        hm_t = cols_t
        xT, xT_t = load_xT_resident(cx, st, xT_in)
        wsrc = cx.dt("f%d_wup" % li, [2 * FC, 128, DC, 128], F32)
        NW = 4
        wb = [st.sb([128, DC, 128], BF16, "w") for _ in range(NW)]
        wb_t = [Trk() for _ in range(NW)]
        wst = [st.sb([128, DC, 128], F32, "wst") for _ in range(NW)]
        wst_t = [Trk() for _ in range(NW)]
        wb_ds = [P.dsem() for _ in range(NW)]
        gps = [st.ps([128, 512], F32, "g") for _ in range(2)]
        gps_t = [Trk() for _ in range(2)]
        vps = [st.ps([128, 512], F32, "v") for _ in range(2)]
        vps_t = [Trk() for _ in range(2)]
        hg = [st.sb([128, 514], F32, "hg") for _ in range(2)]
        hg_t = [Trk() for _ in range(2)]
        hv = [st.sb([128, 514], F32, "hv") for _ in range(2)]
        hv_t = [Trk() for _ in range(2)]
        cg = [st.sb([128, 512], F32, "cg") for _ in range(2)]
        cg_t = [Trk() for _ in range(2)]
        cv = [st.sb([128, 512], F32, "cv") for _ in range(2)]
        cv_t = [Trk() for _ in range(2)]
        sg = [st.sb([128, 512], F32, "sg") for _ in range(2)]
        sg_t = [Trk() for _ in range(2)]
        ub = [st.sb([128, 512], BF16, "u") for _ in range(2)]
        ub_t = [Trk() for _ in range(2)]
        ub_ds = [P.dsem() for _ in range(2)]
        wcnt = 0

        def load_w(j):
            nonlocal wcnt
            slot = wcnt % NW
            wcnt += 1
            P.dma("sp", [(wst[slot][:], wsrc[j])], wb_ds[slot], writes=[wst_t[slot]])
            P.op("pool", "tensor_copy", dict(out=wb[slot][:], in_=wst[slot][:]), reads=[wst_t[slot]], writes=[wb_t[slot]])
            return slot

        pending = [(load_w(0), load_w(FC))]
        it = 0
        for f in range(FC):
            if f + 1 < FC:
                pending.append((load_w(f + 1), load_w(FC + f + 1)))
            sg_slot, sv_slot = pending.pop(0)
            for ti, (t0, W) in enumerate(cfg.tiles):
                i = it % 2
                it += 1
                for (slot, pst, pstt, jj) in ((sg_slot, gps[i], gps_t[i], f), (sv_slot, vps[i], vps_t[i], FC + f)):
                    for k in range(DC):
                        P.op("pe", "matmul", dict(
                            out=pst[:, 0:W], lhsT=wb[slot][:, k, :], rhs=xT[ti][:, k, 0:W], start=(k == 0), stop=(k == DC - 1)),
                            reads=[wb_t[slot], xT_t[ti]], writes=[pstt], signal=(k == DC - 1))
                for (hb, hbt, pst, pstt, jj) in ((hg, hg_t, gps[i], gps_t[i], f), (hv, hv_t, vps[i], vps_t[i], FC + f)):
                    h = hb[i]
                    bcol = cols[:, jj, 0:1]
                    if ti == 0:
                        P.op("pool", "memset", dict(ap=h[:, 0:2], constant=0.0), writes=[hbt[i]])
                        P.op("dve", "tensor_scalar", dict(
                            out=h[:, 2:2 + W], in0=pst[:, 0:W], scalar1=bcol, scalar2=hm[:, 0:1], op0=ALU.add, op1=ALU.mult),
                            reads=[pstt, cols_t, hm_t], writes=[hbt[i]])
                    else:
                        hp = hb[1 - i]
                        Wp = cfg.tiles[ti - 1][1]
                        P.op("pool", "tensor_copy", dict(out=h[:, 0:2], in_=hp[:, Wp:Wp + 2]),
                             reads=[hbt[1 - i]], writes=[hbt[i]])
                        P.op("act", "activation", dict(
                            out=h[:, 2:2 + W], in_=pst[:, 0:W], func=AF.Identity, bias=bcol, scale=1.0),
                            reads=[pstt, cols_t], writes=[hbt[i]])
                for (hb, hbt, cb, cbt, jj) in ((hg, hg_t, cg, cg_t, f), (hv, hv_t, cv, cv_t, FC + f)):
                    h = hb[i]
                    c = cb[i]
                    P.op("act", "activation", dict(
                        out=c[:, 0:W], in_=h[:, 2:2 + W], func=AF.Identity, bias=cols[:, jj, 4:5], scale=cols[:, jj, 3:4]),
                        reads=[hbt[i], cols_t], writes=[cbt[i]])
                    P.op("dve", "scalar_tensor_tensor", dict(
                        out=c[:, 0:W], in0=h[:, 1:1 + W], scalar=cols[:, jj, 2:3], in1=c[:, 0:W], op0=ALU.mult, op1=ALU.add),
                        reads=[hbt[i], cbt[i]], writes=[cbt[i]], signal=False)
                    P.op("dve", "scalar_tensor_tensor", dict(
                        out=c[:, 0:W], in0=h[:, 0:W], scalar=cols[:, jj, 1:2], in1=c[:, 0:W], op0=ALU.mult, op1=ALU.add),
                        reads=[hbt[i], cbt[i]], writes=[cbt[i]])
                P.op("act", "activation", dict(out=sg[i][:, 0:W], in_=cg[i][:, 0:W], func=AF.Silu),
                     reads=[cg_t[i]], writes=[sg_t[i]])
                P.op("dve", "tensor_tensor", dict(out=ub[i][:, 0:W], in0=sg[i][:, 0:W], in1=cv[i][:, 0:W], op=ALU.mult),
                     reads=[sg_t[i], cv_t[i]], writes=[ub_t[i]])
                P.dma("sp", [(uT[f * 128:(f + 1) * 128, t0:t0 + W], ub[i][:, 0:W])], ub_ds[i], reads=[ub_t[i]],
                      writes=[cx.trk(uT_out, (f, ti))])


def load_w_rows(cx, st, wd, wd_t, wsrc, KC, width=D):
    P = cx.P
    stg = [st.sb([128, 2048], F32, "wstg") for _ in range(2)]
    stg_t = [Trk() for _ in range(2)]
    stg_ds = [P.dsem() for _ in range(2)]
    n = 0
    if width <= 1024:
        pieces = [(k0, min(KC, k0 + 2048 // width), 0, width) for k0 in range(0, KC, 2048 // width)]
    else:
        pieces = [(k, k + 1, w0, min(width, w0 + 2048)) for k in range(KC) for w0 in range(0, width, 2048)]
    for (k0, k1, w0, w1) in pieces:
        i = n % 2
        n += 1
        nk, nw = k1 - k0, w1 - w0
        sview = stg[i][:, 0:nk * nw].rearrange("p (k w) -> p k w", k=nk)
        P.dma("sp", [(sview, wsrc[:, k0:k1, w0:w1])], stg_ds[i], writes=[stg_t[i]])
        if i == 0:
            P.op("pool", "tensor_copy", dict(out=wd[:, k0:k1, w0:w1], in_=sview), reads=[stg_t[i]], writes=[wd_t])
        else:
            P.op("act", "copy", dict(out=wd[:, k0:k1, w0:w1], in_=sview), reads=[stg_t[i]], writes=[wd_t])


def stage_proj(cx, name, prefix, li, aT_in, KC, w_name, xres_in, xres_out, xT_out, final_out=None):
    cfg = cx.cfg
    P = cx.P
    with Stage(cx, name) as st:
        aT = cx.dt(aT_in, [KC * 128, cfg.nt], BF16)
        wsrc = cx.dt(w_name, [128, KC, D], F32)
        wd = st.sb([128, KC, D], BF16, "wd")
        wd_t = Trk()
        load_w_rows(cx, st, wd, wd_t, wsrc, KC)
        ep = Epi(cx, st, prefix, li, xres_in, xres_out, xT_out, final_out=final_out)
        NA = 2
        ab = [st.sb([128, KC, 512], BF16, "a") for _ in range(NA)]
        ab_t = [Trk() for _ in range(NA)]
        ab_ds = [P.dsem() for _ in range(NA)]
        yps = [st.ps([128, D], F32, "y") for _ in range(2)]
        yps_t = [Trk() for _ in range(2)]
        tiles = list(enumerate(cfg.tiles))
        if final_out is not None:
            tiles = tiles[1:]

        def load_a(ti):
            t0, W = cfg.tiles[ti]
            slot = ti % NA
            reads = [cx.trk(aT_in, (k, ti)) for k in range(KC)]
            P.dma("sp", [(ab[slot][:, :, 0:W], aT[:, t0:t0 + W].rearrange("(k p) t -> p k t", p=128))], ab_ds[slot],
                  reads=reads, writes=[ab_t[slot]])
            ep.load_xold(ti)
        load_a(tiles[0][0])
        cnt = 0
        for n, (ti, (t0, W)) in enumerate(tiles):
            if n + 1 < len(tiles):
                load_a(tiles[n + 1][0])
            slot = ti % NA
            for s in range(W // 128):
                i = cnt % 2
                cnt += 1
                for h in range(2):
                    for k in range(KC):
                        P.op("pe", "matmul", dict(
                            out=yps[i][:, h * 512:(h + 1) * 512], lhsT=ab[slot][:, k, s * 128:(s + 1) * 128],
                            rhs=wd[:, k, h * 512:(h + 1) * 512], start=(k == 0), stop=(k == KC - 1)),
                            reads=[ab_t[slot], wd_t], writes=[yps_t[i]], signal=(k == KC - 1))
                ep.sub(ti, s, yps[i], yps_t[i])


def build_program(cfg, stages, ext_in, ext_out, internal=(), want_names=False):
    nc = bass.Bass("TRN2", target_bir_lowering=False)
    es = contextlib.ExitStack()
    with es:
        cx = Ctx(nc, es, cfg, None if ext_in is None else set(ext_in), set(ext_out))
        cx.internal = set(internal)
        for fn in stages:
            fn(cx)
        cx.P.finish()
        ninst = cx.P.ninst
    if want_names:
        return nc, ninst, [k for k, v in cx.kinds.items() if v == "ExternalInput"]
    return nc, ninst


def lay_cols(v):
    return np.ascontiguousarray(v.reshape(-1, 128).T)


def lay_bcast(v):
    return np.ascontiguousarray(np.broadcast_to(v[None, :], (128, v.shape[0])))


def lay_w_chunks(w):
    K, N = w.shape
    return np.ascontiguousarray(w.reshape(K // 128, 128, N // 128, 128).transpose(2, 1, 0, 3))


def lay_w_rows(w):
    K, N = w.shape
    return np.ascontiguousarray(w.reshape(K // 128, 128, N).transpose(1, 0, 2))


def host_consts(inp):
    c = {}
    c["ident_f32"] = np.eye(128, dtype=np.float32)
    for li in range(DEPTH):
        bup = inp["f_b_up"][li]
        wdw = inp["f_w_dw"][li]
        bdw = inp["f_b_dw"][li]
        cols = np.stack([lay_cols(bup), lay_cols(wdw[0]), lay_cols(wdw[1]), lay_cols(wdw[2]), lay_cols(bdw)], axis=-1)
        c["f%d_cols" % li] = np.ascontiguousarray(cols.astype(np.float32))
        c["f%d_wup" % li] = lay_w_chunks(inp["f_w_up"][li])
        c["f%d_wdn" % li] = lay_w_rows(inp["f_w_down"][li])
        c["f%d_bb" % li] = lay_bcast(inp["f_b_down"][li])
        c["f%d_g" % li] = lay_bcast(inp["ln_ffn_g"][li])
        c["f%d_b" % li] = lay_bcast(inp["ln_ffn_b"][li])
    for j in range(inp["a_w_in"].shape[0]):
        bi = inp["a_b_in"][j]
        wdw = inp["a_w_dw"][j]
        cols = np.concatenate([lay_cols(bi[:D])[:, :, None], lay_cols(bi[D:])[:, :, None], lay_cols(inp["a_b_dw"][j])[:, :, None],
                               np.stack([lay_cols(wdw[k]) for k in range(CONVK)], axis=-1)], axis=-1)
        c["a%d_cols" % j] = np.ascontiguousarray(cols.astype(np.float32))
        c["a%d_win" % j] = lay_w_chunks(inp["a_w_in"][j])
        c["a%d_wout" % j] = lay_w_rows(inp["a_w_out"][j])
        c["a%d_lcols" % j] = np.ascontiguousarray(np.stack([lay_cols(inp["a_ln_g"][j]), lay_cols(inp["a_ln_b"][j])], axis=-1))
    c["ident_bf16"] = np.eye(128, dtype=np.float32).astype(BF16_NP)
    wqkv = inp["b_w_qkv"][0]
    c["b0_wqk"] = lay_w_chunks(wqkv[:, :2 * D])
    c["b0_wv"] = lay_w_rows(wqkv[:, 2 * D:])
    c["b0_wo"] = lay_w_rows(inp["b_w_o"][0])
    lam = np.stack([inp["b_lq1"][0], inp["b_lk1"][0], inp["b_lq2"][0], inp["b_lk2"][0]], axis=0)
    c["b0_lam"] = np.ascontiguousarray(np.broadcast_to(lam[None], (128, 4, 64))).astype(np.float32)
    c["b0_gsub"] = lay_bcast(inp["b_subln_g"][0])
    win = inp["c_w_in"][0]
    c["c0_wu"] = lay_w_chunks(win[:, :GH])
    c["c0_wv"] = lay_w_rows(win[:, GH:])
    c["c0_wo"] = lay_w_rows(inp["c_w_out"][0])
    c["c0_ucols"] = lay_cols(inp["c_b_in"][0][:GH])
    c["c0_bvb"] = lay_bcast(inp["c_b_in"][0][GH:])
    c["c0_gcols"] = np.ascontiguousarray(np.stack([lay_cols(inp["c_ln_g"][0]), lay_cols(inp["c_ln_b"][0])], axis=-1))
    c["c0_wsT"] = np.ascontiguousarray(inp["c_w_s"][0].transpose(2, 0, 1))
    c["c0_trim"] = (np.arange(128)[None, :] >= np.arange(128)[:, None]).astype(np.float32)
    c["c0_bsb"] = np.ascontiguousarray(np.broadcast_to(inp["c_b_s"][0][None], (128, 4, 128))).astype(np.float32)
    pp = np.arange(128)[:, None, None]
    mm = np.arange(4)[None, :, None]
    cc = np.arange(512)[None, None, :]
    c["trimask"] = (cc >= pp + 128 * mm).astype(np.float32).astype(BF16_NP)
    for li in range(DEPTH):
        kind, j = li % 3, li // 3
        bo = [inp["a_b_out"], None, inp["c_b_out"]][kind]
        c["m%d_bb" % li] = lay_bcast(bo[j]) if bo is not None else np.zeros((128, D), np.float32)
        c["m%d_g" % li] = lay_bcast(inp["ln_mix_g"][li])
        c["m%d_b" % li] = lay_bcast(inp["ln_mix_b"][li])
    return c


class ChunkLoader:
    def __init__(self, cx, st, wsrc, nslots=4, cast_eng="pool"):
        P = cx.P
        self.cx, self.wsrc = cx, wsrc
        self.n = nslots
        self.wb = [st.sb([128, DC, 128], BF16, "w") for _ in range(nslots)]
        self.wb_t = [Trk() for _ in range(nslots)]
        self.wst = [st.sb([128, DC, 128], F32, "wst") for _ in range(nslots)]
        self.wst_t = [Trk() for _ in range(nslots)]
        self.ds = [P.dsem() for _ in range(nslots)]
        self.cnt = 0
        self.cast_eng = cast_eng

    def load(self, j):
        P = self.cx.P
        slot = self.cnt % self.n
        self.cnt += 1
        P.dma("sp", [(self.wst[slot][:], self.wsrc[j])], self.ds[slot], writes=[self.wst_t[slot]])
        if self.cast_eng == "pool":
            P.op("pool", "tensor_copy", dict(out=self.wb[slot][:], in_=self.wst[slot][:]),
                 reads=[self.wst_t[slot]], writes=[self.wb_t[slot]])
        else:
            P.op("act", "copy", dict(out=self.wb[slot][:], in_=self.wst[slot][:]),
                 reads=[self.wst_t[slot]], writes=[self.wb_t[slot]])
        return slot


def stage_c1(cx, j_idx, xT_in, cv_out):
    cfg = cx.cfg
    P = cx.P
    with Stage(cx, "c1_%d" % j_idx) as st:
        cv = cx.dt(cv_out, [cfg.nt, D], F32)
        (cols, hm, ident), cols_t = load_consts(cx, st, [("a%d_cols" % j_idx, [128, DC, 34]), ("hm", [128, 1]),
                                                         ("ident_f32", [128, 128])])
        hb = st.sb([128, DC, 1], F32, "hb")
        wdh = st.sb([128, DC, CONVK], F32, "wdh")
        d_t = Trk()
        P.op("pool", "tensor_scalar", dict(out=hb[:], in0=cols[:, :, 1:2], scalar1=0.5, scalar2=None, op0=ALU.mult),
             reads=[cols_t], writes=[d_t])
        d2_t = Trk()
        P.op("pool", "tensor_scalar", dict(out=wdh[:], in0=cols[:, :, 3:34], scalar1=0.5, scalar2=None, op0=ALU.mult),
             reads=[cols_t], writes=[d2_t])
        xT, xT_t = load_xT_resident(cx, st, xT_in)
        wl = ChunkLoader(cx, st, cx.dt("a%d_win" % j_idx, [2 * DC, 128, DC, 128], F32))
        aps = [st.ps([128, 512], F32, "a") for _ in range(2)]
        aps_t = [Trk() for _ in range(2)]
        gps = [st.ps([128, 512], F32, "g") for _ in range(2)]
        gps_t = [Trk() for _ in range(2)]
        tps = [st.ps([128, 512], F32, "tp") for _ in range(2)]
        tps_t = [Trk() for _ in range(2)]
        th = [st.sb([128, 512], F32, "th") for _ in range(2)]
        th_t = [Trk() for _ in range(2)]
        asb = [st.sb([128, 512], F32, "asb") for _ in range(2)]
        asb_t = [Trk() for _ in range(2)]
        hbuf = [st.sb([128, 30 + 512], F32, "h") for _ in range(2)]
        hbuf_t = [Trk() for _ in range(2)]
        acc1 = [st.sb([128, 512], F32, "acc1") for _ in range(2)]
        acc1_t = [Trk() for _ in range(2)]
        acc2 = [st.sb([128, 512], F32, "acc2") for _ in range(2)]
        acc2_t = [Trk() for _ in range(2)]
        ct = [st.sb([128, 4, 128], F32, "ct") for _ in range(2)]
        ct_t = [Trk() for _ in range(2)]
        ct_ds = [P.dsem() for _ in range(2)]
        pending = [(wl.load(0), wl.load(DC))]
        it = 0
        for j in range(DC):
            if j + 1 < DC:
                pending.append((wl.load(j + 1), wl.load(DC + j + 1)))
            sa, sg = pending.pop(0)
            for ti, (t0, W) in enumerate(cfg.tiles):
                i = it % 2
                it += 1
                for (slot, pst, pstt) in ((sa, aps[i], aps_t[i]), (sg, gps[i], gps_t[i])):
                    for k in range(DC):
                        P.op("pe", "matmul", dict(out=pst[:, 0:W], lhsT=wl.wb[slot][:, k, :], rhs=xT[ti][:, k, 0:W],
                                                  start=(k == 0), stop=(k == DC - 1)),
                             reads=[wl.wb_t[slot], xT_t[ti]], writes=[pstt], signal=(k == DC - 1))
                P.op("act", "activation", dict(out=th[i][:, 0:W], in_=gps[i][:, 0:W], func=AF.Tanh, bias=hb[:, j, :], scale=0.5),
                     reads=[gps_t[i], d_t], writes=[th_t[i]])
                P.op("act", "activation", dict(out=asb[i][:, 0:W], in_=aps[i][:, 0:W], func=AF.Identity, bias=cols[:, j, 0:1],
                                               scale=1.0), reads=[aps_t[i], cols_t], writes=[asb_t[i]])
                h = hbuf[i]
                if ti == 0:
                    P.op("pool", "memset", dict(ap=h[:, 0:30], constant=0.0), writes=[hbuf_t[i]])
                else:
                    Wp = cfg.tiles[ti - 1][1]
                    P.op("pool", "tensor_copy", dict(out=h[:, 0:30], in_=hbuf[1 - i][:, Wp:Wp + 30]),
                         reads=[hbuf_t[1 - i]], writes=[hbuf_t[i]])
                P.op("dve", "scalar_tensor_tensor", dict(out=h[:, 30:30 + W], in0=th[i][:, 0:W], scalar=1.0, in1=asb[i][:, 0:W],
                                                         op0=ALU.add, op1=ALU.mult),
                     reads=[th_t[i], asb_t[i]], writes=[hbuf_t[i]])
                if ti == 0:
                    P.op("dve", "tensor_scalar", dict(out=h[:, 30:30 + W], in0=h[:, 30:30 + W], scalar1=hm[:, 0:1], scalar2=None,
                                                      op0=ALU.mult), reads=[cols_t], writes=[hbuf_t[i]])
                a1, a2 = acc1[i], acc2[i]
                P.op("dve", "tensor_scalar", dict(out=a1[:, 0:W], in0=h[:, 30:30 + W], scalar1=wdh[:, j, 30:31],
                                                  scalar2=cols[:, j, 2:3], op0=ALU.mult, op1=ALU.add),
                     reads=[hbuf_t[i], d2_t, cols_t], writes=[acc1_t[i]])
                P.op("dve", "tensor_scalar", dict(out=a2[:, 0:W], in0=h[:, 29:29 + W], scalar1=wdh[:, j, 29:30], scalar2=None,
                                                  op0=ALU.mult), reads=[hbuf_t[i], d2_t], writes=[acc2_t[i]])
                for k in range(28, -1, -1):
                    a, at = (a1, acc1_t[i]) if k % 2 == 0 else (a2, acc2_t[i])
                    P.op("dve", "scalar_tensor_tensor", dict(out=a[:, 0:W], in0=h[:, k:k + W], scalar=wdh[:, j, k:k + 1],
                                                             in1=a[:, 0:W], op0=ALU.mult, op1=ALU.add),
                         reads=[hbuf_t[i], at], writes=[at])
                P.op("dve", "tensor_tensor", dict(out=a1[:, 0:W], in0=a1[:, 0:W], in1=a2[:, 0:W], op=ALU.add),
                     reads=[acc1_t[i], acc2_t[i]], writes=[acc1_t[i]])
                ns = W // 128
                for s in range(ns):
                    P.op("pe", "transpose", dict(out=tps[i][:, s * 128:(s + 1) * 128], in_=a1[:, s * 128:(s + 1) * 128],
                                                 identity=ident[:]),
                         reads=[acc1_t[i], cols_t], writes=[tps_t[i]], signal=(s == ns - 1))
                P.op("act", "copy", dict(out=ct[i][:, 0:ns, :], in_=tps[i][:, 0:W].rearrange("p (s c) -> p s c", s=ns)),
                     reads=[tps_t[i]], writes=[ct_t[i]])
                P.dma("sp", [(cv[t0:t0 + W, j * 128:(j + 1) * 128].rearrange("(s p) c -> p s c", p=128), ct[i][:, 0:ns, :])],
                      ct_ds[i], reads=[ct_t[i]], writes=[cx.trk(cv_out, (ti, j))])


def stage_tm_proj(cx, name, prefix, src_name, src_dtype, prenorm, cols_name, w_name, xres_in, xres_out, xT_out):
    cfg = cx.cfg
    P = cx.P
    with Stage(cx, name) as st:
        src = cx.dt(src_name, [cfg.nt, D], src_dtype)
        wsrc = cx.dt(w_name, [128, DC, D], F32)
        wd = st.sb([128, DC, D], BF16, "wd")
        wd_t = Trk()
        load_w_rows(cx, st, wd, wd_t, wsrc, DC)
        ep = Epi(cx, st, prefix, 0, xres_in, xres_out, xT_out)
        ds = P.dsem()
        identb = st.sb([128, 128], BF16, "identb")
        identb_t = Trk()
        pairs = [(identb[:], cx.dt("ident_bf16", [128, 128], BF16))]
        if prenorm:
            lcols = st.sb([128, DC, 2], F32, "lcols")
            pairs.append((lcols[:], cx.dt(cols_name, [128, DC, 2], F32)))
        P.dma("sp", pairs, ds, writes=[identb_t])
        NA = 2
        ib = [st.sb([128, 4, D], src_dtype, "in") for _ in range(NA)]
        ib_t = [Trk() for _ in range(NA)]
        ib_ds = [P.dsem() for _ in range(NA)]
        xb = [st.sb([128, D], BF16, "xb") for _ in range(2)]
        xb_t = [Trk() for _ in range(2)]
        stat = [st.sb([128, 16], F32, "pstat") for _ in range(2)]
        stat_t = [Trk() for _ in range(2)]
        zT = [st.sb([128, DC, 128], BF16, "zT") for _ in range(2)]
        zT_t = [Trk() for _ in range(2)]
        tpb = [st.ps([128, D], BF16, "tpb") for _ in range(2)]
        tpb_t = [Trk() for _ in range(2)]
        yps = [st.ps([128, D], F32, "y") for _ in range(2)]
        yps_t = [Trk() for _ in range(2)]

        def load_in(ti):
            t0, W = cfg.tiles[ti]
            slot = ti % NA
            reads = [cx.trk(src_name, (ti, j)) for j in range(DC)]
            P.dma("sp", [(ib[slot][:, 0:W // 128, :], src[t0:t0 + W, :].rearrange("(s p) d -> p s d", p=128))], ib_ds[slot],
                  reads=reads, writes=[ib_t[slot]])
            ep.load_xold(ti)
        load_in(0)
        cnt = 0
        for ti, (t0, W) in enumerate(cfg.tiles):
            if ti + 1 < len(cfg.tiles):
                load_in(ti + 1)
            slot = ti % NA
            for s in range(W // 128):
                i = cnt % 2
                cnt += 1
                xin = ib[slot][:, s, :]
                if prenorm:
                    stt = stat[i]
                    for h in range(2):
                        P.op("dve", "bn_stats", dict(out=stt[:, h * 6:(h + 1) * 6], in_=xin[:, h * 512:(h + 1) * 512]),
                             reads=[ib_t[slot]], writes=[stat_t[i]])
                    P.op("dve", "bn_aggr", dict(out=stt[:, 12:14], in_=stt[:, 0:12]), reads=[stat_t[i]], writes=[stat_t[i]])
                    P.op("pool", "tensor_scalar", dict(out=stt[:, 14:15], in0=stt[:, 13:14], scalar1=float(LN_EPS), scalar2=None,
                                                       op0=ALU.add), reads=[stat_t[i]], writes=[stat_t[i]])
                    P.op("pool", "tensor_tensor", dict(out=stt[:, 14:15], in0=stt[:, 14:15], in1=ep.mh[:], op=ALU.pow),
                         reads=[stat_t[i], ep.mh_t], writes=[stat_t[i]])
                    P.op("dve", "tensor_scalar", dict(out=xb[i][:], in0=xin, scalar1=stt[:, 12:13], scalar2=stt[:, 14:15],
                                                      op0=ALU.subtract, op1=ALU.mult),
                         reads=[ib_t[slot], stat_t[i]], writes=[xb_t[i]])
                    tin, tin_t = xb[i], xb_t[i]
                    tin_ap = lambda k, i=i: xb[i][:, k * 128:(k + 1) * 128]
                else:
                    tin_t = ib_t[slot]
                    tin_ap = lambda k, slot=slot, s=s: ib[slot][:, s, k * 128:(k + 1) * 128]
                for k in range(DC):
                    P.op("pe", "transpose", dict(out=tpb[i][:, k * 128:(k + 1) * 128], in_=tin_ap(k), identity=identb[:]),
                         reads=[tin_t, identb_t], writes=[tpb_t[i]], signal=(k == DC - 1))
                if prenorm:
                    for k in range(DC):
                        P.op("act", "activation", dict(out=zT[i][:, k, :], in_=tpb[i][:, k * 128:(k + 1) * 128], func=AF.Silu,
                                                       bias=lcols[:, k, 1:2], scale=lcols[:, k, 0:1]),
                             reads=[tpb_t[i], identb_t], writes=[zT_t[i]])
                else:
                    P.op("act", "copy", dict(out=zT[i][:], in_=tpb[i][:].rearrange("p (k t) -> p k t", k=DC)),
                         reads=[tpb_t[i]], writes=[zT_t[i]])
                for h in range(2):
                    for k in range(DC):
                        P.op("pe", "matmul", dict(out=yps[i][:, h * 512:(h + 1) * 512], lhsT=zT[i][:, k, :],
                                                  rhs=wd[:, k, h * 512:(h + 1) * 512], start=(k == 0), stop=(k == DC - 1)),
                             reads=[zT_t[i], wd_t], writes=[yps_t[i]], signal=(k == DC - 1))
                ep.sub(ti, s, yps[i], yps_t[i])


def stage_a1(cx, xT_in, qT_out, kT_out, v_out):
    cfg = cx.cfg
    P = cx.P
    with Stage(cx, "a1") as st:
        qT = cx.dt(qT_out, [D, cfg.nt], BF16)
        kT = cx.dt(kT_out, [D, cfg.nown], BF16)
        v = cx.dt(v_out, [cfg.nown, D], BF16)
        xT, xT_t = load_xT_resident(cx, st, xT_in)
        wl = ChunkLoader(cx, st, cx.dt("b0_wqk", [2 * DC, 128, DC, 128], F32))
        wv = st.sb([128, DC, D], BF16, "wv")
        wv_t = Trk()
        load_w_rows(cx, st, wv, wv_t, cx.dt("b0_wv", [128, DC, D], F32), DC)
        ps = [st.ps([128, 512], F32, "ps") for _ in range(2)]
        ps_t = [Trk() for _ in range(2)]
        ob = [st.sb([128, 512], BF16, "ob") for _ in range(2)]
        ob_t = [Trk() for _ in range(2)]
        ob_ds = [P.dsem() for _ in range(2)]
        vps = [st.ps([128, D], F32, "vps") for _ in range(2)]
        vps_t = [Trk() for _ in range(2)]
        vb = [st.sb([128, D], BF16, "vb") for _ in range(2)]
        vb_t = [Trk() for _ in range(2)]
        vb_ds = [P.dsem() for _ in range(2)]
        pending = [wl.load(0)]
        it = 0
        for j in range(2 * DC):
            if j + 1 < 2 * DC:
                pending.append(wl.load(j + 1))
            slot = pending.pop(0)
            for ti, (t0, W) in enumerate(cfg.tiles):
                if j >= DC and ti == 0:
                    continue
                i = it % 2
                it += 1
                for k in range(DC):
                    P.op("pe", "matmul", dict(out=ps[i][:, 0:W], lhsT=wl.wb[slot][:, k, :], rhs=xT[ti][:, k, 0:W],
                                              start=(k == 0), stop=(k == DC - 1)),
                         reads=[wl.wb_t[slot], xT_t[ti]], writes=[ps_t[i]], signal=(k == DC - 1))
                if it % 2 == 0:
                    P.op("act", "copy", dict(out=ob[i][:, 0:W], in_=ps[i][:, 0:W]), reads=[ps_t[i]], writes=[ob_t[i]])
                else:
                    P.op("dve", "tensor_copy", dict(out=ob[i][:, 0:W], in_=ps[i][:, 0:W]), reads=[ps_t[i]], writes=[ob_t[i]])
                if j < DC:
                    dst = qT[j * 128:(j + 1) * 128, t0:t0 + W]
                    key = (qT_out, (j, ti))
                else:
                    dst = kT[(j - DC) * 128:(j - DC + 1) * 128, t0 - HALO:t0 - HALO + W]
                    key = (kT_out, (j - DC, ti))
                P.dma("sp", [(dst, ob[i][:, 0:W])], ob_ds[i], reads=[ob_t[i]], writes=[cx.trk(*key)])
        cnt = 0
        for ti, (t0, W) in enumerate(cfg.tiles):
            if ti == 0:
                continue
            for s in range(W // 128):
                i = cnt % 2
                cnt += 1
                for h in range(2):
                    for k in range(DC):
                        P.op("pe", "matmul", dict(out=vps[i][:, h * 512:(h + 1) * 512], lhsT=xT[ti][:, k, s * 128:(s + 1) * 128],
                                                  rhs=wv[:, k, h * 512:(h + 1) * 512], start=(k == 0), stop=(k == DC - 1)),
                             reads=[xT_t[ti], wv_t], writes=[vps_t[i]], signal=(k == DC - 1))
                if cnt % 2 == 0:
                    P.op("act", "copy", dict(out=vb[i][:], in_=vps[i][:]), reads=[vps_t[i]], writes=[vb_t[i]])
                else:
                    P.op("dve", "tensor_copy", dict(out=vb[i][:], in_=vps[i][:]), reads=[vps_t[i]], writes=[vb_t[i]])
                o0 = t0 - HALO + s * 128
                P.dma("sp", [(v[o0:o0 + 128, :], vb[i][:])], vb_ds[i], reads=[vb_t[i]], writes=[cx.trk(v_out, (ti, s))])


def stage_a2(cx, li, qT_in, kT_own, kT_past, v_own, v_past, attn_out):
    cfg = cx.cfg
    P = cx.P
    nown = cfg.nown
    NKP = nown // 128
    lam_init = 0.8 - 0.6 * math.exp(-0.3 * li)
    with Stage(cx, "a2") as st:
        qT = cx.dt(qT_in, [D, cfg.nt], BF16)
        kTo = cx.dt(kT_own, [D, nown], BF16)
        vo = cx.dt(v_own, [nown, D], BF16)
        if "kpast_fn" in cx.views:
            kpast_fn, vpast_fn = cx.views["kpast_fn"], cx.views["vpast_fn"]
        else:
            kTp = cx.dt(kT_past, [D, nown], BF16)
            vp = cx.dt(v_past, [nown, D], BF16)
            kpast_fn = lambda h: kTp[h * 128:(h + 1) * 128, :]
            vpast_fn = lambda k0, k1, h: vp[k0 * 128:k1 * 128, h * 128:(h + 1) * 128]
        att = cx.dt(attn_out, [cfg.nt, D], BF16)
        (lamin, gsub_raw, pbias), c_t = load_consts(cx, st, [("b0_lam", [128, 4, 64]), ("b0_gsub", [128, 128]), ("pbias", [128, 1])])
        tri = st.sb([128, 4, 512], BF16, "tri")
        tri_t = Trk()
        P.dma("sp", [(tri[:], cx.dt("trimask", [128, 4, 512], BF16))], P.dsem(), writes=[tri_t])
        mh = st.sb([128, 1], F32, "mh")
        mh_t = Trk()
        P.op("pool", "memset", dict(ap=mh[:], constant=-0.5), writes=[mh_t])
        prod = st.sb([128, 2, 64], F32, "prod")
        sm0 = st.sb([128, 8], F32, "sm0")
        l_t = Trk()
        for c in range(2):
            P.op("dve", "tensor_tensor", dict(out=prod[:, c, :], in0=lamin[:, 2 * c, :], in1=lamin[:, 2 * c + 1, :], op=ALU.mult),
                 reads=[c_t], writes=[l_t])
            P.op("dve", "tensor_reduce", dict(out=sm0[:, c:c + 1], in_=prod[:, c, :], axis=mybir.AxisListType.X, op=ALU.add),
                 reads=[l_t], writes=[l_t])
        P.op("act", "activation", dict(out=sm0[:, 2:4], in_=sm0[:, 0:2], func=AF.Exp), reads=[l_t], writes=[l_t])
        nlam = st.sb([128, 1], F32, "nlam")
        gsub = st.sb([128, 128], F32, "gsub")
        P.op("dve", "tensor_tensor", dict(out=nlam[:], in0=sm0[:, 3:4], in1=sm0[:, 2:3], op=ALU.subtract), reads=[l_t], writes=[l_t])
        P.op("dve", "tensor_scalar", dict(out=nlam[:], in0=nlam[:], scalar1=float(-lam_init), scalar2=None, op0=ALU.add),
             reads=[l_t], writes=[l_t])
        P.op("dve", "tensor_scalar", dict(out=gsub[:], in0=gsub_raw[:], scalar1=float(1.0 - lam_init), scalar2=None, op0=ALU.mult),
             reads=[c_t, l_t], writes=[l_t])
        kp_sb = [st.sb([128, nown], BF16, "kp") for _ in range(2)]
        ko_sb = [st.sb([128, nown], BF16, "ko") for _ in range(2)]
        vp_sb = [st.sb([128, NKP, 129], BF16, "vp") for _ in range(2)]
        vo_sb = [st.sb([128, NKP, 129], BF16, "vo") for _ in range(2)]
        kv_t = [Trk() for _ in range(2)]
        kv_ds = [P.dsem() for _ in range(2)]
        for b in range(2):
            P.op("pool", "memset", dict(ap=vp_sb[b][:, :, 128:129], constant=1.0), writes=[kv_t[b]])
            P.op("pool", "memset", dict(ap=vo_sb[b][:, :, 128:129], constant=1.0), writes=[kv_t[b]])

        def load_head(h):
            b = h % 2
            pairs = [(kp_sb[b][:], kpast_fn(h)), (ko_sb[b][:], kTo[h * 128:(h + 1) * 128, :])]
            for k0 in range(0, NKP, 8):
                k1 = min(NKP, k0 + 8)
                pairs.append((vp_sb[b][:, k0:k1, 0:128], vpast_fn(k0, k1, h).rearrange("(kt p) e -> p kt e", p=128)))
                pairs.append((vo_sb[b][:, k0:k1, 0:128],
                              vo[k0 * 128:k1 * 128, h * 128:(h + 1) * 128].rearrange("(kt p) e -> p kt e", p=128)))
            reads = [cx.trk(kT_own, (h, ti)) for ti in range(len(cfg.tiles))] + \
                    [cx.trk(v_own, (ti, s)) for ti in range(len(cfg.tiles)) for s in range(4)] + [cx.trk(kT_past, 0), cx.trk(v_past, 0)]
            P.dma("sp", pairs, kv_ds[b], reads=reads, writes=[kv_t[b]])
        qb = [st.sb([128, 512], BF16, "qb") for _ in range(2)]
        qb_t = [Trk() for _ in range(2)]
        qb_ds = [P.dsem() for _ in range(2)]
        sps = [st.ps([128, 512], F32, "sps") for _ in range(4)]
        sps_t = [Trk() for _ in range(4)]
        pT = [st.sb([128, 512], BF16, "pT") for _ in range(4)]
        pT_t = [Trk() for _ in range(4)]
        acc = [st.ps([128, 512], F32, "acc") for _ in range(4)]
        acc_t = [Trk() for _ in range(4)]
        NF = 2
        ob = [st.sb([128, 128], F32, "o") for _ in range(NF)]
        junk = [st.sb([128, 128], F32, "junk") for _ in range(NF)]
        sm = [st.sb([128, 8], F32, "sm") for _ in range(NF)]
        onb = [st.sb([128, 128], BF16, "onb") for _ in range(NF)]
        f_t = [Trk() for _ in range(NF)]
        onb_t = [Trk() for _ in range(NF)]
        onb_ds = [P.dsem() for _ in range(NF)]
        cnt = 0
        qcnt = 0
        fcnt = 0
        load_head(0)
        for h in range(8):
            if h + 1 < 8:
                load_head(h + 1)
            b = h % 2
            for ti, (t0, W) in enumerate(cfg.tiles):
                q0 = nown - HALO + t0
                NS = W // 128
                qi = qcnt % 2
                qcnt += 1
                P.dma("sp", [(qb[qi][:, 0:W], qT[h * 128:(h + 1) * 128, t0:t0 + W])], qb_ds[qi],
                      reads=[cx.trk(qT_in, (h, ti))], writes=[qb_t[qi]])
                nk = (q0 + W) // 128
                for kt in range(nk):
                    k0 = kt * 128
                    past = kt < NKP
                    ksb = kp_sb[b] if past else ko_sb[b]
                    kcol = k0 if past else k0 - nown
                    vsb = vp_sb[b] if past else vo_sb[b]
                    vkt = kt if past else kt - NKP
                    d = k0 - q0
                    for c in range(2):
                        r = cnt % 4
                        cnt += 1
                        P.op("pe", "matmul", dict(out=sps[r][:, 0:W], lhsT=ksb[c * 64:(c + 1) * 64, kcol:kcol + 128],
                                                  rhs=qb[qi][c * 64:(c + 1) * 64, 0:W], start=True, stop=True),
                             reads=[kv_t[b], qb_t[qi]], writes=[sps_t[r]])
                        P.op("act", "activation", dict(out=pT[r][:, 0:W], in_=sps[r][:, 0:W], func=AF.Exp, scale=0.125,
                                                       bias=(pbias[:, 0:1] if past else 0.0)),
                             reads=[sps_t[r], c_t], writes=[pT_t[r]])
                        if d >= 0:
                            P.op("dve" if c == 0 else "pool", "tensor_tensor",
                                 dict(out=pT[r][:, 0:W], in0=pT[r][:, 0:W], in1=tri[:, d // 128, 0:W], op=ALU.mult),
                                 reads=[pT_t[r], tri_t], writes=[pT_t[r]])
                        js = [j for j in range(NS) if not (d >= 0 and j < d // 128)]
                        for j in js:
                            lastkt = (q0 + j * 128) // 128
                            P.op("pe", "matmul", dict(out=acc[j][:, c * 129:(c + 1) * 129], lhsT=pT[r][:, j * 128:(j + 1) * 128],
                                                      rhs=vsb[:, vkt, 0:129], start=(kt == 0 and c == 0), stop=(kt == lastkt),
                                                      skip_group_check=True),
                                 reads=[pT_t[r], kv_t[b]], writes=[acc_t[j]],
                                 signal=(j == js[-1] or (kt == lastkt and c == 1)))
                    for j in range(NS):
                        if (q0 + j * 128) // 128 != kt:
                            continue
                        fi = fcnt % NF
                        fcnt += 1
                        a = acc[j]
                        o, s_, ft = ob[fi], sm[fi], f_t[fi]
                        P.op("dve", "reciprocal", dict(out=s_[:, 0:1], in_=a[:, 128:129]), reads=[acc_t[j]], writes=[ft])
                        P.op("dve", "reciprocal", dict(out=s_[:, 1:2], in_=a[:, 257:258]), reads=[acc_t[j]], writes=[ft])
                        P.op("dve", "tensor_tensor", dict(out=s_[:, 2:3], in0=s_[:, 1:2], in1=nlam[:, 0:1], op=ALU.mult),
                             reads=[ft, l_t], writes=[ft])
                        P.op("dve", "tensor_scalar", dict(out=o[:], in0=a[:, 0:128], scalar1=s_[:, 0:1], scalar2=None, op0=ALU.mult),
                             reads=[acc_t[j], ft], writes=[ft])
                        P.op("dve", "scalar_tensor_tensor", dict(out=o[:], in0=a[:, 129:257], scalar=s_[:, 2:3], in1=o[:],
                                                                 op0=ALU.mult, op1=ALU.add), reads=[acc_t[j], ft], writes=[ft])
                        P.op("act", "activation", dict(out=junk[fi][:], in_=o[:], func=AF.Square, accum_out=s_[:, 3:4]),
                             reads=[ft], writes=[ft])
                        P.op("pool", "tensor_scalar", dict(out=s_[:, 4:5], in0=s_[:, 3:4], scalar1=1.0 / 128.0, scalar2=float(RMS_EPS),
                                                           op0=ALU.mult, op1=ALU.add), reads=[ft], writes=[ft])
                        P.op("pool", "tensor_tensor", dict(out=s_[:, 4:5], in0=s_[:, 4:5], in1=mh[:], op=ALU.pow),
                             reads=[ft, mh_t], writes=[ft])
                        P.op("dve", "scalar_tensor_tensor", dict(out=onb[fi][:], in0=o[:], scalar=s_[:, 4:5], in1=gsub[:],
                                                                 op0=ALU.mult, op1=ALU.mult), reads=[ft, l_t], writes=[onb_t[fi]])
                        r0 = t0 + j * 128
                        P.dma("sp", [(att[r0:r0 + 128, h * 128:(h + 1) * 128], onb[fi][:])], onb_ds[fi], reads=[onb_t[fi]],
                              writes=[cx.trk(attn_out, (ti, h))])


def stage_g1a(cx, xT_in, gu_out):
    cfg = cx.cfg
    P = cx.P
    with Stage(cx, "g1a") as st:
        gu = cx.dt(gu_out, [GH, cfg.nt], BF16)
        (cols,), cols_t = load_consts(cx, st, [("c0_ucols", [128, GC])])
        xT, xT_t = load_xT_resident(cx, st, xT_in)
        wl = ChunkLoader(cx, st, cx.dt("c0_wu", [GC, 128, DC, 128], F32))
        ps = [st.ps([128, 512], F32, "ps") for _ in range(2)]
        ps_t = [Trk() for _ in range(2)]
        ub = [st.sb([128, 512], BF16, "ub") for _ in range(2)]
        ub_t = [Trk() for _ in range(2)]
        ub_ds = [P.dsem() for _ in range(2)]
        pending = [wl.load(0)]
        it = 0
        for f in range(GC):
            if f + 1 < GC:
                pending.append(wl.load(f + 1))
            slot = pending.pop(0)
            for ti, (t0, W) in enumerate(cfg.tiles):
                i = it % 2
                it += 1
                for k in range(DC):
                    P.op("pe", "matmul", dict(out=ps[i][:, 0:W], lhsT=wl.wb[slot][:, k, :], rhs=xT[ti][:, k, 0:W],
                                              start=(k == 0), stop=(k == DC - 1)),
                         reads=[wl.wb_t[slot], xT_t[ti]], writes=[ps_t[i]], signal=(k == DC - 1))
                P.op("act", "activation", dict(out=ub[i][:, 0:W], in_=ps[i][:, 0:W], func=AF.Gelu, bias=cols[:, f:f + 1], scale=1.0),
                     reads=[ps_t[i], cols_t], writes=[ub_t[i]])
                P.dma("sp", [(gu[f * 128:(f + 1) * 128, t0:t0 + W], ub[i][:, 0:W])], ub_ds[i], reads=[ub_t[i]],
                      writes=[cx.trk(gu_out, (f, ti))])


def stage_g1b(cx, xT_in, gu_in, go_out):
    cfg = cx.cfg
    P = cx.P
    with Stage(cx, "g1b") as st:
        xTd = cx.dt(xT_in, [D, cfg.nt], BF16)
        gu = cx.dt(gu_in, [GH, cfg.nt], BF16)
        go = cx.dt(go_out, [GH, cfg.nt], BF16)
        (bvb, gcols, wsr, trim, bsb), c_t = load_consts(cx, st, [("c0_bvb", [128, GH]), ("c0_gcols", [128, GC, 2]),
                                                                ("c0_wsT", [128, 4, 128]), ("c0_trim", [128, 128]),
                                                                ("c0_bsb", [128, 4, 128])])
        wv = st.sb([128, DC, GH], BF16, "wv")
        wv_t = Trk()
        load_w_rows(cx, st, wv, wv_t, cx.dt("c0_wv", [128, DC, GH], F32), DC, width=GH)
        mh = st.sb([128, 1], F32, "mh")
        mh_t = Trk()
        P.op("pool", "memset", dict(ap=mh[:], constant=-0.5), writes=[mh_t])
        ones = st.sb([128, 128], BF16, "ones")
        s_t = Trk()
        P.op("pool", "memset", dict(ap=ones[:], constant=1.0), writes=[s_t])
        wsT = st.sb([128, 4, 128], BF16, "wsT")
        for g in range(4):
            P.op("dve", "tensor_tensor", dict(out=wsT[:, g, :], in0=wsr[:, g, :], in1=trim[:], op=ALU.mult), reads=[c_t], writes=[s_t])
        rs = st.ps([128, 512], F32, "rs")
        rs_t = Trk()
        for g in range(4):
            P.op("pe", "matmul", dict(out=rs[:, g * 128:(g + 1) * 128], lhsT=ones[:], rhs=wsT[:, g, :], start=True, stop=True),
                 reads=[s_t], writes=[rs_t], signal=(g == 3))
        E = st.sb([128, GC, 128], F32, "E")
        E_t = Trk()
        for cc in range(GC):
            g = cc // 6
            P.op("dve", "scalar_tensor_tensor", dict(out=E[:, cc, :], in0=rs[:, g * 128:(g + 1) * 128], scalar=gcols[:, cc, 1:2],
                                                     in1=bsb[:, g, :], op0=ALU.mult, op1=ALU.add), reads=[rs_t, c_t], writes=[E_t])
        xb = [st.sb([128, DC, 512], BF16, "xTt") for _ in range(2)]
        xb_t = [Trk() for _ in range(2)]
        xb_ds = [P.dsem() for _ in range(2)]
        vps = [st.ps([128, 512], F32, "vps") for _ in range(2)]
        vps_t = [Trk() for _ in range(2)]
        svps = [st.ps([128, 512], F32, "svps") for _ in range(2)]
        svps_t = [Trk() for _ in range(2)]
        vbuf = st.sb([128, GH], F32, "vbuf")
        vbuf_t = [Trk() for _ in range(6)]
        stt = st.sb([128, 48], F32, "stt")
        stt_t = Trk()
        vhat = [st.sb([128, GH], BF16, "vhat") for _ in range(2)]
        vhat_t = [Trk() for _ in range(2)]
        uTn = [st.sb([128, GC, 128], BF16, "uTn") for _ in range(2)]
        uTn_t = [Trk() for _ in range(2)]
        uTn_ds = [P.dsem() for _ in range(2)]
        tmp = st.sb([128, GC, 128], F32, "tmp")
        tmp_t = [Trk() for _ in range(6)]
        oTb = st.sb([128, GC, 512], BF16, "oTb")
        oTb_t = Trk()
        oTb_ds = P.dsem()

        def load_x(ti):
            t0, W = cfg.tiles[ti]
            P.dma("sp", [(xb[ti % 2][:, :, 0:W], xTd[:, t0:t0 + W].rearrange("(k p) t -> p k t", p=128))], xb_ds[ti % 2],
                  reads=[cx.trk(xT_in, ti)], writes=[xb_t[ti % 2]])
        load_x(0)
        cnt = 0
        vcnt = 0
        scnt = 0
        for ti, (t0, W) in enumerate(cfg.tiles):
            if ti + 1 < len(cfg.tiles):
                load_x(ti + 1)
            xt = xb[ti % 2]
            for s in range(W // 128):
                i = cnt % 2
                cnt += 1
                c0 = t0 + s * 128
                P.dma("sp", [(uTn[i][:], gu[:, c0:c0 + 128].rearrange("(k p) t -> p k t", p=128))], uTn_ds[i],
                      reads=[cx.trk(gu_in, (f, ti)) for f in range(GC)], writes=[uTn_t[i]])
                for fc in range(6):
                    vi = vcnt % 2
                    vcnt += 1
                    sl = slice(fc * 512, (fc + 1) * 512)
                    for k in range(DC):
                        P.op("pe", "matmul", dict(out=vps[vi][:], lhsT=xt[:, k, s * 128:(s + 1) * 128], rhs=wv[:, k, sl],
                                                  start=(k == 0), stop=(k == DC - 1)),
                             reads=[xb_t[ti % 2], wv_t], writes=[vps_t[vi]], signal=(k == DC - 1))
                    P.op("dve", "tensor_tensor", dict(out=vbuf[:, sl], in0=vps[vi][:], in1=bvb[:, sl], op=ALU.add),
                         reads=[vps_t[vi], c_t], writes=[vbuf_t[fc]])
                    P.op("act", "activation", dict(out=vbuf[:, sl], in_=vbuf[:, sl], func=AF.Gelu), reads=[vbuf_t[fc]], writes=[vbuf_t[fc]])
                    P.op("dve", "bn_stats", dict(out=stt[:, fc * 6:(fc + 1) * 6], in_=vbuf[:, sl]), reads=[vbuf_t[fc]], writes=[stt_t])
                P.op("dve", "bn_aggr", dict(out=stt[:, 36:38], in_=stt[:, 0:36]), reads=[stt_t], writes=[stt_t])
                P.op("pool", "tensor_scalar", dict(out=stt[:, 38:39], in0=stt[:, 37:38], scalar1=float(LN_EPS), scalar2=None, op0=ALU.add),
                     reads=[stt_t], writes=[stt_t])
                P.op("pool", "tensor_tensor", dict(out=stt[:, 38:39], in0=stt[:, 38:39], in1=mh[:], op=ALU.pow),
                     reads=[stt_t, mh_t], writes=[stt_t])
                vh = vhat[i]
                for hh in range(2):
                    sl = slice(hh * 1536, (hh + 1) * 1536)
                    P.op("dve", "tensor_scalar", dict(out=vh[:, sl], in0=vbuf[:, sl], scalar1=stt[:, 36:37], scalar2=stt[:, 38:39],
                                                      op0=ALU.subtract, op1=ALU.mult),
                         reads=[stt_t] + vbuf_t[3 * hh:3 * hh + 3], writes=[vhat_t[i]])
                for cg in range(6):
                    si = scnt % 2
                    scnt += 1
                    for q in range(4):
                        cc = cg * 4 + q
                        g = cc // 6
                        P.op("pe", "matmul", dict(out=svps[si][:, q * 128:(q + 1) * 128], lhsT=vh[:, cc * 128:(cc + 1) * 128],
                                                  rhs=wsT[:, g, :], start=True, stop=True),
                             reads=[vhat_t[i], s_t], writes=[svps_t[si]], signal=(q == 3))
                    for q in range(4):
                        cc = cg * 4 + q
                        P.op("dve", "scalar_tensor_tensor", dict(out=tmp[:, cc, :], in0=svps[si][:, q * 128:(q + 1) * 128],
                                                                 scalar=gcols[:, cc, 0:1], in1=E[:, cc, :], op0=ALU.mult, op1=ALU.add),
                             reads=[svps_t[si], c_t, E_t], writes=[tmp_t[cg]])
                    P.op("pool", "tensor_tensor", dict(out=oTb[:, cg * 4:cg * 4 + 4, s * 128:(s + 1) * 128], in0=tmp[:, cg * 4:cg * 4 + 4, :],
                                                       in1=uTn[i][:, cg * 4:cg * 4 + 4, :], op=ALU.mult),
                         reads=[tmp_t[cg], uTn_t[i]], writes=[oTb_t])
            P.dma("sp", [(go[:, t0:t0 + W].rearrange("(k p) t -> p k t", p=128), oTb[:, :, 0:W])], oTb_ds, reads=[oTb_t],
                  writes=[cx.trk(go_out, (k, ti)) for k in range(GC)])


def ffn_stages(li, xT_in, xres_in, xres_out, xT_out, final_out=None):
    return [
        lambda cx: stage_f1(cx, li, xT_in, "uT"),
        lambda cx: stage_proj(cx, "f2_%d" % li, "f%d" % li, li, "uT", FC, "f%d_wdn" % li, xres_in, xres_out, xT_out,
                              final_out=final_out),
    ]


def stages_part1():
    st = [lambda cx: stage_prep(cx, "x_in", "xT_a")]
    st += [lambda cx: stage_c1(cx, 0, "xT_a", "cv"),
           lambda cx: stage_tm_proj(cx, "c2_0", "m0", "cv", F32, True, "a0_lcols", "a0_wout", "x_in", "xr_a", "xT_b")]
    st += ffn_stages(0, "xT_b", "xr_a", "xr_b", "xT_a")
    st += [lambda cx: stage_a1(cx, "xT_a", "qT", "kT_own", "v_own")]
    return st


def stages_part2():
    st = [lambda cx: stage_a2(cx, 1, "qT", "kT_own", "kT_past", "v_own", "v_past", "att"),
          lambda cx: stage_tm_proj(cx, "a3", "m1", "att", BF16, False, None, "b0_wo", "xr_b", "xr_a", "xT_b")]
    st += ffn_stages(1, "xT_b", "xr_a", "xr_b", "xT_a")
    st += [lambda cx: stage_g1a(cx, "xT_a", "gu"),
           lambda cx: stage_g1b(cx, "xT_a", "gu", "go"),
           lambda cx: stage_proj(cx, "g2", "m2", 2, "go", GC, "c0_wo", "xr_b", "xr_a", "xT_b")]
    st += ffn_stages(2, "xT_b", "xr_a", "xr_b", "xT_a")
    st += [lambda cx: stage_c1(cx, 1, "xT_a", "cv"),
           lambda cx: stage_tm_proj(cx, "c2_1", "m3", "cv", F32, True, "a1_lcols", "a1_wout", "xr_b", "xr_a", "xT_b")]
    st += ffn_stages(3, "xT_b", "xr_a", None, None, final_out="out")
    return st


def stage_xchg(cx):
    cfg = cx.cfg
    P = cx.P
    nown = cfg.nown
    kT = cx.dt("kT_own", [D, nown], BF16)
    v = cx.dt("v_own", [nown, D], BF16)
    groups = [[0, 1], [2, 3], [4, 5], [6, 7]]
    KR = 256
    VR = min(1024, nown)
    P.barrier()
    t = Trk()
    kall, vall = [], []
    for p in range(D // KR):
        dst = cx.nc.dram_tensor("kT_all%d" % p, [2 * KR, nown], BF16).ap()
        P.op("pool", "collective_compute", dict(kind="AllGather", op=ALU.bypass, replica_groups=groups,
                                                ins=[kT[p * KR:(p + 1) * KR, :].opt()], outs=[dst.opt()]), writes=[t])
        kall.append(dst)
    for p in range(nown // VR):
        dst = cx.nc.dram_tensor("v_all%d" % p, [2 * VR, D], BF16).ap()
        P.op("pool", "collective_compute", dict(kind="AllGather", op=ALU.bypass, replica_groups=groups,
                                                ins=[v[p * VR:(p + 1) * VR, :].opt()], outs=[dst.opt()]), writes=[t])
        vall.append(dst)

    def kpast_fn(h):
        r0 = (h % 2) * 128
        return kall[h // 2][r0:r0 + 128, :]

    def vpast_fn(k0, k1, h):
        p = (k0 * 128) // VR
        assert (k1 * 128 - 1) // VR == p
        r0 = k0 * 128 - p * VR
        return vall[p][r0:r0 + (k1 - k0) * 128, h * 128:(h + 1) * 128]
    cx.views["kpast_fn"] = kpast_fn
    cx.views["vpast_fn"] = vpast_fn
    P.barrier()


SCRATCH = {"xT_a", "xT_b", "xr_a", "xr_b", "cv", "uT", "qT", "kT_own", "v_own", "kT_past", "v_past", "att", "gu", "go",
           "kT_all", "v_all"}
FUSED = True
_PROG_CACHE = {}


def _get_prog(key, cfg, stages, ext_out, internal):
    if key not in _PROG_CACHE:
        _PROG_CACHE[key] = build_program(cfg, stages, None, ext_out, internal=internal, want_names=True)
    return _PROG_CACHE[key]


def kernel(**inputs):
    inputs = {k: np.asarray(v) for k, v in inputs.items()}
    x = inputs["x"].astype(np.float32, copy=False)
    B, S, _ = x.shape
    ncore = 8
    nown = S // 2
    cfg = Cfg(nown)
    consts = host_consts(inputs)
    per_core = []
    for c in range(ncore):
        b, half = c // 2, c % 2
        if half == 0:
            xin = np.concatenate([np.zeros((HALO, D), np.float32), x[b, 0:nown]], axis=0)
        else:
            xin = x[b, nown - HALO:2 * nown]
        per_core.append({
            "x_in": np.ascontiguousarray(xin),
            "hm": np.full((128, 1), float(half), np.float32),
            "pbias": np.full((128, 1), 0.0 if half == 1 else -80.0, np.float32),
        })
    if FUSED:
        nc, _, names = _get_prog(("fused", nown), cfg, stages_part1() + [stage_xchg] + stages_part2(), {"out"}, SCRATCH)
        maps = []
        for c in range(ncore):
            maps.append({k: (per_core[c][k] if k in per_core[c] else consts[k]) for k in names})
        res = run_bass_kernel_spmd(nc, maps, core_ids=list(range(ncore))).results
        out = np.empty((B, S, D), np.float32)
        for c in range(ncore):
            b, half = c // 2, c % 2
            out[b, half * nown:(half + 1) * nown] = res[c]["out"]
        return out
    out1 = {"xr_b", "qT", "kT_own", "v_own"}
    nc1, _, names1 = _get_prog(("p1", nown), cfg, stages_part1(), out1, SCRATCH - out1)
    maps1 = []
    for c in range(ncore):
        m = {}
        for k in names1:
            m[k] = per_core[c][k] if k in per_core[c] else consts[k]
        maps1.append(m)
    res1 = run_bass_kernel_spmd(nc1, maps1, core_ids=list(range(ncore))).results
    out2 = {"out"}
    in2 = {"xr_b", "qT", "kT_own", "v_own", "kT_past", "v_past"}
    nc2, _, names2 = _get_prog(("p2", nown), cfg, stages_part2(), out2, SCRATCH - in2)
    maps2 = []
    for c in range(ncore):
        m = {}
        src = res1[c - (c % 2)]
        for k in names2:
            if k == "kT_past":
                m[k] = src["kT_own"]
            elif k == "v_past":
                m[k] = src["v_own"]
            elif k in ("xr_b", "qT", "kT_own", "v_own"):
                m[k] = res1[c][k]
            elif k in per_core[c]:
                m[k] = per_core[c][k]
            else:
                m[k] = consts[k]
        maps2.append(m)
    res2 = run_bass_kernel_spmd(nc2, maps2, core_ids=list(range(ncore))).results
    out = np.empty((B, S, D), np.float32)
    for c in range(ncore):
        b, half = c // 2, c % 2
        out[b, half * nown:(half + 1) * nown] = res2[c]["out"]
    return out
```

```python
import contextlib
import math
import numpy as np
import ml_dtypes
import concourse.bass as bass
import concourse.mybir as mybir
from concourse.bass_utils import run_bass_kernel_spmd

F32 = mybir.dt.float32
BF16 = mybir.dt.bfloat16
BF16_NP = ml_dtypes.bfloat16
AF = mybir.ActivationFunctionType
ALU = mybir.AluOpType

D = 1024
DC = 8
DEPTH = 4
HALO = 256
NOWN_FULL = 4096
FF = 2816
FC = 22
CONVK = 31
GH = 3072
GC = 24
ALPHA = (2 * DEPTH) ** 0.25
LN_EPS = 1e-5
RMS_EPS = 1e-5
SEM_LIMIT = 60000
SELF_SYNC = True


class Tok:
    __slots__ = ("sem", "val")

    def __init__(self, sem, val):
        self.sem = sem
        self.val = val


class Trk:
    __slots__ = ("w", "r")

    def __init__(self):
        self.w = []
        self.r = []


class DSem:
    __slots__ = ("sem", "cnt")

    def __init__(self, sem):
        self.sem = sem
        self.cnt = 0


class Prog:
    def __init__(self, nc, es):
        self.nc = nc
        self.es = es
        self.engs = {"pe": nc.tensor, "act": nc.scalar, "dve": nc.vector, "pool": nc.gpsimd, "sp": nc.sync}
        self.q = {k: [] for k in self.engs}
        self.pool_sems = []
        n = 0
        while n < 96:
            try:
                self.pool_sems.append(es.enter_context(nc.semaphore("s%d" % n)))
            except Exception:
                break
            n += 1
        self.esem = {k: self.pool_sems.pop() for k in self.engs}
        self.dpool = [DSem(x) for x in self.pool_sems[8:]]
        self.pool_sems = self.pool_sems[:8]
        self.stage_ds = []
        self.ecnt = {k: 0 for k in self.engs}
        self.unsig = {k: False for k in self.engs}
        self.seen = {k: {} for k in self.engs}
        self.stage_dma = {}
        self.dsems = []
        self.ninst = 0

    def dsem(self):
        d = self.dpool.pop()
        self.stage_ds.append(d)
        return d

    def release_dsems(self):
        self.dpool.extend(self.stage_ds)
        self.stage_ds = []

    def _collect(self, eng, reads, writes, extra):
        need = {}

        def add(t):
            if t is None:
                return
            k = id(t.sem)
            if k not in need or need[k].val < t.val:
                need[k] = t
        for b in reads:
            for t in b.w:
                add(t)
        for b in writes:
            for t in b.w:
                add(t)
            for t in b.r:
                add(t)
        for t in extra:
            add(t)
        waits = []
        seen = self.seen[eng]
        own = self.esem[eng]
        for k, t in need.items():
            if t.sem is own and (eng == "pe" or not SELF_SYNC):
                continue
            if seen.get(k, 0) >= t.val:
                continue
            seen[k] = t.val
            waits.append(t)
        return waits

    @staticmethod
    def _commit(tok, reads, writes):
        for b in reads:
            b.r.append(tok)
            if len(b.r) > 64:
                best = {}
                for t in b.r:
                    k = id(t.sem)
                    if k not in best or best[k].val < t.val:
                        best[k] = t
                b.r = list(best.values())
        for b in writes:
            b.w = [tok]
            b.r = []

    def op(self, eng, name, kw, reads=(), writes=(), extra=(), signal=True):
        waits = self._collect(eng, reads, writes, extra)
        sem = self.esem[eng]
        if SELF_SYNC and eng != "pe":
            signal = True
        self.unsig[eng] = not signal
        if signal:
            self.ecnt[eng] += 1
            tok = Tok(sem, self.ecnt[eng])
        else:
            tok = Tok(sem, self.ecnt[eng] + 1)

        def emit(e):
            for t in waits:
                e.wait_ge(t.sem, t.val)
            ins = getattr(e, name)(**kw)
            if signal:
                ins.then_inc(sem, 1)
        self.q[eng].append(emit)
        self.ninst += 1
        self._commit(tok, reads, writes)
        if signal and self.ecnt[eng] >= SEM_LIMIT:
            self.esem[eng] = self.pool_sems.pop()
            self.ecnt[eng] = 0
        return tok

    def dma(self, eng, pairs, ds, reads=(), writes=(), extra=()):
        if ds.cnt + 16 * len(pairs) >= SEM_LIMIT:
            ds.sem = self.pool_sems.pop()
            ds.cnt = 0
        waits = self._collect(eng, reads, writes, extra)
        ds.cnt += 16 * len(pairs)
        sem = ds.sem
        tok = Tok(sem, ds.cnt)

        def emit(e):
            for t in waits:
                e.wait_ge(t.sem, t.val)
            for (o, i) in pairs:
                e.dma_start(out=o, in_=i).then_inc(sem, 16)
        self.q[eng].append(emit)
        self.ninst += len(pairs)
        self._commit(tok, reads, writes)
        self.stage_dma[id(sem)] = tok
        return tok

    def barrier(self):
        toks = []
        for k in self.engs:
            assert not self.unsig[k], k
            if self.ecnt[k] > 0:
                toks.append(Tok(self.esem[k], self.ecnt[k]))
        toks += list(self.stage_dma.values())
        self.stage_dma = {}
        for k in self.engs:
            waits = self._collect(k, (), (), toks)
            if waits:
                def emit(e, waits=waits):
                    for t in waits:
                        e.wait_ge(t.sem, t.val)
                self.q[k].append(emit)

    def finish(self):
        self.barrier()
        block = self.es.enter_context(self.nc.Block())
        q = self.q

        @block.tensor
        def _(e):
            for f in q["pe"]:
                f(e)

        @block.scalar
        def _(e):
            for f in q["act"]:
                f(e)

        @block.vector
        def _(e):
            for f in q["dve"]:
                f(e)

        @block.gpsimd
        def _(e):
            for f in q["pool"]:
                f(e)

        @block.sync
        def _(e):
            for f in q["sp"]:
                f(e)


class Cfg:
    def __init__(self, nown):
        self.nown = nown
        self.nt = HALO + nown
        self.tiles = [(0, HALO)] + [(HALO + 512 * i, 512) for i in range(nown // 512)]


class Ctx:
    def __init__(self, nc, es, cfg, ext_in, ext_out):
        self.nc = nc
        self.es = es
        self.cfg = cfg
        self.P = Prog(nc, es)
        self.ext_in = ext_in
        self.ext_out = ext_out
        self.dram = {}
        self.dtrk = {}
        self.kinds = {}
        self.internal = set()
        self.views = {}

    def dt(self, name, shape=None, dtype=None):
        if name in self.views:
            return self.views[name]
        if name not in self.dram:
            kind = "Internal"
            if name in self.ext_out:
                kind = "ExternalOutput"
            elif self.ext_in is None:
                if name not in self.internal:
                    kind = "ExternalInput"
            elif name in self.ext_in:
                kind = "ExternalInput"
            self.kinds[name] = kind
            self.dram[name] = self.nc.dram_tensor(name, list(shape), dtype, kind=kind).ap()
        return self.dram[name]

    def trk(self, name, idx):
        key = (name, idx)
        if key not in self.dtrk:
            self.dtrk[key] = Trk()
        return self.dtrk[key]


class Stage:
    def __init__(self, cx, name):
        self.cx = cx
        self.name = name
        self.es = contextlib.ExitStack()
        self.n = 0

    def __enter__(self):
        self.es.__enter__()
        return self

    def __exit__(self, *a):
        self.cx.P.barrier()
        self.cx.P.release_dsems()
        return self.es.__exit__(*a)

    def sb(self, shape, dtype, tag="t"):
        self.n += 1
        return self.es.enter_context(self.cx.nc.sbuf_tensor("%s_%s%d" % (self.name, tag, self.n), list(shape), dtype))

    def ps(self, shape, dtype, tag="p"):
        self.n += 1
        return self.es.enter_context(self.cx.nc.psum_tensor("%s_%s%d" % (self.name, tag, self.n), list(shape), dtype))


def load_consts(cx, st, specs, eng="sp"):
    P = cx.P
    ds = P.dsem()
    trk = Trk()
    ts = []
    pairs = []
    for (name, shape) in specs:
        src = cx.dt(name, shape, F32)
        t = st.sb(shape, F32, tag=name)
        ts.append(t)
        pairs.append((t[:], src))
    P.dma(eng, pairs, ds, writes=[trk])
    return ts, trk


class Epi:
    def __init__(self, cx, st, prefix, li, xres_in, xres_out, xT_out, final_out=None):
        P = cx.P
        self.cx, self.st = cx, st
        cfg = cx.cfg
        self.xres_in = cx.dt(xres_in, [cfg.nt, D], F32)
        self.xres_in_name = xres_in
        self.final_out = final_out
        if final_out is None:
            self.xres_out = cx.dt(xres_out, [cfg.nt, D], F32)
            self.xT_out = cx.dt(xT_out, [D, cfg.nt], BF16)
        else:
            self.xres_out = cx.dt(final_out, [cfg.nown, D], F32)
            self.xT_out = None
        self.xres_out_name = xres_out if final_out is None else final_out
        self.xT_out_name = xT_out
        (self.bb, self.gam, self.bet, self.ident), ct = load_consts(
            cx, st, [(prefix + "_bb", [128, D]), (prefix + "_g", [128, D]), (prefix + "_b", [128, D]),
                     ("ident_f32", [128, 128])])
        self.bb_t = self.gam_t = self.bet_t = self.ident_t = ct
        self.mh = st.sb([128, 1], F32, "mh")
        self.mh_t = Trk()
        P.op("pool", "memset", dict(ap=self.mh[:], constant=-0.5), writes=[self.mh_t])
        NR = 2
        self.xold = [st.sb([128, 4, D], F32, "xold") for _ in range(NR)]
        self.xold_t = [Trk() for _ in range(NR)]
        self.xold_ds = [P.dsem() for _ in range(NR)]
        self.s_ = [st.sb([128, D], F32, "s") for _ in range(2)]
        self.s_t = [Trk() for _ in range(2)]
        self.xn = [st.sb([128, D], F32, "xn") for _ in range(2)]
        self.xn_t = [Trk() for _ in range(2)]
        self.xnew = [st.sb([128, D], F32, "xnew") for _ in range(2)]
        self.xnew_t = [Trk() for _ in range(2)]
        self.xnew_ds = [P.dsem() for _ in range(2)]
        self.stat = [st.sb([128, 16], F32, "stat") for _ in range(2)]
        self.stat_t = [Trk() for _ in range(2)]
        self.xTo = [st.sb([128, DC, 512], BF16, "xTo") for _ in range(2)]
        self.xTo_t = [Trk() for _ in range(2)]
        self.xTo_ds = [P.dsem() for _ in range(2)]
        self.tp = [st.ps([128, D], F32, "tp") for _ in range(1)]
        self.tp_t = [Trk() for _ in range(1)]
        self.cnt = 0
        self.tile_cnt = 0

    def load_xold(self, ti):
        P = self.cx.P
        t0, W = self.cx.cfg.tiles[ti]
        slot = ti % len(self.xold)
        src = self.xres_in[t0:t0 + W, :].rearrange("(s p) d -> p s d", p=128)
        P.dma("sp", [(self.xold[slot][:, 0:W // 128, :], src)], self.xold_ds[slot],
              reads=[self.cx.trk(self.xres_in_name, (ti, s)) for s in range(W // 128)], writes=[self.xold_t[slot]])

    def sub_a(self, ti, s, y_ps, y_t):
        cx, P = self.cx, self.cx.P
        t0, W = cx.cfg.tiles[ti]
        slot = ti % len(self.xold)
        i = self.cnt % 2
        self.cnt += 1
        xold = self.xold[slot][:, s, :]
        if y_ps is not None:
            xnew = self.xnew[i]
            s_ = self.s_[i]
            P.op("dve", "scalar_tensor_tensor", dict(out=s_[:], in0=xold, scalar=float(ALPHA), in1=self.bb[:],
                                                         op0=ALU.mult, op1=ALU.add),
                 reads=[self.xold_t[slot], self.bb_t], writes=[self.s_t[i]])
            for h in range(2):
                P.op("dve", "tensor_tensor", dict(out=s_[:, h * 512:(h + 1) * 512], in0=y_ps[:, h * 512:(h + 1) * 512],
                                                          in1=s_[:, h * 512:(h + 1) * 512], op=ALU.add),
                     reads=[y_t], writes=[self.s_t[i]], signal=(h == 1))
            stt = self.stat[i]
            for h in range(2):
                P.op("dve", "bn_stats", dict(out=stt[:, h * 6:(h + 1) * 6], in_=s_[:, h * 512:(h + 1) * 512]),
                     reads=[self.s_t[i]], writes=[self.stat_t[i]], signal=False)
            P.op("dve", "bn_aggr", dict(out=stt[:, 12:14], in_=stt[:, 0:12]), reads=[self.stat_t[i]], writes=[self.stat_t[i]])
            P.op("pool", "tensor_scalar", dict(out=stt[:, 14:15], in0=stt[:, 13:14], scalar1=float(LN_EPS), scalar2=None,
                                                   op0=ALU.add), reads=[self.stat_t[i]], writes=[self.stat_t[i]], signal=False)
            P.op("pool", "tensor_tensor", dict(out=stt[:, 14:15], in0=stt[:, 14:15], in1=self.mh[:], op=ALU.pow),
                 reads=[self.stat_t[i], self.mh_t], writes=[self.stat_t[i]])
            xn = self.xn[i]
            P.op("dve", "tensor_scalar", dict(out=xn[:], in0=s_[:], scalar1=stt[:, 12:13], scalar2=stt[:, 14:15],
                                                  op0=ALU.subtract, op1=ALU.mult),
                 reads=[self.s_t[i], self.stat_t[i]], writes=[self.xn_t[i]])
            P.op("pool", "tensor_tensor", dict(out=xn[:], in0=xn[:], in1=self.gam[:], op=ALU.mult),
                 reads=[self.xn_t[i], self.gam_t], writes=[self.xn_t[i]])
            P.op("pool", "tensor_tensor", dict(out=xnew[:], in0=xn[:], in1=self.bet[:], op=ALU.add),
                 reads=[self.xn_t[i], self.bet_t], writes=[self.xnew_t[i]])
            src_for_T = xnew
            src_t = self.xnew_t[i]
            if self.final_out is None:
                dst = self.xres_out[t0 + s * 128:t0 + (s + 1) * 128, :]
                P.dma("sp", [(dst, xnew[:])], self.xnew_ds[i], reads=[self.xnew_t[i]],
                      writes=[cx.trk(self.xres_out_name, (ti, s))])
            else:
                if t0 >= HALO:
                    o0 = t0 - HALO + s * 128
                    dst = self.xres_out[o0:o0 + 128, :]
                    P.dma("sp", [(dst, xnew[:])], self.xnew_ds[i], reads=[self.xnew_t[i]],
                          writes=[cx.trk(self.xres_out_name, (ti, s))])
            src_ap = xnew
        else:
            src_ap = None
            src_t = self.xold_t[slot]
        return (ti, s, src_ap, src_t, slot)

    def sub_b(self, state):
        cx, P = self.cx, self.cx.P
        ti, s, src_ap, src_t, slot = state
        t0, W = cx.cfg.tiles[ti]
        if self.xT_out is None:
            return
        j = self.tile_cnt % 2
        tp = self.tp[0]
        for k in range(DC):
            if src_ap is not None:
                inp = src_ap[:, k * 128:(k + 1) * 128]
            else:
                inp = self.xold[slot][:, s, k * 128:(k + 1) * 128]
            P.op("pe", "transpose", dict(out=tp[:, k * 128:(k + 1) * 128], in_=inp, identity=self.ident[:]),
                 reads=[src_t, self.ident_t], writes=[self.tp_t[0]], signal=(k == DC - 1))
        xTo = self.xTo[j]
        P.op("act", "copy", dict(out=xTo[:, :, s * 128:(s + 1) * 128], in_=tp[:].rearrange("p (k t) -> p k t", k=DC)),
             reads=[self.tp_t[0]], writes=[self.xTo_t[j]])
        if s == W // 128 - 1:
            dst = self.xT_out[:, t0:t0 + W].rearrange("(k p) t -> p k t", p=128)
            P.dma("sp", [(dst, xTo[:, :, 0:W])], self.xTo_ds[j], reads=[self.xTo_t[j]],
                  writes=[cx.trk(self.xT_out_name, ti)])
            self.tile_cnt += 1


    def sub(self, ti, s, y_ps, y_t):
        self.sub_b(self.sub_a(ti, s, y_ps, y_t))


def stage_prep(cx, x_in, xT_out):
    cfg = cx.cfg
    with Stage(cx, "prep") as st:
        ep = Epi.__new__(Epi)
        P = cx.P
        ep.cx, ep.st = cx, st
        ep.xres_in = cx.dt(x_in, [cfg.nt, D], F32)
        ep.xres_in_name = x_in
        ep.final_out = None
        ep.xT_out = cx.dt(xT_out, [D, cfg.nt], BF16)
        ep.xT_out_name = xT_out
        (ep.ident,), ep.ident_t = load_consts(cx, st, [("ident_f32", [128, 128])])
        ep.xold = [st.sb([128, 4, D], F32, "xold") for _ in range(2)]
        ep.xold_t = [Trk() for _ in range(2)]
        ep.xold_ds = [P.dsem() for _ in range(2)]
        ep.xTo = [st.sb([128, DC, 512], BF16, "xTo") for _ in range(2)]
        ep.xTo_t = [Trk() for _ in range(2)]
        ep.xTo_ds = [P.dsem() for _ in range(2)]
        ep.tp = [st.ps([128, D], F32, "tp")]
        ep.tp_t = [Trk()]
        ep.cnt = 0
        ep.tile_cnt = 0
        for ti, (t0, W) in enumerate(cfg.tiles):
            ep.load_xold(ti)
            for s in range(W // 128):
                ep.sub(ti, s, None, None)


def load_xT_resident(cx, st, xT_in):
    cfg = cx.cfg
    P = cx.P
    src = cx.dt(xT_in, [D, cfg.nt], BF16)
    xT = []
    trks = []
    for ti, (t0, W) in enumerate(cfg.tiles):
        ds = P.dsem()
        trk = Trk()
        xt = st.sb([128, DC, W], BF16, "xTres")
        P.dma("sp", [(xt[:], src[:, t0:t0 + W].rearrange("(k p) t -> p k t", p=128))], ds,
              reads=[cx.trk(xT_in, ti)], writes=[trk])
        trks.append(trk)
        xT.append(xt)
    return xT, trks


def stage_f1(cx, li, xT_in, uT_out):
    cfg = cx.cfg
    P = cx.P
    with Stage(cx, "f1_%d" % li) as st:
        uT = cx.dt(uT_out, [FF, cfg.nt], BF16)
        (cols, hm), cols_t = load_consts(cx, st, [("f%d_cols" % li, [128, 2 * FC, 5]), ("hm", [128, 1])])
        hm_t = cols_t
        xT, xT_t = load_xT_resident(cx, st, xT_in)
        wsrc = cx.dt("f%d_wup" % li, [2 * FC, 128, DC, 128], F32)
        NW = 4
        wb = [st.sb([128, DC, 128], BF16, "w") for _ in range(NW)]
        wb_t = [Trk() for _ in range(NW)]
        wst = [st.sb([128, DC, 128], F32, "wst") for _ in range(NW)]
        wst_t = [Trk() for _ in range(NW)]
        wb_ds = [P.dsem() for _ in range(NW)]
        gps = [st.ps([128, 512], F32, "g") for _ in range(2)]
        gps_t = [Trk() for _ in range(2)]
        vps = [st.ps([128, 512], F32, "v") for _ in range(2)]
        vps_t = [Trk() for _ in range(2)]
        hg = [st.sb([128, 514], F32, "hg") for _ in range(2)]
        hg_t = [Trk() for _ in range(2)]
        hv = [st.sb([128, 514], F32, "hv") for _ in range(2)]
        hv_t = [Trk() for _ in range(2)]
        cg = [st.sb([128, 512], F32, "cg") for _ in range(2)]
        cg_t = [Trk() for _ in range(2)]
        cv = [st.sb([128, 512], F32, "cv") for _ in range(2)]
        cv_t = [Trk() for _ in range(2)]
        sg = [st.sb([128, 512], F32, "sg") for _ in range(2)]
        sg_t = [Trk() for _ in range(2)]
        ub = [st.sb([128, 512], BF16, "u") for _ in range(2)]
        ub_t = [Trk() for _ in range(2)]
        ub_ds = [P.dsem() for _ in range(2)]
        wcnt = 0

        def load_w(j):
            nonlocal wcnt
            slot = wcnt % NW
            wcnt += 1
            P.dma("sp", [(wst[slot][:], wsrc[j])], wb_ds[slot], writes=[wst_t[slot]])
            P.op("pool", "tensor_copy", dict(out=wb[slot][:], in_=wst[slot][:]), reads=[wst_t[slot]], writes=[wb_t[slot]])
            return slot

        pending = [(load_w(0), load_w(FC))]
        it = 0
        for f in range(FC):
            if f + 1 < FC:
                pending.append((load_w(f + 1), load_w(FC + f + 1)))
            sg_slot, sv_slot = pending.pop(0)
            for ti, (t0, W) in enumerate(cfg.tiles):
                i = it % 2
                it += 1
                for (slot, pst, pstt, jj) in ((sg_slot, gps[i], gps_t[i], f), (sv_slot, vps[i], vps_t[i], FC + f)):
                    for k in range(DC):
                        P.op("pe", "matmul", dict(
                            out=pst[:, 0:W], lhsT=wb[slot][:, k, :], rhs=xT[ti][:, k, 0:W], start=(k == 0), stop=(k == DC - 1)),
                            reads=[wb_t[slot], xT_t[ti]], writes=[pstt], signal=(k == DC - 1))
                for (hb, hbt, pst, pstt, jj) in ((hg, hg_t, gps[i], gps_t[i], f), (hv, hv_t, vps[i], vps_t[i], FC + f)):
                    h = hb[i]
                    bcol = cols[:, jj, 0:1]
                    if ti == 0:
                        P.op("pool", "memset", dict(ap=h[:, 0:2], constant=0.0), writes=[hbt[i]])
                        P.op("dve", "tensor_scalar", dict(
                            out=h[:, 2:2 + W], in0=pst[:, 0:W], scalar1=bcol, scalar2=hm[:, 0:1], op0=ALU.add, op1=ALU.mult),
                            reads=[pstt, cols_t, hm_t], writes=[hbt[i]])
                    else:
                        hp = hb[1 - i]
                        Wp = cfg.tiles[ti - 1][1]
                        P.op("pool", "tensor_copy", dict(out=h[:, 0:2], in_=hp[:, Wp:Wp + 2]),
                             reads=[hbt[1 - i]], writes=[hbt[i]])
                        P.op("act", "activation", dict(
                            out=h[:, 2:2 + W], in_=pst[:, 0:W], func=AF.Identity, bias=bcol, scale=1.0),
                            reads=[pstt, cols_t], writes=[hbt[i]])
                for (hb, hbt, cb, cbt, jj) in ((hg, hg_t, cg, cg_t, f), (hv, hv_t, cv, cv_t, FC + f)):
                    h = hb[i]
                    c = cb[i]
                    P.op("act", "activation", dict(
                        out=c[:, 0:W], in_=h[:, 2:2 + W], func=AF.Identity, bias=cols[:, jj, 4:5], scale=cols[:, jj, 3:4]),
                        reads=[hbt[i], cols_t], writes=[cbt[i]])
                    P.op("dve", "scalar_tensor_tensor", dict(
                        out=c[:, 0:W], in0=h[:, 1:1 + W], scalar=cols[:, jj, 2:3], in1=c[:, 0:W], op0=ALU.mult, op1=ALU.add),
                        reads=[hbt[i], cbt[i]], writes=[cbt[i]], signal=False)
                    P.op("dve", "scalar_tensor_tensor", dict(
                        out=c[:, 0:W], in0=h[:, 0:W], scalar=cols[:, jj, 1:2], in1=c[:, 0:W], op0=ALU.mult, op1=ALU.add),
                        reads=[hbt[i], cbt[i]], writes=[cbt[i]])
                P.op("act", "activation", dict(out=sg[i][:, 0:W], in_=cg[i][:, 0:W], func=AF.Silu),
                     reads=[cg_t[i]], writes=[sg_t[i]])
                P.op("dve", "tensor_tensor", dict(out=ub[i][:, 0:W], in0=sg[i][:, 0:W], in1=cv[i][:, 0:W], op=ALU.mult),
                     reads=[sg_t[i], cv_t[i]], writes=[ub_t[i]])
                P.dma("sp", [(uT[f * 128:(f + 1) * 128, t0:t0 + W], ub[i][:, 0:W])], ub_ds[i], reads=[ub_t[i]],
                      writes=[cx.trk(uT_out, (f, ti))])


def load_w_rows(cx, st, wd, wd_t, wsrc, KC, width=D):
    P = cx.P
    stg = [st.sb([128, 2048], F32, "wstg") for _ in range(2)]
    stg_t = [Trk() for _ in range(2)]
    stg_ds = [P.dsem() for _ in range(2)]
    n = 0
    if width <= 1024:
        pieces = [(k0, min(KC, k0 + 2048 // width), 0, width) for k0 in range(0, KC, 2048 // width)]
    else:
        pieces = [(k, k + 1, w0, min(width, w0 + 2048)) for k in range(KC) for w0 in range(0, width, 2048)]
    for (k0, k1, w0, w1) in pieces:
        i = n % 2
        n += 1
        nk, nw = k1 - k0, w1 - w0
        sview = stg[i][:, 0:nk * nw].rearrange("p (k w) -> p k w", k=nk)
        P.dma("sp", [(sview, wsrc[:, k0:k1, w0:w1])], stg_ds[i], writes=[stg_t[i]])
        if i == 0:
            P.op("pool", "tensor_copy", dict(out=wd[:, k0:k1, w0:w1], in_=sview), reads=[stg_t[i]], writes=[wd_t])
        else:
            P.op("act", "copy", dict(out=wd[:, k0:k1, w0:w1], in_=sview), reads=[stg_t[i]], writes=[wd_t])


def stage_proj(cx, name, prefix, li, aT_in, KC, w_name, xres_in, xres_out, xT_out, final_out=None):
    cfg = cx.cfg
    P = cx.P
    with Stage(cx, name) as st:
        aT = cx.dt(aT_in, [KC * 128, cfg.nt], BF16)
        wsrc = cx.dt(w_name, [128, KC, D], F32)
        wd = st.sb([128, KC, D], BF16, "wd")
        wd_t = Trk()
        load_w_rows(cx, st, wd, wd_t, wsrc, KC)
        ep = Epi(cx, st, prefix, li, xres_in, xres_out, xT_out, final_out=final_out)
        NA = 2
        ab = [st.sb([128, KC, 512], BF16, "a") for _ in range(NA)]
        ab_t = [Trk() for _ in range(NA)]
        ab_ds = [P.dsem() for _ in range(NA)]
        yps = [st.ps([128, D], F32, "y") for _ in range(2)]
        yps_t = [Trk() for _ in range(2)]
        tiles = list(enumerate(cfg.tiles))
        if final_out is not None:
            tiles = tiles[1:]

        def load_a(ti):
            t0, W = cfg.tiles[ti]
            slot = ti % NA
            reads = [cx.trk(aT_in, (k, ti)) for k in range(KC)]
            P.dma("sp", [(ab[slot][:, :, 0:W], aT[:, t0:t0 + W].rearrange("(k p) t -> p k t", p=128))], ab_ds[slot],
                  reads=reads, writes=[ab_t[slot]])
            ep.load_xold(ti)
        load_a(tiles[0][0])
        cnt = 0
        pend = None
        for n, (ti, (t0, W)) in enumerate(tiles):
            if n + 1 < len(tiles):
                load_a(tiles[n + 1][0])
            slot = ti % NA
            for s in range(W // 128):
                i = cnt % 2
                cnt += 1
                for h in range(2):
                    for k in range(KC):
                        P.op("pe", "matmul", dict(
                            out=yps[i][:, h * 512:(h + 1) * 512], lhsT=ab[slot][:, k, s * 128:(s + 1) * 128],
                            rhs=wd[:, k, h * 512:(h + 1) * 512], start=(k == 0), stop=(k == KC - 1)),
                            reads=[ab_t[slot], wd_t], writes=[yps_t[i]], signal=(k == KC - 1))
                st_a = ep.sub_a(ti, s, yps[i], yps_t[i])
                if pend is not None:
                    ep.sub_b(pend)
                pend = st_a
        if pend is not None:
            ep.sub_b(pend)


def build_program(cfg, stages, ext_in, ext_out, internal=(), want_names=False):
    nc = bass.Bass("TRN2", target_bir_lowering=False)
    es = contextlib.ExitStack()
    with es:
        cx = Ctx(nc, es, cfg, None if ext_in is None else set(ext_in), set(ext_out))
        cx.internal = set(internal)
        for fn in stages:
            fn(cx)
        cx.P.finish()
        ninst = cx.P.ninst
    if want_names:
        return nc, ninst, [k for k, v in cx.kinds.items() if v == "ExternalInput"]
    return nc, ninst


def lay_cols(v):
    return np.ascontiguousarray(v.reshape(-1, 128).T)


def lay_bcast(v):
    return np.ascontiguousarray(np.broadcast_to(v[None, :], (128, v.shape[0])))


def lay_w_chunks(w):
    K, N = w.shape
    return np.ascontiguousarray(w.reshape(K // 128, 128, N // 128, 128).transpose(2, 1, 0, 3))


def lay_w_rows(w):
    K, N = w.shape
    return np.ascontiguousarray(w.reshape(K // 128, 128, N).transpose(1, 0, 2))


def host_consts(inp):
    c = {}
    c["ident_f32"] = np.eye(128, dtype=np.float32)
    for li in range(DEPTH):
        bup = inp["f_b_up"][li]
        wdw = inp["f_w_dw"][li]
        bdw = inp["f_b_dw"][li]
        cols = np.stack([lay_cols(bup), lay_cols(wdw[0]), lay_cols(wdw[1]), lay_cols(wdw[2]), lay_cols(bdw)], axis=-1)
        c["f%d_cols" % li] = np.ascontiguousarray(cols.astype(np.float32))
        c["f%d_wup" % li] = lay_w_chunks(inp["f_w_up"][li])
        c["f%d_wdn" % li] = lay_w_rows(inp["f_w_down"][li])
        c["f%d_bb" % li] = lay_bcast(inp["f_b_down"][li])
        c["f%d_g" % li] = lay_bcast(inp["ln_ffn_g"][li])
        c["f%d_b" % li] = lay_bcast(inp["ln_ffn_b"][li])
    for j in range(inp["a_w_in"].shape[0]):
        bi = inp["a_b_in"][j]
        wdw = inp["a_w_dw"][j]
        cols = np.concatenate([lay_cols(bi[:D])[:, :, None], lay_cols(bi[D:])[:, :, None], lay_cols(inp["a_b_dw"][j])[:, :, None],
                               np.stack([lay_cols(wdw[k]) for k in range(CONVK)], axis=-1)], axis=-1)
        c["a%d_cols" % j] = np.ascontiguousarray(cols.astype(np.float32))
        c["a%d_win" % j] = lay_w_chunks(inp["a_w_in"][j])
        c["a%d_wout" % j] = lay_w_rows(inp["a_w_out"][j])
        c["a%d_lcols" % j] = np.ascontiguousarray(np.stack([lay_cols(inp["a_ln_g"][j]), lay_cols(inp["a_ln_b"][j])], axis=-1))
    c["ident_bf16"] = np.eye(128, dtype=np.float32).astype(BF16_NP)
    wqkv = inp["b_w_qkv"][0]
    c["b0_wqk"] = lay_w_chunks(wqkv[:, :2 * D])
    c["b0_wv"] = lay_w_rows(wqkv[:, 2 * D:])
    c["b0_wo"] = lay_w_rows(inp["b_w_o"][0])
    lam = np.stack([inp["b_lq1"][0], inp["b_lk1"][0], inp["b_lq2"][0], inp["b_lk2"][0]], axis=0)
    c["b0_lam"] = np.ascontiguousarray(np.broadcast_to(lam[None], (128, 4, 64))).astype(np.float32)
    c["b0_gsub"] = lay_bcast(inp["b_subln_g"][0])
    win = inp["c_w_in"][0]
    c["c0_wu"] = lay_w_chunks(win[:, :GH])
    c["c0_wv"] = lay_w_rows(win[:, GH:])
    c["c0_wo"] = lay_w_rows(inp["c_w_out"][0])
    c["c0_ucols"] = lay_cols(inp["c_b_in"][0][:GH])
    c["c0_bvb"] = lay_bcast(inp["c_b_in"][0][GH:])
    c["c0_gcols"] = np.ascontiguousarray(np.stack([lay_cols(inp["c_ln_g"][0]), lay_cols(inp["c_ln_b"][0])], axis=-1))
    c["c0_wsT"] = np.ascontiguousarray(inp["c_w_s"][0].transpose(2, 0, 1))
    c["c0_trim"] = (np.arange(128)[None, :] >= np.arange(128)[:, None]).astype(np.float32)
    c["c0_bsb"] = np.ascontiguousarray(np.broadcast_to(inp["c_b_s"][0][None], (128, 4, 128))).astype(np.float32)
    pp = np.arange(128)[:, None, None]
    mm = np.arange(4)[None, :, None]
    cc = np.arange(512)[None, None, :]
    c["trimask"] = (cc >= pp + 128 * mm).astype(np.float32).astype(BF16_NP)
    for li in range(DEPTH):
        kind, j = li % 3, li // 3
        bo = [inp["a_b_out"], None, inp["c_b_out"]][kind]
        c["m%d_bb" % li] = lay_bcast(bo[j]) if bo is not None else np.zeros((128, D), np.float32)
        c["m%d_g" % li] = lay_bcast(inp["ln_mix_g"][li])
        c["m%d_b" % li] = lay_bcast(inp["ln_mix_b"][li])
    return c


class ChunkLoader:
    def __init__(self, cx, st, wsrc, nslots=4, cast_eng="pool"):
        P = cx.P
        self.cx, self.wsrc = cx, wsrc
        self.n = nslots
        self.wb = [st.sb([128, DC, 128], BF16, "w") for _ in range(nslots)]
        self.wb_t = [Trk() for _ in range(nslots)]
        self.wst = [st.sb([128, DC, 128], F32, "wst") for _ in range(nslots)]
        self.wst_t = [Trk() for _ in range(nslots)]
        self.ds = [P.dsem() for _ in range(nslots)]
        self.cnt = 0
        self.cast_eng = cast_eng

    def load(self, j):
        P = self.cx.P
        slot = self.cnt % self.n
        self.cnt += 1
        P.dma("sp", [(self.wst[slot][:], self.wsrc[j])], self.ds[slot], writes=[self.wst_t[slot]])
        if self.cast_eng == "pool":
            P.op("pool", "tensor_copy", dict(out=self.wb[slot][:], in_=self.wst[slot][:]),
                 reads=[self.wst_t[slot]], writes=[self.wb_t[slot]])
        else:
            P.op("act", "copy", dict(out=self.wb[slot][:], in_=self.wst[slot][:]),
                 reads=[self.wst_t[slot]], writes=[self.wb_t[slot]])
        return slot


def stage_c1(cx, j_idx, xT_in, cv_out):
    cfg = cx.cfg
    P = cx.P
    with Stage(cx, "c1_%d" % j_idx) as st:
        cv = cx.dt(cv_out, [cfg.nt, D], F32)
        (cols, hm, ident), cols_t = load_consts(cx, st, [("a%d_cols" % j_idx, [128, DC, 34]), ("hm", [128, 1]),
                                                         ("ident_f32", [128, 128])])
        hb = st.sb([128, DC, 1], F32, "hb")
        wdh = st.sb([128, DC, CONVK], F32, "wdh")
        d_t = Trk()
        P.op("pool", "tensor_scalar", dict(out=hb[:], in0=cols[:, :, 1:2], scalar1=0.5, scalar2=None, op0=ALU.mult),
             reads=[cols_t], writes=[d_t])
        d2_t = Trk()
        P.op("pool", "tensor_scalar", dict(out=wdh[:], in0=cols[:, :, 3:34], scalar1=0.5, scalar2=None, op0=ALU.mult),
             reads=[cols_t], writes=[d2_t])
        xT, xT_t = load_xT_resident(cx, st, xT_in)
        wl = ChunkLoader(cx, st, cx.dt("a%d_win" % j_idx, [2 * DC, 128, DC, 128], F32))
        aps = [st.ps([128, 512], F32, "a") for _ in range(2)]
        aps_t = [Trk() for _ in range(2)]
        gps = [st.ps([128, 512], F32, "g") for _ in range(2)]
        gps_t = [Trk() for _ in range(2)]
        tps = [st.ps([128, 512], F32, "tp") for _ in range(2)]
        tps_t = [Trk() for _ in range(2)]
        th = [st.sb([128, 512], F32, "th") for _ in range(2)]
        th_t = [Trk() for _ in range(2)]
        asb = [st.sb([128, 512], F32, "asb") for _ in range(2)]
        asb_t = [Trk() for _ in range(2)]
        hbuf = [st.sb([128, 30 + 512], F32, "h") for _ in range(2)]
        hbuf_t = [Trk() for _ in range(2)]
        acc1 = [st.sb([128, 512], F32, "acc1") for _ in range(2)]
        acc1_t = [Trk() for _ in range(2)]
        acc2 = [st.sb([128, 512], F32, "acc2") for _ in range(2)]
        acc2_t = [Trk() for _ in range(2)]
        ct = [st.sb([128, 4, 128], F32, "ct") for _ in range(2)]
        ct_t = [Trk() for _ in range(2)]
        ct_ds = [P.dsem() for _ in range(2)]
        pending = [(wl.load(0), wl.load(DC))]
        it = 0
        for j in range(DC):
            if j + 1 < DC:
                pending.append((wl.load(j + 1), wl.load(DC + j + 1)))
            sa, sg = pending.pop(0)
            for ti, (t0, W) in enumerate(cfg.tiles):
                i = it % 2
                it += 1
                for (slot, pst, pstt) in ((sa, aps[i], aps_t[i]), (sg, gps[i], gps_t[i])):
                    for k in range(DC):
                        P.op("pe", "matmul", dict(out=pst[:, 0:W], lhsT=wl.wb[slot][:, k, :], rhs=xT[ti][:, k, 0:W],
                                                  start=(k == 0), stop=(k == DC - 1)),
                             reads=[wl.wb_t[slot], xT_t[ti]], writes=[pstt], signal=(k == DC - 1))
                P.op("act", "activation", dict(out=th[i][:, 0:W], in_=gps[i][:, 0:W], func=AF.Tanh, bias=hb[:, j, :], scale=0.5),
                     reads=[gps_t[i], d_t], writes=[th_t[i]])
                P.op("act", "activation", dict(out=asb[i][:, 0:W], in_=aps[i][:, 0:W], func=AF.Identity, bias=cols[:, j, 0:1],
                                               scale=1.0), reads=[aps_t[i], cols_t], writes=[asb_t[i]])
                h = hbuf[i]
                if ti == 0:
                    P.op("pool", "memset", dict(ap=h[:, 0:30], constant=0.0), writes=[hbuf_t[i]])
                else:
                    Wp = cfg.tiles[ti - 1][1]
                    P.op("pool", "tensor_copy", dict(out=h[:, 0:30], in_=hbuf[1 - i][:, Wp:Wp + 30]),
                         reads=[hbuf_t[1 - i]], writes=[hbuf_t[i]])
                P.op("dve", "scalar_tensor_tensor", dict(out=h[:, 30:30 + W], in0=th[i][:, 0:W], scalar=1.0, in1=asb[i][:, 0:W],
                                                         op0=ALU.add, op1=ALU.mult),
                     reads=[th_t[i], asb_t[i]], writes=[hbuf_t[i]])
                if ti == 0:
                    P.op("dve", "tensor_scalar", dict(out=h[:, 30:30 + W], in0=h[:, 30:30 + W], scalar1=hm[:, 0:1], scalar2=None,
                                                      op0=ALU.mult), reads=[cols_t], writes=[hbuf_t[i]])
                a1, a2 = acc1[i], acc2[i]
                P.op("dve", "tensor_scalar", dict(out=a1[:, 0:W], in0=h[:, 30:30 + W], scalar1=wdh[:, j, 30:31],
                                                  scalar2=cols[:, j, 2:3], op0=ALU.mult, op1=ALU.add),
                     reads=[hbuf_t[i], d2_t, cols_t], writes=[acc1_t[i]])
                P.op("dve", "tensor_scalar", dict(out=a2[:, 0:W], in0=h[:, 29:29 + W], scalar1=wdh[:, j, 29:30], scalar2=None,
                                                  op0=ALU.mult), reads=[hbuf_t[i], d2_t], writes=[acc2_t[i]])
                for k in range(28, -1, -1):
                    a, at = (a1, acc1_t[i]) if k % 2 == 0 else (a2, acc2_t[i])
                    P.op("dve", "scalar_tensor_tensor", dict(out=a[:, 0:W], in0=h[:, k:k + W], scalar=wdh[:, j, k:k + 1],
                                                             in1=a[:, 0:W], op0=ALU.mult, op1=ALU.add),
                         reads=[hbuf_t[i], at], writes=[at])
                P.op("dve", "tensor_tensor", dict(out=a1[:, 0:W], in0=a1[:, 0:W], in1=a2[:, 0:W], op=ALU.add),
                     reads=[acc1_t[i], acc2_t[i]], writes=[acc1_t[i]])
                ns = W // 128
                for s in range(ns):
                    P.op("pe", "transpose", dict(out=tps[i][:, s * 128:(s + 1) * 128], in_=a1[:, s * 128:(s + 1) * 128],
                                                 identity=ident[:]),
                         reads=[acc1_t[i], cols_t], writes=[tps_t[i]], signal=(s == ns - 1))
                P.op("act", "copy", dict(out=ct[i][:, 0:ns, :], in_=tps[i][:, 0:W].rearrange("p (s c) -> p s c", s=ns)),
                     reads=[tps_t[i]], writes=[ct_t[i]])
                P.dma("sp", [(cv[t0:t0 + W, j * 128:(j + 1) * 128].rearrange("(s p) c -> p s c", p=128), ct[i][:, 0:ns, :])],
                      ct_ds[i], reads=[ct_t[i]], writes=[cx.trk(cv_out, (ti, j))])


def stage_tm_proj(cx, name, prefix, src_name, src_dtype, prenorm, cols_name, w_name, xres_in, xres_out, xT_out):
    cfg = cx.cfg
    P = cx.P
    with Stage(cx, name) as st:
        src = cx.dt(src_name, [cfg.nt, D], src_dtype)
        wsrc = cx.dt(w_name, [128, DC, D], F32)
        wd = st.sb([128, DC, D], BF16, "wd")
        wd_t = Trk()
        load_w_rows(cx, st, wd, wd_t, wsrc, DC)
        ep = Epi(cx, st, prefix, 0, xres_in, xres_out, xT_out)
        ds = P.dsem()
        identb = st.sb([128, 128], BF16, "identb")
        identb_t = Trk()
        pairs = [(identb[:], cx.dt("ident_bf16", [128, 128], BF16))]
        if prenorm:
            lcols = st.sb([128, DC, 2], F32, "lcols")
            pairs.append((lcols[:], cx.dt(cols_name, [128, DC, 2], F32)))
        P.dma("sp", pairs, ds, writes=[identb_t])
        NA = 2
        ib = [st.sb([128, 4, D], src_dtype, "in") for _ in range(NA)]
        ib_t = [Trk() for _ in range(NA)]
        ib_ds = [P.dsem() for _ in range(NA)]
        xb = [st.sb([128, D], BF16, "xb") for _ in range(2)]
        xb_t = [Trk() for _ in range(2)]
        stat = [st.sb([128, 16], F32, "pstat") for _ in range(2)]
        stat_t = [Trk() for _ in range(2)]
        zT = [st.sb([128, DC, 128], BF16, "zT") for _ in range(2)]
        zT_t = [Trk() for _ in range(2)]
        tpb = [st.ps([128, D], BF16, "tpb") for _ in range(2)]
        tpb_t = [Trk() for _ in range(2)]
        yps = [st.ps([128, D], F32, "y") for _ in range(2)]
        yps_t = [Trk() for _ in range(2)]

        def load_in(ti):
            t0, W = cfg.tiles[ti]
            slot = ti % NA
            reads = [cx.trk(src_name, (ti, j)) for j in range(DC)]
            P.dma("sp", [(ib[slot][:, 0:W // 128, :], src[t0:t0 + W, :].rearrange("(s p) d -> p s d", p=128))], ib_ds[slot],
                  reads=reads, writes=[ib_t[slot]])
            ep.load_xold(ti)
        subs = [(ti, s) for ti, (t0, W) in enumerate(cfg.tiles) for s in range(W // 128)]
        loaded = set()

        def ensure_loaded(ti):
            if ti < len(cfg.tiles) and ti not in loaded:
                load_in(ti)
                loaded.add(ti)

        def phase_a(n):
            ti, s = subs[n]
            slot = ti % NA
            i = n % 2
            ensure_loaded(ti)
            if s == 1:
                ensure_loaded(ti + 1)
            xin = ib[slot][:, s, :]
            if prenorm:
                stt = stat[i]
                for h in range(2):
                    P.op("dve", "bn_stats", dict(out=stt[:, h * 6:(h + 1) * 6], in_=xin[:, h * 512:(h + 1) * 512]),
                         reads=[ib_t[slot]], writes=[stat_t[i]])
                P.op("dve", "bn_aggr", dict(out=stt[:, 12:14], in_=stt[:, 0:12]), reads=[stat_t[i]], writes=[stat_t[i]])
                P.op("pool", "tensor_scalar", dict(out=stt[:, 14:15], in0=stt[:, 13:14], scalar1=float(LN_EPS), scalar2=None,
                                                   op0=ALU.add), reads=[stat_t[i]], writes=[stat_t[i]])
                P.op("pool", "tensor_tensor", dict(out=stt[:, 14:15], in0=stt[:, 14:15], in1=ep.mh[:], op=ALU.pow),
                     reads=[stat_t[i], ep.mh_t], writes=[stat_t[i]])
                P.op("dve", "tensor_scalar", dict(out=xb[i][:], in0=xin, scalar1=stt[:, 12:13], scalar2=stt[:, 14:15],
                                                  op0=ALU.subtract, op1=ALU.mult),
                     reads=[ib_t[slot], stat_t[i]], writes=[xb_t[i]])
                tin_t = xb_t[i]
                tin_ap = lambda k: xb[i][:, k * 128:(k + 1) * 128]
            else:
                tin_t = ib_t[slot]
                tin_ap = lambda k: ib[slot][:, s, k * 128:(k + 1) * 128]
            for k in range(DC):
                P.op("pe", "transpose", dict(out=tpb[i][:, k * 128:(k + 1) * 128], in_=tin_ap(k), identity=identb[:]),
                     reads=[tin_t, identb_t], writes=[tpb_t[i]], signal=(k == DC - 1))
            if prenorm:
                for k in range(DC):
                    P.op("act", "activation", dict(out=zT[i][:, k, :], in_=tpb[i][:, k * 128:(k + 1) * 128], func=AF.Silu,
                                                   bias=lcols[:, k, 1:2], scale=lcols[:, k, 0:1]),
                         reads=[tpb_t[i], identb_t], writes=[zT_t[i]])
            else:
                P.op("act", "copy", dict(out=zT[i][:], in_=tpb[i][:].rearrange("p (k t) -> p k t", k=DC)),
                     reads=[tpb_t[i]], writes=[zT_t[i]])

        def phase_b(n):
            ti, s = subs[n]
            i = n % 2
            for h in range(2):
                for k in range(DC):
                    P.op("pe", "matmul", dict(out=yps[i][:, h * 512:(h + 1) * 512], lhsT=zT[i][:, k, :],
                                              rhs=wd[:, k, h * 512:(h + 1) * 512], start=(k == 0), stop=(k == DC - 1)),
                         reads=[zT_t[i], wd_t], writes=[yps_t[i]], signal=(k == DC - 1))
            return ep.sub_a(ti, s, yps[i], yps_t[i])

        phase_a(0)
        pend = None
        for n in range(len(subs)):
            if n + 1 < len(subs):
                phase_a(n + 1)
            st_a = phase_b(n)
            if pend is not None:
                ep.sub_b(pend)
            pend = st_a
        ep.sub_b(pend)


def stage_a1(cx, xT_in, qT_out, kT_out, v_out):
    cfg = cx.cfg
    P = cx.P
    with Stage(cx, "a1") as st:
        qT = cx.dt(qT_out, [D, cfg.nt], BF16)
        kT = cx.dt(kT_out, [D, cfg.nown], BF16)
        v = cx.dt(v_out, [cfg.nown, D], BF16)
        xT, xT_t = load_xT_resident(cx, st, xT_in)
        wl = ChunkLoader(cx, st, cx.dt("b0_wqk", [2 * DC, 128, DC, 128], F32))
        wv = st.sb([128, DC, D], BF16, "wv")
        wv_t = Trk()
        load_w_rows(cx, st, wv, wv_t, cx.dt("b0_wv", [128, DC, D], F32), DC)
        ps = [st.ps([128, 512], F32, "ps") for _ in range(2)]
        ps_t = [Trk() for _ in range(2)]
        ob = [st.sb([128, 512], BF16, "ob") for _ in range(2)]
        ob_t = [Trk() for _ in range(2)]
        ob_ds = [P.dsem() for _ in range(2)]
        vps = [st.ps([128, D], F32, "vps") for _ in range(2)]
        vps_t = [Trk() for _ in range(2)]
        vb = [st.sb([128, D], BF16, "vb") for _ in range(2)]
        vb_t = [Trk() for _ in range(2)]
        vb_ds = [P.dsem() for _ in range(2)]
        pending = [wl.load(0)]
        it = 0
        for j in range(2 * DC):
            if j + 1 < 2 * DC:
                pending.append(wl.load(j + 1))
            slot = pending.pop(0)
            for ti, (t0, W) in enumerate(cfg.tiles):
                if j >= DC and ti == 0:
                    continue
                i = it % 2
                it += 1
                for k in range(DC):
                    P.op("pe", "matmul", dict(out=ps[i][:, 0:W], lhsT=wl.wb[slot][:, k, :], rhs=xT[ti][:, k, 0:W],
                                              start=(k == 0), stop=(k == DC - 1)),
                         reads=[wl.wb_t[slot], xT_t[ti]], writes=[ps_t[i]], signal=(k == DC - 1))
                if it % 2 == 0:
                    P.op("act", "copy", dict(out=ob[i][:, 0:W], in_=ps[i][:, 0:W]), reads=[ps_t[i]], writes=[ob_t[i]])
                else:
                    P.op("dve", "tensor_copy", dict(out=ob[i][:, 0:W], in_=ps[i][:, 0:W]), reads=[ps_t[i]], writes=[ob_t[i]])
                if j < DC:
                    dst = qT[j * 128:(j + 1) * 128, t0:t0 + W]
                    key = (qT_out, (j, ti))
                else:
                    dst = kT[(j - DC) * 128:(j - DC + 1) * 128, t0 - HALO:t0 - HALO + W]
                    key = (kT_out, (j - DC, ti))
                P.dma("sp", [(dst, ob[i][:, 0:W])], ob_ds[i], reads=[ob_t[i]], writes=[cx.trk(*key)])
        cnt = 0
        for ti, (t0, W) in enumerate(cfg.tiles):
            if ti == 0:
                continue
            for s in range(W // 128):
                i = cnt % 2
                cnt += 1
                for h in range(2):
                    for k in range(DC):
                        P.op("pe", "matmul", dict(out=vps[i][:, h * 512:(h + 1) * 512], lhsT=xT[ti][:, k, s * 128:(s + 1) * 128],
                                                  rhs=wv[:, k, h * 512:(h + 1) * 512], start=(k == 0), stop=(k == DC - 1)),
                             reads=[xT_t[ti], wv_t], writes=[vps_t[i]], signal=(k == DC - 1))
                if cnt % 2 == 0:
                    P.op("act", "copy", dict(out=vb[i][:], in_=vps[i][:]), reads=[vps_t[i]], writes=[vb_t[i]])
                else:
                    P.op("dve", "tensor_copy", dict(out=vb[i][:], in_=vps[i][:]), reads=[vps_t[i]], writes=[vb_t[i]])
                o0 = t0 - HALO + s * 128
                P.dma("sp", [(v[o0:o0 + 128, :], vb[i][:])], vb_ds[i], reads=[vb_t[i]], writes=[cx.trk(v_out, (ti, s))])


def stage_a2(cx, li, qT_in, kT_own, kT_past, v_own, v_past, attn_out):
    cfg = cx.cfg
    P = cx.P
    nown = cfg.nown
    NKP = nown // 128
    lam_init = 0.8 - 0.6 * math.exp(-0.3 * li)
    with Stage(cx, "a2") as st:
        qT = cx.dt(qT_in, [D, cfg.nt], BF16)
        kTo = cx.dt(kT_own, [D, nown], BF16)
        vo = cx.dt(v_own, [nown, D], BF16)
        if "kpast_fn" in cx.views:
            kpast_fn, vpast_fn = cx.views["kpast_fn"], cx.views["vpast_fn"]
        else:
            kTp = cx.dt(kT_past, [D, nown], BF16)
            vp = cx.dt(v_past, [nown, D], BF16)
            kpast_fn = lambda h: kTp[h * 128:(h + 1) * 128, :]
            vpast_fn = lambda k0, k1, h: vp[k0 * 128:k1 * 128, h * 128:(h + 1) * 128]
        att = cx.dt(attn_out, [cfg.nt, D], BF16)
        (lamin, gsub_raw, pbias), c_t = load_consts(cx, st, [("b0_lam", [128, 4, 64]), ("b0_gsub", [128, 128]), ("pbias", [128, 1])])
        tri = st.sb([128, 4, 512], BF16, "tri")
        tri_t = Trk()
        P.dma("sp", [(tri[:], cx.dt("trimask", [128, 4, 512], BF16))], P.dsem(), writes=[tri_t])
        mh = st.sb([128, 1], F32, "mh")
        mh_t = Trk()
        P.op("pool", "memset", dict(ap=mh[:], constant=-0.5), writes=[mh_t])
        prod = st.sb([128, 2, 64], F32, "prod")
        sm0 = st.sb([128, 8], F32, "sm0")
        l_t = Trk()
        for c in range(2):
            P.op("dve", "tensor_tensor", dict(out=prod[:, c, :], in0=lamin[:, 2 * c, :], in1=lamin[:, 2 * c + 1, :], op=ALU.mult),
                 reads=[c_t], writes=[l_t])
            P.op("dve", "tensor_reduce", dict(out=sm0[:, c:c + 1], in_=prod[:, c, :], axis=mybir.AxisListType.X, op=ALU.add),
                 reads=[l_t], writes=[l_t])
        P.op("act", "activation", dict(out=sm0[:, 2:4], in_=sm0[:, 0:2], func=AF.Exp), reads=[l_t], writes=[l_t])
        nlam = st.sb([128, 1], F32, "nlam")
        gsub = st.sb([128, 128], F32, "gsub")
        P.op("dve", "tensor_tensor", dict(out=nlam[:], in0=sm0[:, 3:4], in1=sm0[:, 2:3], op=ALU.subtract), reads=[l_t], writes=[l_t])
        P.op("dve", "tensor_scalar", dict(out=nlam[:], in0=nlam[:], scalar1=float(-lam_init), scalar2=None, op0=ALU.add),
             reads=[l_t], writes=[l_t])
        P.op("dve", "tensor_scalar", dict(out=gsub[:], in0=gsub_raw[:], scalar1=float(1.0 - lam_init), scalar2=None, op0=ALU.mult),
             reads=[c_t, l_t], writes=[l_t])
        kp_sb = [st.sb([128, nown], BF16, "kp") for _ in range(2)]
        ko_sb = [st.sb([128, nown], BF16, "ko") for _ in range(2)]
        vp_sb = [st.sb([128, NKP, 129], BF16, "vp") for _ in range(2)]
        vo_sb = [st.sb([128, NKP, 129], BF16, "vo") for _ in range(2)]
        kv_t = [Trk() for _ in range(2)]
        kv_ds = [P.dsem() for _ in range(2)]
        for b in range(2):
            P.op("pool", "memset", dict(ap=vp_sb[b][:, :, 128:129], constant=1.0), writes=[kv_t[b]])
            P.op("pool", "memset", dict(ap=vo_sb[b][:, :, 128:129], constant=1.0), writes=[kv_t[b]])

        def load_head(h):
            b = h % 2
            pairs = [(kp_sb[b][:], kpast_fn(h)), (ko_sb[b][:], kTo[h * 128:(h + 1) * 128, :])]
            for k0 in range(0, NKP, 8):
                k1 = min(NKP, k0 + 8)
                pairs.append((vp_sb[b][:, k0:k1, 0:128], vpast_fn(k0, k1, h).rearrange("(kt p) e -> p kt e", p=128)))
                pairs.append((vo_sb[b][:, k0:k1, 0:128],
                              vo[k0 * 128:k1 * 128, h * 128:(h + 1) * 128].rearrange("(kt p) e -> p kt e", p=128)))
            reads = [cx.trk(kT_own, (h, ti)) for ti in range(len(cfg.tiles))] + \
                    [cx.trk(v_own, (ti, s)) for ti in range(len(cfg.tiles)) for s in range(4)] + [cx.trk(kT_past, 0), cx.trk(v_past, 0)]
            P.dma("sp", pairs, kv_ds[b], reads=reads, writes=[kv_t[b]])
        qb = [st.sb([128, 512], BF16, "qb") for _ in range(2)]
        qb_t = [Trk() for _ in range(2)]
        qb_ds = [P.dsem() for _ in range(2)]
        sps = [st.ps([128, 512], F32, "sps") for _ in range(4)]
        sps_t = [Trk() for _ in range(4)]
        acc = [st.ps([128, 512], F32, "acc") for _ in range(4)]
        acc_t = [Trk() for _ in range(4)]
        NF = 2
        ob = [st.sb([128, 128], F32, "o") for _ in range(NF)]
        junk = [st.sb([128, 128], F32, "junk") for _ in range(NF)]
        sm = [st.sb([128, 8], F32, "sm") for _ in range(NF)]
        onb = [st.sb([128, 128], BF16, "onb") for _ in range(NF)]
        f_t = [Trk() for _ in range(NF)]
        onb_t = [Trk() for _ in range(NF)]
        onb_ds = [P.dsem() for _ in range(NF)]
        NPT = 6
        pT = [st.sb([128, 512], BF16, "pT") for _ in range(NPT)]
        pT_t = [Trk() for _ in range(NPT)]
        units = []
        for h in range(8):
            for ti, (t0, W) in enumerate(cfg.tiles):
                q0 = nown - HALO + t0
                for kt in range((q0 + W) // 128):
                    units.append((h, ti, kt))
        state = {"qcnt": 0, "fcnt": 0, "scnt": 0, "pcnt": 0, "heads": set(), "q": {}}
        unit_bufs = {}

        def stage_s(u):
            h, ti, kt = units[u]
            t0, W = cfg.tiles[ti]
            q0 = nown - HALO + t0
            b = h % 2
            if (h, ti) not in state["q"]:
                qi = state["qcnt"] % 2
                state["qcnt"] += 1
                state["q"] = {(h, ti): qi}
                P.dma("sp", [(qb[qi][:, 0:W], qT[h * 128:(h + 1) * 128, t0:t0 + W])], qb_ds[qi],
                      reads=[cx.trk(qT_in, (h, ti))], writes=[qb_t[qi]])
            qi = state["q"][(h, ti)]
            k0 = kt * 128
            past = kt < NKP
            ksb = kp_sb[b] if past else ko_sb[b]
            kcol = k0 if past else k0 - nown
            d = k0 - q0
            rs = []
            for c in range(2):
                r = state["scnt"] % 4
                state["scnt"] += 1
                pr = state["pcnt"] % NPT
                state["pcnt"] += 1
                P.op("pe", "matmul", dict(out=sps[r][:, 0:W], lhsT=ksb[c * 64:(c + 1) * 64, kcol:kcol + 128],
                                          rhs=qb[qi][c * 64:(c + 1) * 64, 0:W], start=True, stop=True),
                     reads=[kv_t[b], qb_t[qi]], writes=[sps_t[r]])
                P.op("act", "activation", dict(out=pT[pr][:, 0:W], in_=sps[r][:, 0:W], func=AF.Exp, scale=0.125,
                                               bias=(pbias[:, 0:1] if past else 0.0)),
                     reads=[sps_t[r], c_t], writes=[pT_t[pr]])
                if d >= 0:
                    P.op("dve" if c == 0 else "pool", "tensor_tensor",
                         dict(out=pT[pr][:, 0:W], in0=pT[pr][:, 0:W], in1=tri[:, d // 128, 0:W], op=ALU.mult),
                         reads=[pT_t[pr], tri_t], writes=[pT_t[pr]])
                rs.append(pr)
            unit_bufs[u] = rs

        def stage_pv(u):
            h, ti, kt = units[u]
            t0, W = cfg.tiles[ti]
            q0 = nown - HALO + t0
            NS = W // 128
            b = h % 2
            k0 = kt * 128
            past = kt < NKP
            vsb = vp_sb[b] if past else vo_sb[b]
            vkt = kt if past else kt - NKP
            d = k0 - q0
            rs = unit_bufs.pop(u)
            if h not in state["heads"]:
                state["heads"].add(h)
                if h + 1 < 8:
                    load_head(h + 1)
            for c in range(2):
                pr = rs[c]
                js = [j for j in range(NS) if not (d >= 0 and j < d // 128)]
                for j in js:
                    lastkt = (q0 + j * 128) // 128
                    P.op("pe", "matmul", dict(out=acc[j][:, c * 129:(c + 1) * 129], lhsT=pT[pr][:, j * 128:(j + 1) * 128],
                                              rhs=vsb[:, vkt, 0:129], start=(kt == 0 and c == 0), stop=(kt == lastkt),
                                              skip_group_check=True),
                         reads=[pT_t[pr], kv_t[b]], writes=[acc_t[j]],
                         signal=(j == js[-1] or (kt == lastkt and c == 1)))
            for j in range(NS):
                if (q0 + j * 128) // 128 != kt:
                    continue
                fi = state["fcnt"] % NF
                state["fcnt"] += 1
                a = acc[j]
                o, s_, ft = ob[fi], sm[fi], f_t[fi]
                P.op("dve", "reciprocal", dict(out=s_[:, 0:1], in_=a[:, 128:129]), reads=[acc_t[j]], writes=[ft])
                P.op("dve", "reciprocal", dict(out=s_[:, 1:2], in_=a[:, 257:258]), reads=[acc_t[j]], writes=[ft])
                P.op("dve", "tensor_tensor", dict(out=s_[:, 2:3], in0=s_[:, 1:2], in1=nlam[:, 0:1], op=ALU.mult),
                     reads=[ft, l_t], writes=[ft])
                P.op("dve", "tensor_scalar", dict(out=o[:], in0=a[:, 0:128], scalar1=s_[:, 0:1], scalar2=None, op0=ALU.mult),
                     reads=[acc_t[j], ft], writes=[ft])
                P.op("dve", "scalar_tensor_tensor", dict(out=o[:], in0=a[:, 129:257], scalar=s_[:, 2:3], in1=o[:],
                                                         op0=ALU.mult, op1=ALU.add), reads=[acc_t[j], ft], writes=[ft])
                P.op("act", "activation", dict(out=junk[fi][:], in_=o[:], func=AF.Square, accum_out=s_[:, 3:4]),
                     reads=[ft], writes=[ft])
                P.op("pool", "tensor_scalar", dict(out=s_[:, 4:5], in0=s_[:, 3:4], scalar1=1.0 / 128.0, scalar2=float(RMS_EPS),
                                                   op0=ALU.mult, op1=ALU.add), reads=[ft], writes=[ft])
                P.op("pool", "tensor_tensor", dict(out=s_[:, 4:5], in0=s_[:, 4:5], in1=mh[:], op=ALU.pow),
                     reads=[ft, mh_t], writes=[ft])
                P.op("dve", "scalar_tensor_tensor", dict(out=onb[fi][:], in0=o[:], scalar=s_[:, 4:5], in1=gsub[:],
                                                         op0=ALU.mult, op1=ALU.mult), reads=[ft, l_t], writes=[onb_t[fi]])
                r0 = t0 + j * 128
                P.dma("sp", [(att[r0:r0 + 128, h * 128:(h + 1) * 128], onb[fi][:])], onb_ds[fi], reads=[onb_t[fi]],
                      writes=[cx.trk(attn_out, (ti, h))])

        load_head(0)
        stage_s(0)
        for u in range(len(units)):
            if u + 1 < len(units):
                stage_s(u + 1)
            stage_pv(u)


def stage_g1a(cx, xT_in, gu_out):
    cfg = cx.cfg
    P = cx.P
    with Stage(cx, "g1a") as st:
        gu = cx.dt(gu_out, [GH, cfg.nt], BF16)
        (cols,), cols_t = load_consts(cx, st, [("c0_ucols", [128, GC])])
        xT, xT_t = load_xT_resident(cx, st, xT_in)
        wl = ChunkLoader(cx, st, cx.dt("c0_wu", [GC, 128, DC, 128], F32))
        ps = [st.ps([128, 512], F32, "ps") for _ in range(2)]
        ps_t = [Trk() for _ in range(2)]
        ub = [st.sb([128, 512], BF16, "ub") for _ in range(2)]
        ub_t = [Trk() for _ in range(2)]
        ub_ds = [P.dsem() for _ in range(2)]
        pending = [wl.load(0)]
        it = 0
        for f in range(GC):
            if f + 1 < GC:
                pending.append(wl.load(f + 1))
            slot = pending.pop(0)
            for ti, (t0, W) in enumerate(cfg.tiles):
                i = it % 2
                it += 1
                for k in range(DC):
                    P.op("pe", "matmul", dict(out=ps[i][:, 0:W], lhsT=wl.wb[slot][:, k, :], rhs=xT[ti][:, k, 0:W],
                                              start=(k == 0), stop=(k == DC - 1)),
                         reads=[wl.wb_t[slot], xT_t[ti]], writes=[ps_t[i]], signal=(k == DC - 1))
                P.op("act", "activation", dict(out=ub[i][:, 0:W], in_=ps[i][:, 0:W], func=AF.Gelu, bias=cols[:, f:f + 1], scale=1.0),
                     reads=[ps_t[i], cols_t], writes=[ub_t[i]])
                P.dma("sp", [(gu[f * 128:(f + 1) * 128, t0:t0 + W], ub[i][:, 0:W])], ub_ds[i], reads=[ub_t[i]],
                      writes=[cx.trk(gu_out, (f, ti))])


def stage_g1b(cx, xT_in, gu_in, go_out):
    cfg = cx.cfg
    P = cx.P
    with Stage(cx, "g1b") as st:
        xTd = cx.dt(xT_in, [D, cfg.nt], BF16)
        gu = cx.dt(gu_in, [GH, cfg.nt], BF16)
        go = cx.dt(go_out, [GH, cfg.nt], BF16)
        (bvb, gcols, wsr, trim, bsb), c_t = load_consts(cx, st, [("c0_bvb", [128, GH]), ("c0_gcols", [128, GC, 2]),
                                                                ("c0_wsT", [128, 4, 128]), ("c0_trim", [128, 128]),
                                                                ("c0_bsb", [128, 4, 128])])
        wv = st.sb([128, DC, GH], BF16, "wv")
        wv_t = Trk()
        load_w_rows(cx, st, wv, wv_t, cx.dt("c0_wv", [128, DC, GH], F32), DC, width=GH)
        mh = st.sb([128, 1], F32, "mh")
        mh_t = Trk()
        P.op("pool", "memset", dict(ap=mh[:], constant=-0.5), writes=[mh_t])
        ones = st.sb([128, 128], BF16, "ones")
        s_t = Trk()
        P.op("pool", "memset", dict(ap=ones[:], constant=1.0), writes=[s_t])
        wsT = st.sb([128, 4, 128], BF16, "wsT")
        for g in range(4):
            P.op("dve", "tensor_tensor", dict(out=wsT[:, g, :], in0=wsr[:, g, :], in1=trim[:], op=ALU.mult), reads=[c_t], writes=[s_t])
        rs = st.ps([128, 512], F32, "rs")
        rs_t = Trk()
        for g in range(4):
            P.op("pe", "matmul", dict(out=rs[:, g * 128:(g + 1) * 128], lhsT=ones[:], rhs=wsT[:, g, :], start=True, stop=True),
                 reads=[s_t], writes=[rs_t], signal=(g == 3))
        E = st.sb([128, GC, 128], F32, "E")
        E_t = Trk()
        for cc in range(GC):
            g = cc // 6
            P.op("dve", "scalar_tensor_tensor", dict(out=E[:, cc, :], in0=rs[:, g * 128:(g + 1) * 128], scalar=gcols[:, cc, 1:2],
                                                     in1=bsb[:, g, :], op0=ALU.mult, op1=ALU.add), reads=[rs_t, c_t], writes=[E_t])
        xb = [st.sb([128, DC, 512], BF16, "xTt") for _ in range(2)]
        xb_t = [Trk() for _ in range(2)]
        xb_ds = [P.dsem() for _ in range(2)]
        vps = [st.ps([128, 512], F32, "vps") for _ in range(2)]
        vps_t = [Trk() for _ in range(2)]
        svps = [st.ps([128, 512], F32, "svps") for _ in range(2)]
        svps_t = [Trk() for _ in range(2)]
        vbuf = st.sb([128, GH], F32, "vbuf")
        vbuf_t = [Trk() for _ in range(6)]
        stt = st.sb([128, 48], F32, "stt")
        stt_t = Trk()
        vhat = [st.sb([128, GH], BF16, "vhat") for _ in range(2)]
        vhat_t = [Trk() for _ in range(2)]
        uTn = [st.sb([128, GC, 128], BF16, "uTn") for _ in range(2)]
        uTn_t = [Trk() for _ in range(2)]
        uTn_ds = [P.dsem() for _ in range(2)]
        tmp = st.sb([128, GC, 128], F32, "tmp")
        tmp_t = [Trk() for _ in range(6)]
        oTb = st.sb([128, GC, 512], BF16, "oTb")
        oTb_t = Trk()
        oTb_ds = P.dsem()

        def load_x(ti):
            t0, W = cfg.tiles[ti]
            P.dma("sp", [(xb[ti % 2][:, :, 0:W], xTd[:, t0:t0 + W].rearrange("(k p) t -> p k t", p=128))], xb_ds[ti % 2],
                  reads=[cx.trk(xT_in, ti)], writes=[xb_t[ti % 2]])
        load_x(0)
        cnt = 0
        vcnt = 0
        scnt = 0
        for ti, (t0, W) in enumerate(cfg.tiles):
            if ti + 1 < len(cfg.tiles):
                load_x(ti + 1)
            xt = xb[ti % 2]
            for s in range(W // 128):
                i = cnt % 2
                cnt += 1
                c0 = t0 + s * 128
                P.dma("sp", [(uTn[i][:], gu[:, c0:c0 + 128].rearrange("(k p) t -> p k t", p=128))], uTn_ds[i],
                      reads=[cx.trk(gu_in, (f, ti)) for f in range(GC)], writes=[uTn_t[i]])
                for fc in range(6):
                    vi = vcnt % 2
                    vcnt += 1
                    sl = slice(fc * 512, (fc + 1) * 512)
                    for k in range(DC):
                        P.op("pe", "matmul", dict(out=vps[vi][:], lhsT=xt[:, k, s * 128:(s + 1) * 128], rhs=wv[:, k, sl],
                                                  start=(k == 0), stop=(k == DC - 1)),
                             reads=[xb_t[ti % 2], wv_t], writes=[vps_t[vi]], signal=(k == DC - 1))
                    P.op("dve", "tensor_tensor", dict(out=vbuf[:, sl], in0=vps[vi][:], in1=bvb[:, sl], op=ALU.add),
                         reads=[vps_t[vi], c_t], writes=[vbuf_t[fc]])
                    P.op("act", "activation", dict(out=vbuf[:, sl], in_=vbuf[:, sl], func=AF.Gelu), reads=[vbuf_t[fc]], writes=[vbuf_t[fc]])
                    P.op("dve", "bn_stats", dict(out=stt[:, fc * 6:(fc + 1) * 6], in_=vbuf[:, sl]), reads=[vbuf_t[fc]], writes=[stt_t])
                P.op("dve", "bn_aggr", dict(out=stt[:, 36:38], in_=stt[:, 0:36]), reads=[stt_t], writes=[stt_t])
                P.op("pool", "tensor_scalar", dict(out=stt[:, 38:39], in0=stt[:, 37:38], scalar1=float(LN_EPS), scalar2=None, op0=ALU.add),
                     reads=[stt_t], writes=[stt_t])
                P.op("pool", "tensor_tensor", dict(out=stt[:, 38:39], in0=stt[:, 38:39], in1=mh[:], op=ALU.pow),
                     reads=[stt_t, mh_t], writes=[stt_t])
                vh = vhat[i]
                for hh in range(2):
                    sl = slice(hh * 1536, (hh + 1) * 1536)
                    P.op("dve", "tensor_scalar", dict(out=vh[:, sl], in0=vbuf[:, sl], scalar1=stt[:, 36:37], scalar2=stt[:, 38:39],
                                                      op0=ALU.subtract, op1=ALU.mult),
                         reads=[stt_t] + vbuf_t[3 * hh:3 * hh + 3], writes=[vhat_t[i]])
                for cg in range(6):
                    si = scnt % 2
                    scnt += 1
                    for q in range(4):
                        cc = cg * 4 + q
                        g = cc // 6
                        P.op("pe", "matmul", dict(out=svps[si][:, q * 128:(q + 1) * 128], lhsT=vh[:, cc * 128:(cc + 1) * 128],
                                                  rhs=wsT[:, g, :], start=True, stop=True),
                             reads=[vhat_t[i], s_t], writes=[svps_t[si]], signal=(q == 3))
                    for q in range(4):
                        cc = cg * 4 + q
                        P.op("dve", "scalar_tensor_tensor", dict(out=tmp[:, cc, :], in0=svps[si][:, q * 128:(q + 1) * 128],
                                                                 scalar=gcols[:, cc, 0:1], in1=E[:, cc, :], op0=ALU.mult, op1=ALU.add),
                             reads=[svps_t[si], c_t, E_t], writes=[tmp_t[cg]])
                    P.op("pool", "tensor_tensor", dict(out=oTb[:, cg * 4:cg * 4 + 4, s * 128:(s + 1) * 128], in0=tmp[:, cg * 4:cg * 4 + 4, :],
                                                       in1=uTn[i][:, cg * 4:cg * 4 + 4, :], op=ALU.mult),
                         reads=[tmp_t[cg], uTn_t[i]], writes=[oTb_t])
            P.dma("sp", [(go[:, t0:t0 + W].rearrange("(k p) t -> p k t", p=128), oTb[:, :, 0:W])], oTb_ds, reads=[oTb_t],
                  writes=[cx.trk(go_out, (k, ti)) for k in range(GC)])


def ffn_stages(li, xT_in, xres_in, xres_out, xT_out, final_out=None):
    return [
        lambda cx: stage_f1(cx, li, xT_in, "uT"),
        lambda cx: stage_proj(cx, "f2_%d" % li, "f%d" % li, li, "uT", FC, "f%d_wdn" % li, xres_in, xres_out, xT_out,
                              final_out=final_out),
    ]


def stages_part1():
    st = [lambda cx: stage_prep(cx, "x_in", "xT_a")]
    st += [lambda cx: stage_c1(cx, 0, "xT_a", "cv"),
           lambda cx: stage_tm_proj(cx, "c2_0", "m0", "cv", F32, True, "a0_lcols", "a0_wout", "x_in", "xr_a", "xT_b")]
    st += ffn_stages(0, "xT_b", "xr_a", "xr_b", "xT_a")
    st += [lambda cx: stage_a1(cx, "xT_a", "qT", "kT_own", "v_own")]
    return st


def stages_part2():
    st = [lambda cx: stage_a2(cx, 1, "qT", "kT_own", "kT_past", "v_own", "v_past", "att"),
          lambda cx: stage_tm_proj(cx, "a3", "m1", "att", BF16, False, None, "b0_wo", "xr_b", "xr_a", "xT_b")]
    st += ffn_stages(1, "xT_b", "xr_a", "xr_b", "xT_a")
    st += [lambda cx: stage_g1a(cx, "xT_a", "gu"),
           lambda cx: stage_g1b(cx, "xT_a", "gu", "go"),
           lambda cx: stage_proj(cx, "g2", "m2", 2, "go", GC, "c0_wo", "xr_b", "xr_a", "xT_b")]
    st += ffn_stages(2, "xT_b", "xr_a", "xr_b", "xT_a")
    st += [lambda cx: stage_c1(cx, 1, "xT_a", "cv"),
           lambda cx: stage_tm_proj(cx, "c2_1", "m3", "cv", F32, True, "a1_lcols", "a1_wout", "xr_b", "xr_a", "xT_b")]
    st += ffn_stages(3, "xT_b", "xr_a", None, None, final_out="out")
    return st


def stage_xchg(cx):
    cfg = cx.cfg
    P = cx.P
    nown = cfg.nown
    kT = cx.dt("kT_own", [D, nown], BF16)
    v = cx.dt("v_own", [nown, D], BF16)
    groups = [[0, 1], [2, 3], [4, 5], [6, 7]]
    KR = 256
    VR = min(1024, nown)
    P.barrier()
    t = Trk()
    kall, vall = [], []
    for p in range(D // KR):
        dst = cx.nc.dram_tensor("kT_all%d" % p, [2 * KR, nown], BF16).ap()
        P.op("pool", "collective_compute", dict(kind="AllGather", op=ALU.bypass, replica_groups=groups,
                                                ins=[kT[p * KR:(p + 1) * KR, :].opt()], outs=[dst.opt()]), writes=[t])
        kall.append(dst)
    for p in range(nown // VR):
        dst = cx.nc.dram_tensor("v_all%d" % p, [2 * VR, D], BF16).ap()
        P.op("pool", "collective_compute", dict(kind="AllGather", op=ALU.bypass, replica_groups=groups,
                                                ins=[v[p * VR:(p + 1) * VR, :].opt()], outs=[dst.opt()]), writes=[t])
        vall.append(dst)

    def kpast_fn(h):
        r0 = (h % 2) * 128
        return kall[h // 2][r0:r0 + 128, :]

    def vpast_fn(k0, k1, h):
        p = (k0 * 128) // VR
        assert (k1 * 128 - 1) // VR == p
        r0 = k0 * 128 - p * VR
        return vall[p][r0:r0 + (k1 - k0) * 128, h * 128:(h + 1) * 128]
    cx.views["kpast_fn"] = kpast_fn
    cx.views["vpast_fn"] = vpast_fn
    P.barrier()


SCRATCH = {"xT_a", "xT_b", "xr_a", "xr_b", "cv", "uT", "qT", "kT_own", "v_own", "kT_past", "v_past", "att", "gu", "go",
           "kT_all", "v_all"}
FUSED = True
_PROG_CACHE = {}


def _get_prog(key, cfg, stages, ext_out, internal):
    if key not in _PROG_CACHE:
        _PROG_CACHE[key] = build_program(cfg, stages, None, ext_out, internal=internal, want_names=True)
    return _PROG_CACHE[key]


def kernel(**inputs):
    inputs = {k: np.asarray(v) for k, v in inputs.items()}
    x = inputs["x"].astype(np.float32, copy=False)
    B, S, _ = x.shape
    ncore = 8
    nown = S // 2
    cfg = Cfg(nown)
    consts = host_consts(inputs)
    per_core = []
    for c in range(ncore):
        b, half = c // 2, c % 2
        if half == 0:
            xin = np.concatenate([np.zeros((HALO, D), np.float32), x[b, 0:nown]], axis=0)
        else:
            xin = x[b, nown - HALO:2 * nown]
        per_core.append({
            "x_in": np.ascontiguousarray(xin),
            "hm": np.full((128, 1), float(half), np.float32),
            "pbias": np.full((128, 1), 0.0 if half == 1 else -80.0, np.float32),
        })
    if FUSED:
        nc, _, names = _get_prog(("fused", nown), cfg, stages_part1() + [stage_xchg] + stages_part2(), {"out"}, SCRATCH)
        maps = []
        for c in range(ncore):
            maps.append({k: (per_core[c][k] if k in per_core[c] else consts[k]) for k in names})
        res = run_bass_kernel_spmd(nc, maps, core_ids=list(range(ncore))).results
        out = np.empty((B, S, D), np.float32)
        for c in range(ncore):
            b, half = c // 2, c % 2
            out[b, half * nown:(half + 1) * nown] = res[c]["out"]
        return out
    out1 = {"xr_b", "qT", "kT_own", "v_own"}
    nc1, _, names1 = _get_prog(("p1", nown), cfg, stages_part1(), out1, SCRATCH - out1)
    maps1 = []
    for c in range(ncore):
        m = {}
        for k in names1:
            m[k] = per_core[c][k] if k in per_core[c] else consts[k]
        maps1.append(m)
    res1 = run_bass_kernel_spmd(nc1, maps1, core_ids=list(range(ncore))).results
    out2 = {"out"}
    in2 = {"xr_b", "qT", "kT_own", "v_own", "kT_past", "v_past"}
    nc2, _, names2 = _get_prog(("p2", nown), cfg, stages_part2(), out2, SCRATCH - in2)
    maps2 = []
    for c in range(ncore):
        m = {}
        src = res1[c - (c % 2)]
        for k in names2:
            if k == "kT_past":
                m[k] = src["kT_own"]
            elif k == "v_past":
                m[k] = src["v_own"]
            elif k in ("xr_b", "qT", "kT_own", "v_own"):
                m[k] = res1[c][k]
            elif k in per_core[c]:
                m[k] = per_core[c][k]
            else:
                m[k] = consts[k]
        maps2.append(m)
    res2 = run_bass_kernel_spmd(nc2, maps2, core_ids=list(range(ncore))).results
    out = np.empty((B, S, D), np.float32)
    for c in range(ncore):
        b, half = c // 2, c % 2
        out[b, half * nown:(half + 1) * nown] = res2[c]["out"]
    return out
```

```python
import contextlib
import math
import numpy as np
import ml_dtypes
import concourse.bass as bass
import concourse.mybir as mybir
from concourse.bass_utils import run_bass_kernel_spmd

F32 = mybir.dt.float32
BF16 = mybir.dt.bfloat16
BF16_NP = ml_dtypes.bfloat16
AF = mybir.ActivationFunctionType
ALU = mybir.AluOpType

D = 1024
DC = 8
DEPTH = 4
HALO = 256
NOWN_FULL = 4096
FF = 2816
FC = 22
CONVK = 31
GH = 3072
GC = 24
ALPHA = (2 * DEPTH) ** 0.25
LN_EPS = 1e-5
RMS_EPS = 1e-5
SEM_LIMIT = 60000
SELF_SYNC = True


class Tok:
    __slots__ = ("sem", "val")

    def __init__(self, sem, val):
        self.sem = sem
        self.val = val


class Trk:
    __slots__ = ("w", "r")

    def __init__(self):
        self.w = []
        self.r = []


class DSem:
    __slots__ = ("sem", "cnt")

    def __init__(self, sem):
        self.sem = sem
        self.cnt = 0


class Prog:
    def __init__(self, nc, es):
        self.nc = nc
        self.es = es
        self.engs = {"pe": nc.tensor, "act": nc.scalar, "dve": nc.vector, "pool": nc.gpsimd, "sp": nc.sync}
        self.q = {k: [] for k in self.engs}
        self.pool_sems = []
        n = 0
        while n < 96:
            try:
                self.pool_sems.append(es.enter_context(nc.semaphore("s%d" % n)))
            except Exception:
                break
            n += 1
        self.esem = {k: self.pool_sems.pop() for k in self.engs}
        self.dpool = [DSem(x) for x in self.pool_sems[8:]]
        self.pool_sems = self.pool_sems[:8]
        self.stage_ds = []
        self.ecnt = {k: 0 for k in self.engs}
        self.unsig = {k: False for k in self.engs}
        self.seen = {k: {} for k in self.engs}
        self.stage_dma = {}
        self.dsems = []
        self.ninst = 0

    def dsem(self):
        d = self.dpool.pop()
        self.stage_ds.append(d)
        return d

    def release_dsems(self):
        self.dpool.extend(self.stage_ds)
        self.stage_ds = []

    def _collect(self, eng, reads, writes, extra):
        need = {}

        def add(t):
            if t is None:
                return
            k = id(t.sem)
            if k not in need or need[k].val < t.val:
                need[k] = t
        for b in reads:
            for t in b.w:
                add(t)
        for b in writes:
            for t in b.w:
                add(t)
            for t in b.r:
                add(t)
        for t in extra:
            add(t)
        waits = []
        seen = self.seen[eng]
        own = self.esem[eng]
        for k, t in need.items():
            if t.sem is own and (eng == "pe" or not SELF_SYNC):
                continue
            if seen.get(k, 0) >= t.val:
                continue
            seen[k] = t.val
            waits.append(t)
        return waits

    @staticmethod
    def _commit(tok, reads, writes):
        for b in reads:
            b.r.append(tok)
            if len(b.r) > 64:
                best = {}
                for t in b.r:
                    k = id(t.sem)
                    if k not in best or best[k].val < t.val:
                        best[k] = t
                b.r = list(best.values())
        for b in writes:
            b.w = [tok]
            b.r = []

    def op(self, eng, name, kw, reads=(), writes=(), extra=(), signal=True):
        waits = self._collect(eng, reads, writes, extra)
        sem = self.esem[eng]
        if SELF_SYNC and eng != "pe":
            signal = True
        self.unsig[eng] = not signal
        if signal:
            self.ecnt[eng] += 1
            tok = Tok(sem, self.ecnt[eng])
        else:
            tok = Tok(sem, self.ecnt[eng] + 1)

        def emit(e):
            for t in waits:
                e.wait_ge(t.sem, t.val)
            ins = getattr(e, name)(**kw)
            if signal:
                ins.then_inc(sem, 1)
        self.q[eng].append(emit)
        self.ninst += 1
        self._commit(tok, reads, writes)
        if signal and self.ecnt[eng] >= SEM_LIMIT:
            self.esem[eng] = self.pool_sems.pop()
            self.ecnt[eng] = 0
        return tok

    def dma(self, eng, pairs, ds, reads=(), writes=(), extra=()):
        if ds.cnt + 16 * len(pairs) >= SEM_LIMIT:
            ds.sem = self.pool_sems.pop()
            ds.cnt = 0
        waits = self._collect(eng, reads, writes, extra)
        ds.cnt += 16 * len(pairs)
        sem = ds.sem
        tok = Tok(sem, ds.cnt)

        def emit(e):
            for t in waits:
                e.wait_ge(t.sem, t.val)
            for (o, i) in pairs:
                e.dma_start(out=o, in_=i).then_inc(sem, 16)
        self.q[eng].append(emit)
        self.ninst += len(pairs)
        self._commit(tok, reads, writes)
        self.stage_dma[id(sem)] = tok
        return tok

    def barrier(self):
        toks = []
        for k in self.engs:
            assert not self.unsig[k], k
            if self.ecnt[k] > 0:
                toks.append(Tok(self.esem[k], self.ecnt[k]))
        toks += list(self.stage_dma.values())
        self.stage_dma = {}
        for k in self.engs:
            waits = self._collect(k, (), (), toks)
            if waits:
                def emit(e, waits=waits):
                    for t in waits:
                        e.wait_ge(t.sem, t.val)
                self.q[k].append(emit)

    def finish(self):
        self.barrier()
        block = self.es.enter_context(self.nc.Block())
        q = self.q

        @block.tensor
        def _(e):
            for f in q["pe"]:
                f(e)

        @block.scalar
        def _(e):
            for f in q["act"]:
                f(e)

        @block.vector
        def _(e):
            for f in q["dve"]:
                f(e)

        @block.gpsimd
        def _(e):
            for f in q["pool"]:
                f(e)

        @block.sync
        def _(e):
            for f in q["sp"]:
                f(e)


class Cfg:
    def __init__(self, nown):
        self.nown = nown
        self.nt = HALO + nown
        self.tiles = [(0, HALO)] + [(HALO + 512 * i, 512) for i in range(nown // 512)]


class Ctx:
    def __init__(self, nc, es, cfg, ext_in, ext_out):
        self.nc = nc
        self.es = es
        self.cfg = cfg
        self.P = Prog(nc, es)
        self.ext_in = ext_in
        self.ext_out = ext_out
        self.dram = {}
        self.dtrk = {}
        self.kinds = {}
        self.internal = set()
        self.views = {}

    def dt(self, name, shape=None, dtype=None):
        if name in self.views:
            return self.views[name]
        if name not in self.dram:
            kind = "Internal"
            if name in self.ext_out:
                kind = "ExternalOutput"
            elif self.ext_in is None:
                if name not in self.internal:
                    kind = "ExternalInput"
            elif name in self.ext_in:
                kind = "ExternalInput"
            self.kinds[name] = kind
            self.dram[name] = self.nc.dram_tensor(name, list(shape), dtype, kind=kind).ap()
        return self.dram[name]

    def trk(self, name, idx):
        key = (name, idx)
        if key not in self.dtrk:
            self.dtrk[key] = Trk()
        return self.dtrk[key]


class Stage:
    def __init__(self, cx, name):
        self.cx = cx
        self.name = name
        self.es = contextlib.ExitStack()
        self.n = 0

    def __enter__(self):
        self.es.__enter__()
        return self

    def __exit__(self, *a):
        self.cx.P.barrier()
        self.cx.P.release_dsems()
        return self.es.__exit__(*a)

    def sb(self, shape, dtype, tag="t"):
        self.n += 1
        return self.es.enter_context(self.cx.nc.sbuf_tensor("%s_%s%d" % (self.name, tag, self.n), list(shape), dtype))

    def ps(self, shape, dtype, tag="p"):
        self.n += 1
        return self.es.enter_context(self.cx.nc.psum_tensor("%s_%s%d" % (self.name, tag, self.n), list(shape), dtype))


def load_consts(cx, st, specs, eng="sp"):
    P = cx.P
    ds = P.dsem()
    trk = Trk()
    ts = []
    pairs = []
    for (name, shape) in specs:
        src = cx.dt(name, shape, F32)
        t = st.sb(shape, F32, tag=name)
        ts.append(t)
        pairs.append((t[:], src))
    P.dma(eng, pairs, ds, writes=[trk])
    return ts, trk


class Epi:
    def __init__(self, cx, st, prefix, li, xres_in, xres_out, xT_out, final_out=None):
        P = cx.P
        self.cx, self.st = cx, st
        cfg = cx.cfg
        self.xres_in = cx.dt(xres_in, [cfg.nt, D], F32)
        self.xres_in_name = xres_in
        self.final_out = final_out
        if final_out is None:
            self.xres_out = cx.dt(xres_out, [cfg.nt, D], F32)
            self.xT_out = cx.dt(xT_out, [D, cfg.nt], BF16)
        else:
            self.xres_out = cx.dt(final_out, [cfg.nown, D], F32)
            self.xT_out = None
        self.xres_out_name = xres_out if final_out is None else final_out
        self.xT_out_name = xT_out
        (self.bb, self.gam, self.bet, self.ident), ct = load_consts(
            cx, st, [(prefix + "_bb", [128, D]), (prefix + "_g", [128, D]), (prefix + "_b", [128, D]),
                     ("ident_f32", [128, 128])])
        self.bb_t = self.gam_t = self.bet_t = self.ident_t = ct
        self.mh = st.sb([128, 1], F32, "mh")
        self.mh_t = Trk()
        P.op("pool", "memset", dict(ap=self.mh[:], constant=-0.5), writes=[self.mh_t])
        NR = 2
        self.xold = [st.sb([128, 4, D], F32, "xold") for _ in range(NR)]
        self.xold_t = [Trk() for _ in range(NR)]
        self.xold_ds = [P.dsem() for _ in range(NR)]
        self.s_ = [st.sb([128, D], F32, "s") for _ in range(2)]
        self.s_t = [Trk() for _ in range(2)]
        self.xn = [st.sb([128, D], F32, "xn") for _ in range(2)]
        self.xn_t = [Trk() for _ in range(2)]
        self.xnew = [st.sb([128, D], F32, "xnew") for _ in range(2)]
        self.xnew_t = [Trk() for _ in range(2)]
        self.xnew_ds = [P.dsem() for _ in range(2)]
        self.stat = [st.sb([128, 16], F32, "stat") for _ in range(2)]
        self.stat_t = [Trk() for _ in range(2)]
        self.xTo = [st.sb([128, DC, 512], BF16, "xTo") for _ in range(2)]
        self.xTo_t = [Trk() for _ in range(2)]
        self.xTo_ds = [P.dsem() for _ in range(2)]
        self.tp = [st.ps([128, D], F32, "tp") for _ in range(1)]
        self.tp_t = [Trk() for _ in range(1)]
        self.cnt = 0
        self.tile_cnt = 0

    def load_xold(self, ti):
        P = self.cx.P
        t0, W = self.cx.cfg.tiles[ti]
        slot = ti % len(self.xold)
        src = self.xres_in[t0:t0 + W, :].rearrange("(s p) d -> p s d", p=128)
        P.dma("sp", [(self.xold[slot][:, 0:W // 128, :], src)], self.xold_ds[slot],
              reads=[self.cx.trk(self.xres_in_name, (ti, s)) for s in range(W // 128)], writes=[self.xold_t[slot]])

    def sub_a(self, ti, s, y_ps, y_t):
        cx, P = self.cx, self.cx.P
        t0, W = cx.cfg.tiles[ti]
        slot = ti % len(self.xold)
        i = self.cnt % 2
        self.cnt += 1
        xold = self.xold[slot][:, s, :]
        if y_ps is not None:
            xnew = self.xnew[i]
            s_ = self.s_[i]
            P.op("dve", "scalar_tensor_tensor", dict(out=s_[:], in0=xold, scalar=float(ALPHA), in1=self.bb[:],
                                                         op0=ALU.mult, op1=ALU.add),
                 reads=[self.xold_t[slot], self.bb_t], writes=[self.s_t[i]])
            for h in range(2):
                P.op("dve", "tensor_tensor", dict(out=s_[:, h * 512:(h + 1) * 512], in0=y_ps[:, h * 512:(h + 1) * 512],
                                                          in1=s_[:, h * 512:(h + 1) * 512], op=ALU.add),
                     reads=[y_t], writes=[self.s_t[i]], signal=(h == 1))
            stt = self.stat[i]
            for h in range(2):
                P.op("dve", "bn_stats", dict(out=stt[:, h * 6:(h + 1) * 6], in_=s_[:, h * 512:(h + 1) * 512]),
                     reads=[self.s_t[i]], writes=[self.stat_t[i]], signal=False)
            P.op("dve", "bn_aggr", dict(out=stt[:, 12:14], in_=stt[:, 0:12]), reads=[self.stat_t[i]], writes=[self.stat_t[i]])
            P.op("pool", "tensor_scalar", dict(out=stt[:, 14:15], in0=stt[:, 13:14], scalar1=float(LN_EPS), scalar2=None,
                                                   op0=ALU.add), reads=[self.stat_t[i]], writes=[self.stat_t[i]], signal=False)
            P.op("pool", "tensor_tensor", dict(out=stt[:, 14:15], in0=stt[:, 14:15], in1=self.mh[:], op=ALU.pow),
                 reads=[self.stat_t[i], self.mh_t], writes=[self.stat_t[i]])
            xn = self.xn[i]
            P.op("dve", "tensor_scalar", dict(out=xn[:], in0=s_[:], scalar1=stt[:, 12:13], scalar2=stt[:, 14:15],
                                                  op0=ALU.subtract, op1=ALU.mult),
                 reads=[self.s_t[i], self.stat_t[i]], writes=[self.xn_t[i]])
            P.op("pool", "tensor_tensor", dict(out=xn[:], in0=xn[:], in1=self.gam[:], op=ALU.mult),
                 reads=[self.xn_t[i], self.gam_t], writes=[self.xn_t[i]])
            P.op("pool", "tensor_tensor", dict(out=xnew[:], in0=xn[:], in1=self.bet[:], op=ALU.add),
                 reads=[self.xn_t[i], self.bet_t], writes=[self.xnew_t[i]])
            src_for_T = xnew
            src_t = self.xnew_t[i]
            if self.final_out is None:
                dst = self.xres_out[t0 + s * 128:t0 + (s + 1) * 128, :]
                P.dma("sp", [(dst, xnew[:])], self.xnew_ds[i], reads=[self.xnew_t[i]],
                      writes=[cx.trk(self.xres_out_name, (ti, s))])
            else:
                if t0 >= HALO:
                    o0 = t0 - HALO + s * 128
                    dst = self.xres_out[o0:o0 + 128, :]
                    P.dma("sp", [(dst, xnew[:])], self.xnew_ds[i], reads=[self.xnew_t[i]],
                          writes=[cx.trk(self.xres_out_name, (ti, s))])
            src_ap = xnew
        else:
            src_ap = None
            src_t = self.xold_t[slot]
        return (ti, s, src_ap, src_t, slot)

    def sub_b(self, state):
        cx, P = self.cx, self.cx.P
        ti, s, src_ap, src_t, slot = state
        t0, W = cx.cfg.tiles[ti]
        if self.xT_out is None:
            return
        j = self.tile_cnt % 2
        tp = self.tp[0]
        for k in range(DC):
            if src_ap is not None:
                inp = src_ap[:, k * 128:(k + 1) * 128]
            else:
                inp = self.xold[slot][:, s, k * 128:(k + 1) * 128]
            P.op("pe", "transpose", dict(out=tp[:, k * 128:(k + 1) * 128], in_=inp, identity=self.ident[:]),
                 reads=[src_t, self.ident_t], writes=[self.tp_t[0]], signal=(k == DC - 1))
        xTo = self.xTo[j]
        P.op("act", "copy", dict(out=xTo[:, :, s * 128:(s + 1) * 128], in_=tp[:].rearrange("p (k t) -> p k t", k=DC)),
             reads=[self.tp_t[0]], writes=[self.xTo_t[j]])
        if s == W // 128 - 1:
            dst = self.xT_out[:, t0:t0 + W].rearrange("(k p) t -> p k t", p=128)
            P.dma("sp", [(dst, xTo[:, :, 0:W])], self.xTo_ds[j], reads=[self.xTo_t[j]],
                  writes=[cx.trk(self.xT_out_name, ti)])
            self.tile_cnt += 1


    def sub(self, ti, s, y_ps, y_t):
        self.sub_b(self.sub_a(ti, s, y_ps, y_t))


def stage_prep(cx, x_in, xT_out):
    cfg = cx.cfg
    with Stage(cx, "prep") as st:
        ep = Epi.__new__(Epi)
        P = cx.P
        ep.cx, ep.st = cx, st
        ep.xres_in = cx.dt(x_in, [cfg.nt, D], F32)
        ep.xres_in_name = x_in
        ep.final_out = None
        ep.xT_out = cx.dt(xT_out, [D, cfg.nt], BF16)
        ep.xT_out_name = xT_out
        (ep.ident,), ep.ident_t = load_consts(cx, st, [("ident_f32", [128, 128])])
        ep.xold = [st.sb([128, 4, D], F32, "xold") for _ in range(2)]
        ep.xold_t = [Trk() for _ in range(2)]
        ep.xold_ds = [P.dsem() for _ in range(2)]
        ep.xTo = [st.sb([128, DC, 512], BF16, "xTo") for _ in range(2)]
        ep.xTo_t = [Trk() for _ in range(2)]
        ep.xTo_ds = [P.dsem() for _ in range(2)]
        ep.tp = [st.ps([128, D], F32, "tp")]
        ep.tp_t = [Trk()]
        ep.cnt = 0
        ep.tile_cnt = 0
        for ti, (t0, W) in enumerate(cfg.tiles):
            ep.load_xold(ti)
            for s in range(W // 128):
                ep.sub(ti, s, None, None)


def load_xT_resident(cx, st, xT_in):
    cfg = cx.cfg
    P = cx.P
    src = cx.dt(xT_in, [D, cfg.nt], BF16)
    xT = []
    trks = []
    for ti, (t0, W) in enumerate(cfg.tiles):
        ds = P.dsem()
        trk = Trk()
        xt = st.sb([128, DC, W], BF16, "xTres")
        P.dma("sp", [(xt[:], src[:, t0:t0 + W].rearrange("(k p) t -> p k t", p=128))], ds,
              reads=[cx.trk(xT_in, ti)], writes=[trk])
        trks.append(trk)
        xT.append(xt)
    return xT, trks


def stage_f1(cx, li, xT_in, uT_out):
    cfg = cx.cfg
    P = cx.P
    with Stage(cx, "f1_%d" % li) as st:
        uT = cx.dt(uT_out, [FF, cfg.nt], BF16)
        (cols, hm), cols_t = load_consts(cx, st, [("f%d_cols" % li, [128, 2 * FC, 5]), ("hm", [128, 1])])
        hm_t = cols_t
        xT, xT_t = load_xT_resident(cx, st, xT_in)
        wsrc = cx.dt("f%d_wup" % li, [2 * FC, 128, DC, 128], F32)
        NW = 4
        wb = [st.sb([128, DC, 128], BF16, "w") for _ in range(NW)]
        wb_t = [Trk() for _ in range(NW)]
        wst = [st.sb([128, DC, 128], F32, "wst") for _ in range(NW)]
        wst_t = [Trk() for _ in range(NW)]
        wb_ds = [P.dsem() for _ in range(NW)]
        gps = [st.ps([128, 512], F32, "g") for _ in range(2)]
        gps_t = [Trk() for _ in range(2)]
        vps = [st.ps([128, 512], F32, "v") for _ in range(2)]
        vps_t = [Trk() for _ in range(2)]
        hg = [st.sb([128, 514], F32, "hg") for _ in range(2)]
        hg_t = [Trk() for _ in range(2)]
        hv = [st.sb([128, 514], F32, "hv") for _ in range(2)]
        hv_t = [Trk() for _ in range(2)]
        cg = [st.sb([128, 512], F32, "cg") for _ in range(2)]
        cg_t = [Trk() for _ in range(2)]
        cv = [st.sb([128, 512], F32, "cv") for _ in range(2)]
        cv_t = [Trk() for _ in range(2)]
        sg = [st.sb([128, 512], F32, "sg") for _ in range(2)]
        sg_t = [Trk() for _ in range(2)]
        ub = [st.sb([128, 512], BF16, "u") for _ in range(2)]
        ub_t = [Trk() for _ in range(2)]
        ub_ds = [P.dsem() for _ in range(2)]
        wcnt = 0

        def load_w(j):
            nonlocal wcnt
            slot = wcnt % NW
            wcnt += 1
            P.dma("sp", [(wst[slot][:], wsrc[j])], wb_ds[slot], writes=[wst_t[slot]])
            P.op("pool", "tensor_copy", dict(out=wb[slot][:], in_=wst[slot][:]), reads=[wst_t[slot]], writes=[wb_t[slot]])
            return slot

        pending = [(load_w(0), load_w(FC))]
        it = 0
        for f in range(FC):
            if f + 1 < FC:
                pending.append((load_w(f + 1), load_w(FC + f + 1)))
            sg_slot, sv_slot = pending.pop(0)
            for ti, (t0, W) in enumerate(cfg.tiles):
                i = it % 2
                it += 1
                for (slot, pst, pstt, jj) in ((sg_slot, gps[i], gps_t[i], f), (sv_slot, vps[i], vps_t[i], FC + f)):
                    for k in range(DC):
                        P.op("pe", "matmul", dict(
                            out=pst[:, 0:W], lhsT=wb[slot][:, k, :], rhs=xT[ti][:, k, 0:W], start=(k == 0), stop=(k == DC - 1)),
                            reads=[wb_t[slot], xT_t[ti]], writes=[pstt], signal=(k == DC - 1))
                for (hb, hbt, pst, pstt, jj) in ((hg, hg_t, gps[i], gps_t[i], f), (hv, hv_t, vps[i], vps_t[i], FC + f)):
                    h = hb[i]
                    bcol = cols[:, jj, 0:1]
                    if ti == 0:
                        P.op("pool", "memset", dict(ap=h[:, 0:2], constant=0.0), writes=[hbt[i]])
                        P.op("dve", "tensor_scalar", dict(
                            out=h[:, 2:2 + W], in0=pst[:, 0:W], scalar1=bcol, scalar2=hm[:, 0:1], op0=ALU.add, op1=ALU.mult),
                            reads=[pstt, cols_t, hm_t], writes=[hbt[i]])
                    else:
                        hp = hb[1 - i]
                        Wp = cfg.tiles[ti - 1][1]
                        P.op("pool", "tensor_copy", dict(out=h[:, 0:2], in_=hp[:, Wp:Wp + 2]),
                             reads=[hbt[1 - i]], writes=[hbt[i]])
                        P.op("act", "activation", dict(
                            out=h[:, 2:2 + W], in_=pst[:, 0:W], func=AF.Identity, bias=bcol, scale=1.0),
                            reads=[pstt, cols_t], writes=[hbt[i]])
                for (hb, hbt, cb, cbt, jj) in ((hg, hg_t, cg, cg_t, f), (hv, hv_t, cv, cv_t, FC + f)):
                    h = hb[i]
                    c = cb[i]
                    P.op("act", "activation", dict(
                        out=c[:, 0:W], in_=h[:, 2:2 + W], func=AF.Identity, bias=cols[:, jj, 4:5], scale=cols[:, jj, 3:4]),
                        reads=[hbt[i], cols_t], writes=[cbt[i]])
                    P.op("dve", "scalar_tensor_tensor", dict(
                        out=c[:, 0:W], in0=h[:, 1:1 + W], scalar=cols[:, jj, 2:3], in1=c[:, 0:W], op0=ALU.mult, op1=ALU.add),
                        reads=[hbt[i], cbt[i]], writes=[cbt[i]], signal=False)
                    P.op("dve", "scalar_tensor_tensor", dict(
                        out=c[:, 0:W], in0=h[:, 0:W], scalar=cols[:, jj, 1:2], in1=c[:, 0:W], op0=ALU.mult, op1=ALU.add),
                        reads=[hbt[i], cbt[i]], writes=[cbt[i]])
                P.op("act", "activation", dict(out=sg[i][:, 0:W], in_=cg[i][:, 0:W], func=AF.Silu),
                     reads=[cg_t[i]], writes=[sg_t[i]])
                P.op("dve", "tensor_tensor", dict(out=ub[i][:, 0:W], in0=sg[i][:, 0:W], in1=cv[i][:, 0:W], op=ALU.mult),
                     reads=[sg_t[i], cv_t[i]], writes=[ub_t[i]])
                P.dma("sp", [(uT[f * 128:(f + 1) * 128, t0:t0 + W], ub[i][:, 0:W])], ub_ds[i], reads=[ub_t[i]],
                      writes=[cx.trk(uT_out, (f, ti))])


def load_w_rows(cx, st, wd, wd_t, wsrc, KC, width=D):
    P = cx.P
    stg = [st.sb([128, 2048], F32, "wstg") for _ in range(2)]
    stg_t = [Trk() for _ in range(2)]
    stg_ds = [P.dsem() for _ in range(2)]
    n = 0
    if width <= 1024:
        pieces = [(k0, min(KC, k0 + 2048 // width), 0, width) for k0 in range(0, KC, 2048 // width)]
    else:
        pieces = [(k, k + 1, w0, min(width, w0 + 2048)) for k in range(KC) for w0 in range(0, width, 2048)]
    for (k0, k1, w0, w1) in pieces:
        i = n % 2
        n += 1
        nk, nw = k1 - k0, w1 - w0
        sview = stg[i][:, 0:nk * nw].rearrange("p (k w) -> p k w", k=nk)
        P.dma("sp", [(sview, wsrc[:, k0:k1, w0:w1])], stg_ds[i], writes=[stg_t[i]])
        if i == 0:
            P.op("pool", "tensor_copy", dict(out=wd[:, k0:k1, w0:w1], in_=sview), reads=[stg_t[i]], writes=[wd_t])
        else:
            P.op("act", "copy", dict(out=wd[:, k0:k1, w0:w1], in_=sview), reads=[stg_t[i]], writes=[wd_t])


def stage_proj(cx, name, prefix, li, aT_in, KC, w_name, xres_in, xres_out, xT_out, final_out=None):
    cfg = cx.cfg
    P = cx.P
    with Stage(cx, name) as st:
        aT = cx.dt(aT_in, [KC * 128, cfg.nt], BF16)
        wsrc = cx.dt(w_name, [128, KC, D], F32)
        wd = st.sb([128, KC, D], BF16, "wd")
        wd_t = Trk()
        load_w_rows(cx, st, wd, wd_t, wsrc, KC)
        ep = Epi(cx, st, prefix, li, xres_in, xres_out, xT_out, final_out=final_out)
        NA = 2
        ab = [st.sb([128, KC, 512], BF16, "a") for _ in range(NA)]
        ab_t = [Trk() for _ in range(NA)]
        ab_ds = [P.dsem() for _ in range(NA)]
        yps = [st.ps([128, D], F32, "y") for _ in range(2)]
        yps_t = [Trk() for _ in range(2)]
        tiles = list(enumerate(cfg.tiles))
        if final_out is not None:
            tiles = tiles[1:]

        def load_a(ti):
            t0, W = cfg.tiles[ti]
            slot = ti % NA
            reads = [cx.trk(aT_in, (k, ti)) for k in range(KC)]
            P.dma("sp", [(ab[slot][:, :, 0:W], aT[:, t0:t0 + W].rearrange("(k p) t -> p k t", p=128))], ab_ds[slot],
                  reads=reads, writes=[ab_t[slot]])
            ep.load_xold(ti)
        load_a(tiles[0][0])
        cnt = 0
        pend = None
        for n, (ti, (t0, W)) in enumerate(tiles):
            if n + 1 < len(tiles):
                load_a(tiles[n + 1][0])
            slot = ti % NA
            for s in range(W // 128):
                i = cnt % 2
                cnt += 1
                for h in range(2):
                    for k in range(KC):
                        P.op("pe", "matmul", dict(
                            out=yps[i][:, h * 512:(h + 1) * 512], lhsT=ab[slot][:, k, s * 128:(s + 1) * 128],
                            rhs=wd[:, k, h * 512:(h + 1) * 512], start=(k == 0), stop=(k == KC - 1)),
                            reads=[ab_t[slot], wd_t], writes=[yps_t[i]], signal=(k == KC - 1))
                st_a = ep.sub_a(ti, s, yps[i], yps_t[i])
                if pend is not None:
                    ep.sub_b(pend)
                pend = st_a
        if pend is not None:
            ep.sub_b(pend)


def build_program(cfg, stages, ext_in, ext_out, internal=(), want_names=False):
    nc = bass.Bass("TRN2", target_bir_lowering=False)
    es = contextlib.ExitStack()
    with es:
        cx = Ctx(nc, es, cfg, None if ext_in is None else set(ext_in), set(ext_out))
        cx.internal = set(internal)
        for fn in stages:
            fn(cx)
        cx.P.finish()
        ninst = cx.P.ninst
    if want_names:
        return nc, ninst, [k for k, v in cx.kinds.items() if v == "ExternalInput"]
    return nc, ninst


def lay_cols(v):
    return np.ascontiguousarray(v.reshape(-1, 128).T)


def lay_bcast(v):
    return np.ascontiguousarray(np.broadcast_to(v[None, :], (128, v.shape[0])))


def lay_w_chunks(w):
    K, N = w.shape
    return np.ascontiguousarray(w.reshape(K // 128, 128, N // 128, 128).transpose(2, 1, 0, 3))


def lay_w_rows(w):
    K, N = w.shape
    return np.ascontiguousarray(w.reshape(K // 128, 128, N).transpose(1, 0, 2))


def host_consts(inp):
    c = {}
    c["ident_f32"] = np.eye(128, dtype=np.float32)
    for li in range(DEPTH):
        bup = inp["f_b_up"][li]
        wdw = inp["f_w_dw"][li]
        bdw = inp["f_b_dw"][li]
        cols = np.stack([lay_cols(bup), lay_cols(wdw[0]), lay_cols(wdw[1]), lay_cols(wdw[2]), lay_cols(bdw)], axis=-1)
        c["f%d_cols" % li] = np.ascontiguousarray(cols.astype(np.float32))
        c["f%d_wup" % li] = lay_w_chunks(inp["f_w_up"][li])
        c["f%d_wdn" % li] = lay_w_rows(inp["f_w_down"][li])
        c["f%d_bb" % li] = lay_bcast(inp["f_b_down"][li])
        c["f%d_g" % li] = lay_bcast(inp["ln_ffn_g"][li])
        c["f%d_b" % li] = lay_bcast(inp["ln_ffn_b"][li])
    for j in range(inp["a_w_in"].shape[0]):
        bi = inp["a_b_in"][j]
        wdw = inp["a_w_dw"][j]
        cols = np.concatenate([lay_cols(bi[:D])[:, :, None], lay_cols(bi[D:])[:, :, None], lay_cols(inp["a_b_dw"][j])[:, :, None],
                               np.stack([lay_cols(wdw[k]) for k in range(CONVK)], axis=-1)], axis=-1)
        c["a%d_cols" % j] = np.ascontiguousarray(cols.astype(np.float32))
        c["a%d_win" % j] = lay_w_chunks(inp["a_w_in"][j])
        c["a%d_wout" % j] = lay_w_rows(inp["a_w_out"][j])
        c["a%d_lcols" % j] = np.ascontiguousarray(np.stack([lay_cols(inp["a_ln_g"][j]), lay_cols(inp["a_ln_b"][j])], axis=-1))
    c["ident_bf16"] = np.eye(128, dtype=np.float32).astype(BF16_NP)
    wqkv = inp["b_w_qkv"][0]
    c["b0_wqk"] = lay_w_chunks(wqkv[:, :2 * D])
    c["b0_wv"] = lay_w_rows(wqkv[:, 2 * D:])
    c["b0_wo"] = lay_w_rows(inp["b_w_o"][0])
    lam = np.stack([inp["b_lq1"][0], inp["b_lk1"][0], inp["b_lq2"][0], inp["b_lk2"][0]], axis=0)
    c["b0_lam"] = np.ascontiguousarray(np.broadcast_to(lam[None], (128, 4, 64))).astype(np.float32)
    c["b0_gsub"] = lay_bcast(inp["b_subln_g"][0])
    win = inp["c_w_in"][0]
    c["c0_wu"] = lay_w_chunks(win[:, :GH])
    c["c0_wv"] = lay_w_rows(win[:, GH:])
    c["c0_wo"] = lay_w_rows(inp["c_w_out"][0])
    c["c0_ucols"] = lay_cols(inp["c_b_in"][0][:GH])
    c["c0_bvb"] = lay_bcast(inp["c_b_in"][0][GH:])
    c["c0_gcols"] = np.ascontiguousarray(np.stack([lay_cols(inp["c_ln_g"][0]), lay_cols(inp["c_ln_b"][0])], axis=-1))
    c["c0_wsT"] = np.ascontiguousarray(inp["c_w_s"][0].transpose(2, 0, 1))
    c["c0_trim"] = (np.arange(128)[None, :] >= np.arange(128)[:, None]).astype(np.float32)
    c["c0_bsb"] = np.ascontiguousarray(np.broadcast_to(inp["c_b_s"][0][None], (128, 4, 128))).astype(np.float32)
    pp = np.arange(128)[:, None, None]
    mm = np.arange(4)[None, :, None]
    cc = np.arange(512)[None, None, :]
    c["trimask"] = (cc >= pp + 128 * mm).astype(np.float32).astype(BF16_NP)
    for li in range(DEPTH):
        kind, j = li % 3, li // 3
        bo = [inp["a_b_out"], None, inp["c_b_out"]][kind]
        c["m%d_bb" % li] = lay_bcast(bo[j]) if bo is not None else np.zeros((128, D), np.float32)
        c["m%d_g" % li] = lay_bcast(inp["ln_mix_g"][li])
        c["m%d_b" % li] = lay_bcast(inp["ln_mix_b"][li])
    return c


class ChunkLoader:
    def __init__(self, cx, st, wsrc, nslots=4, cast_eng="pool"):
        P = cx.P
        self.cx, self.wsrc = cx, wsrc
        self.n = nslots
        self.wb = [st.sb([128, DC, 128], BF16, "w") for _ in range(nslots)]
        self.wb_t = [Trk() for _ in range(nslots)]
        self.wst = [st.sb([128, DC, 128], F32, "wst") for _ in range(nslots)]
        self.wst_t = [Trk() for _ in range(nslots)]
        self.ds = [P.dsem() for _ in range(nslots)]
        self.cnt = 0
        self.cast_eng = cast_eng

    def load(self, j):
        P = self.cx.P
        slot = self.cnt % self.n
        self.cnt += 1
        P.dma("sp", [(self.wst[slot][:], self.wsrc[j])], self.ds[slot], writes=[self.wst_t[slot]])
        if self.cast_eng == "pool":
            P.op("pool", "tensor_copy", dict(out=self.wb[slot][:], in_=self.wst[slot][:]),
                 reads=[self.wst_t[slot]], writes=[self.wb_t[slot]])
        else:
            P.op("act", "copy", dict(out=self.wb[slot][:], in_=self.wst[slot][:]),
                 reads=[self.wst_t[slot]], writes=[self.wb_t[slot]])
        return slot


def stage_c1(cx, j_idx, xT_in, cv_out):
    cfg = cx.cfg
    P = cx.P
    with Stage(cx, "c1_%d" % j_idx) as st:
        cv = cx.dt(cv_out, [cfg.nt, D], F32)
        (cols, hm, ident), cols_t = load_consts(cx, st, [("a%d_cols" % j_idx, [128, DC, 34]), ("hm", [128, 1]),
                                                         ("ident_f32", [128, 128])])
        hb = st.sb([128, DC, 1], F32, "hb")
        wdh = st.sb([128, DC, CONVK], F32, "wdh")
        d_t = Trk()
        P.op("pool", "tensor_scalar", dict(out=hb[:], in0=cols[:, :, 1:2], scalar1=0.5, scalar2=None, op0=ALU.mult),
             reads=[cols_t], writes=[d_t])
        d2_t = Trk()
        P.op("pool", "tensor_scalar", dict(out=wdh[:], in0=cols[:, :, 3:34], scalar1=0.5, scalar2=None, op0=ALU.mult),
             reads=[cols_t], writes=[d2_t])
        xT, xT_t = load_xT_resident(cx, st, xT_in)
        wl = ChunkLoader(cx, st, cx.dt("a%d_win" % j_idx, [2 * DC, 128, DC, 128], F32))
        aps = [st.ps([128, 512], F32, "a") for _ in range(2)]
        aps_t = [Trk() for _ in range(2)]
        gps = [st.ps([128, 512], F32, "g") for _ in range(2)]
        gps_t = [Trk() for _ in range(2)]
        tps = [st.ps([128, 512], F32, "tp") for _ in range(2)]
        tps_t = [Trk() for _ in range(2)]
        th = [st.sb([128, 512], F32, "th") for _ in range(2)]
        th_t = [Trk() for _ in range(2)]
        asb = [st.sb([128, 512], F32, "asb") for _ in range(2)]
        asb_t = [Trk() for _ in range(2)]
        hbuf = [st.sb([128, 30 + 512], F32, "h") for _ in range(2)]
        hbuf_t = [Trk() for _ in range(2)]
        acc1 = [st.sb([128, 512], F32, "acc1") for _ in range(2)]
        acc1_t = [Trk() for _ in range(2)]
        acc2 = [st.sb([128, 512], F32, "acc2") for _ in range(2)]
        acc2_t = [Trk() for _ in range(2)]
        ct = [st.sb([128, 4, 128], F32, "ct") for _ in range(2)]
        ct_t = [Trk() for _ in range(2)]
        ct_ds = [P.dsem() for _ in range(2)]
        NPE = 16
        cps = [st.ps([128, 512], F32, "cps") for _ in range(2)]
        cps_t = [Trk() for _ in range(2)]
        h16 = [st.sb([128, 30 + 512], BF16, "h16") for _ in range(2)]
        h16_t = [Trk() for _ in range(2)]
        dg = [st.sb([128, NPE, 128], BF16, "dg") for _ in range(2)]
        dg_t = [Trk() for _ in range(2)]
        pending = [(wl.load(0), wl.load(DC))]
        it = 0
        for j in range(DC):
            if j + 1 < DC:
                pending.append((wl.load(j + 1), wl.load(DC + j + 1)))
            sa, sg = pending.pop(0)
            dgj, dgj_t = dg[j % 2], dg_t[j % 2]
            for k in range(NPE):
                P.op("pool", "tensor_scalar", dict(out=dgj[:, k, :], in0=ident[:], scalar1=wdh[:, j, k:k + 1], scalar2=None,
                                                   op0=ALU.mult), reads=[cols_t, d2_t], writes=[dgj_t])
            for ti, (t0, W) in enumerate(cfg.tiles):
                i = it % 2
                it += 1
                for (slot, pst, pstt) in ((sa, aps[i], aps_t[i]), (sg, gps[i], gps_t[i])):
                    for k in range(DC):
                        P.op("pe", "matmul", dict(out=pst[:, 0:W], lhsT=wl.wb[slot][:, k, :], rhs=xT[ti][:, k, 0:W],
                                                  start=(k == 0), stop=(k == DC - 1)),
                             reads=[wl.wb_t[slot], xT_t[ti]], writes=[pstt], signal=(k == DC - 1))
                P.op("act", "activation", dict(out=th[i][:, 0:W], in_=gps[i][:, 0:W], func=AF.Tanh, bias=hb[:, j, :], scale=0.5),
                     reads=[gps_t[i], d_t], writes=[th_t[i]])
                P.op("act", "activation", dict(out=asb[i][:, 0:W], in_=aps[i][:, 0:W], func=AF.Identity, bias=cols[:, j, 0:1],
                                               scale=1.0), reads=[aps_t[i], cols_t], writes=[asb_t[i]])
                h = hbuf[i]
                if ti == 0:
                    P.op("pool", "memset", dict(ap=h[:, 0:30], constant=0.0), writes=[hbuf_t[i]])
                else:
                    Wp = cfg.tiles[ti - 1][1]
                    P.op("pool", "tensor_copy", dict(out=h[:, 0:30], in_=hbuf[1 - i][:, Wp:Wp + 30]),
                         reads=[hbuf_t[1 - i]], writes=[hbuf_t[i]])
                P.op("dve", "scalar_tensor_tensor", dict(out=h[:, 30:30 + W], in0=th[i][:, 0:W], scalar=1.0, in1=asb[i][:, 0:W],
                                                         op0=ALU.add, op1=ALU.mult),
                     reads=[th_t[i], asb_t[i]], writes=[hbuf_t[i]])
                if ti == 0:
                    P.op("dve", "tensor_scalar", dict(out=h[:, 30:30 + W], in0=h[:, 30:30 + W], scalar1=hm[:, 0:1], scalar2=None,
                                                      op0=ALU.mult), reads=[cols_t], writes=[hbuf_t[i]])
                P.op("act", "copy", dict(out=h16[i][:, 0:30 + W], in_=h[:, 0:30 + W]), reads=[hbuf_t[i]], writes=[h16_t[i]])
                for k in range(NPE):
                    P.op("pe", "matmul", dict(out=cps[i][:, 0:W], lhsT=dgj[:, k, :], rhs=h16[i][:, k:k + W],
                                              start=(k == 0), stop=(k == NPE - 1)),
                         reads=[dgj_t, h16_t[i]], writes=[cps_t[i]], signal=(k == NPE - 1))
                a1, a2 = acc1[i], acc2[i]
                P.op("dve", "tensor_scalar", dict(out=a1[:, 0:W], in0=h[:, 30:30 + W], scalar1=wdh[:, j, 30:31],
                                                  scalar2=cols[:, j, 2:3], op0=ALU.mult, op1=ALU.add),
                     reads=[hbuf_t[i], d2_t, cols_t], writes=[acc1_t[i]])
                P.op("dve", "tensor_scalar", dict(out=a2[:, 0:W], in0=h[:, 29:29 + W], scalar1=wdh[:, j, 29:30], scalar2=None,
                                                  op0=ALU.mult), reads=[hbuf_t[i], d2_t], writes=[acc2_t[i]])
                for k in range(28, NPE - 1, -1):
                    a, at = (a1, acc1_t[i]) if k % 2 == 0 else (a2, acc2_t[i])
                    P.op("dve", "scalar_tensor_tensor", dict(out=a[:, 0:W], in0=h[:, k:k + W], scalar=wdh[:, j, k:k + 1],
                                                             in1=a[:, 0:W], op0=ALU.mult, op1=ALU.add),
                         reads=[hbuf_t[i], at], writes=[at])
                P.op("dve", "tensor_tensor", dict(out=a1[:, 0:W], in0=a1[:, 0:W], in1=a2[:, 0:W], op=ALU.add),
                     reads=[acc1_t[i], acc2_t[i]], writes=[acc1_t[i]])
                P.op("dve", "tensor_tensor", dict(out=a1[:, 0:W], in0=cps[i][:, 0:W], in1=a1[:, 0:W], op=ALU.add),
                     reads=[acc1_t[i], cps_t[i]], writes=[acc1_t[i]])
                ns = W // 128
                for s in range(ns):
                    P.op("pe", "transpose", dict(out=tps[i][:, s * 128:(s + 1) * 128], in_=a1[:, s * 128:(s + 1) * 128],
                                                 identity=ident[:]),
                         reads=[acc1_t[i], cols_t], writes=[tps_t[i]], signal=(s == ns - 1))
                P.op("act", "copy", dict(out=ct[i][:, 0:ns, :], in_=tps[i][:, 0:W].rearrange("p (s c) -> p s c", s=ns)),
                     reads=[tps_t[i]], writes=[ct_t[i]])
                P.dma("sp", [(cv[t0:t0 + W, j * 128:(j + 1) * 128].rearrange("(s p) c -> p s c", p=128), ct[i][:, 0:ns, :])],
                      ct_ds[i], reads=[ct_t[i]], writes=[cx.trk(cv_out, (ti, j))])


def stage_tm_proj(cx, name, prefix, src_name, src_dtype, prenorm, cols_name, w_name, xres_in, xres_out, xT_out):
    cfg = cx.cfg
    P = cx.P
    with Stage(cx, name) as st:
        src = cx.dt(src_name, [cfg.nt, D], src_dtype)
        wsrc = cx.dt(w_name, [128, DC, D], F32)
        wd = st.sb([128, DC, D], BF16, "wd")
        wd_t = Trk()
        load_w_rows(cx, st, wd, wd_t, wsrc, DC)
        ep = Epi(cx, st, prefix, 0, xres_in, xres_out, xT_out)
        ds = P.dsem()
        identb = st.sb([128, 128], BF16, "identb")
        identb_t = Trk()
        pairs = [(identb[:], cx.dt("ident_bf16", [128, 128], BF16))]
        if prenorm:
            lcols = st.sb([128, DC, 2], F32, "lcols")
            pairs.append((lcols[:], cx.dt(cols_name, [128, DC, 2], F32)))
        P.dma("sp", pairs, ds, writes=[identb_t])
        NA = 2
        ib = [st.sb([128, 4, D], src_dtype, "in") for _ in range(NA)]
        ib_t = [Trk() for _ in range(NA)]
        ib_ds = [P.dsem() for _ in range(NA)]
        xb = [st.sb([128, D], BF16, "xb") for _ in range(2)]
        xb_t = [Trk() for _ in range(2)]
        stat = [st.sb([128, 16], F32, "pstat") for _ in range(2)]
        stat_t = [Trk() for _ in range(2)]
        zT = [st.sb([128, DC, 128], BF16, "zT") for _ in range(2)]
        zT_t = [Trk() for _ in range(2)]
        tpb = [st.ps([128, D], BF16, "tpb") for _ in range(2)]
        tpb_t = [Trk() for _ in range(2)]
        yps = [st.ps([128, D], F32, "y") for _ in range(2)]
        yps_t = [Trk() for _ in range(2)]

        def load_in(ti):
            t0, W = cfg.tiles[ti]
            slot = ti % NA
            reads = [cx.trk(src_name, (ti, j)) for j in range(DC)]
            P.dma("sp", [(ib[slot][:, 0:W // 128, :], src[t0:t0 + W, :].rearrange("(s p) d -> p s d", p=128))], ib_ds[slot],
                  reads=reads, writes=[ib_t[slot]])
            ep.load_xold(ti)
        subs = [(ti, s) for ti, (t0, W) in enumerate(cfg.tiles) for s in range(W // 128)]
        loaded = set()

        def ensure_loaded(ti):
            if ti < len(cfg.tiles) and ti not in loaded:
                load_in(ti)
                loaded.add(ti)

        def phase_a(n):
            ti, s = subs[n]
            slot = ti % NA
            i = n % 2
            ensure_loaded(ti)
            if s == 1:
                ensure_loaded(ti + 1)
            xin = ib[slot][:, s, :]
            if prenorm:
                stt = stat[i]
                for h in range(2):
                    P.op("dve", "bn_stats", dict(out=stt[:, h * 6:(h + 1) * 6], in_=xin[:, h * 512:(h + 1) * 512]),
                         reads=[ib_t[slot]], writes=[stat_t[i]])
                P.op("dve", "bn_aggr", dict(out=stt[:, 12:14], in_=stt[:, 0:12]), reads=[stat_t[i]], writes=[stat_t[i]])
                P.op("pool", "tensor_scalar", dict(out=stt[:, 14:15], in0=stt[:, 13:14], scalar1=float(LN_EPS), scalar2=None,
                                                   op0=ALU.add), reads=[stat_t[i]], writes=[stat_t[i]])
                P.op("pool", "tensor_tensor", dict(out=stt[:, 14:15], in0=stt[:, 14:15], in1=ep.mh[:], op=ALU.pow),
                     reads=[stat_t[i], ep.mh_t], writes=[stat_t[i]])
                P.op("dve", "tensor_scalar", dict(out=xb[i][:], in0=xin, scalar1=stt[:, 12:13], scalar2=stt[:, 14:15],
                                                  op0=ALU.subtract, op1=ALU.mult),
                     reads=[ib_t[slot], stat_t[i]], writes=[xb_t[i]])
                tin_t = xb_t[i]
                tin_ap = lambda k: xb[i][:, k * 128:(k + 1) * 128]
            else:
                tin_t = ib_t[slot]
                tin_ap = lambda k: ib[slot][:, s, k * 128:(k + 1) * 128]
            for k in range(DC):
                P.op("pe", "transpose", dict(out=tpb[i][:, k * 128:(k + 1) * 128], in_=tin_ap(k), identity=identb[:]),
                     reads=[tin_t, identb_t], writes=[tpb_t[i]], signal=(k == DC - 1))
            if prenorm:
                for k in range(DC):
                    P.op("act", "activation", dict(out=zT[i][:, k, :], in_=tpb[i][:, k * 128:(k + 1) * 128], func=AF.Silu,
                                                   bias=lcols[:, k, 1:2], scale=lcols[:, k, 0:1]),
                         reads=[tpb_t[i], identb_t], writes=[zT_t[i]])
            else:
                P.op("act", "copy", dict(out=zT[i][:], in_=tpb[i][:].rearrange("p (k t) -> p k t", k=DC)),
                     reads=[tpb_t[i]], writes=[zT_t[i]])

        def phase_b(n):
            ti, s = subs[n]
            i = n % 2
            for h in range(2):
                for k in range(DC):
                    P.op("pe", "matmul", dict(out=yps[i][:, h * 512:(h + 1) * 512], lhsT=zT[i][:, k, :],
                                              rhs=wd[:, k, h * 512:(h + 1) * 512], start=(k == 0), stop=(k == DC - 1)),
                         reads=[zT_t[i], wd_t], writes=[yps_t[i]], signal=(k == DC - 1))
            return ep.sub_a(ti, s, yps[i], yps_t[i])

        phase_a(0)
        pend = None
        for n in range(len(subs)):
            if n + 1 < len(subs):
                phase_a(n + 1)
            st_a = phase_b(n)
            if pend is not None:
                ep.sub_b(pend)
            pend = st_a
        ep.sub_b(pend)


def stage_a1(cx, xT_in, qT_out, kT_out, v_out):
    cfg = cx.cfg
    P = cx.P
    with Stage(cx, "a1") as st:
        qT = cx.dt(qT_out, [D, cfg.nt], BF16)
        kT = cx.dt(kT_out, [D, cfg.nown], BF16)
        v = cx.dt(v_out, [cfg.nown, D], BF16)
        xT, xT_t = load_xT_resident(cx, st, xT_in)
        wl = ChunkLoader(cx, st, cx.dt("b0_wqk", [2 * DC, 128, DC, 128], F32))
        wv = st.sb([128, DC, D], BF16, "wv")
        wv_t = Trk()
        load_w_rows(cx, st, wv, wv_t, cx.dt("b0_wv", [128, DC, D], F32), DC)
        ps = [st.ps([128, 512], F32, "ps") for _ in range(2)]
        ps_t = [Trk() for _ in range(2)]
        ob = [st.sb([128, 512], BF16, "ob") for _ in range(2)]
        ob_t = [Trk() for _ in range(2)]
        ob_ds = [P.dsem() for _ in range(2)]
        vps = [st.ps([128, D], F32, "vps") for _ in range(2)]
        vps_t = [Trk() for _ in range(2)]
        vb = [st.sb([128, D], BF16, "vb") for _ in range(2)]
        vb_t = [Trk() for _ in range(2)]
        vb_ds = [P.dsem() for _ in range(2)]
        pending = [wl.load(0)]
        it = 0
        for j in range(2 * DC):
            if j + 1 < 2 * DC:
                pending.append(wl.load(j + 1))
            slot = pending.pop(0)
            for ti, (t0, W) in enumerate(cfg.tiles):
                if j >= DC and ti == 0:
                    continue
                i = it % 2
                it += 1
                for k in range(DC):
                    P.op("pe", "matmul", dict(out=ps[i][:, 0:W], lhsT=wl.wb[slot][:, k, :], rhs=xT[ti][:, k, 0:W],
                                              start=(k == 0), stop=(k == DC - 1)),
                         reads=[wl.wb_t[slot], xT_t[ti]], writes=[ps_t[i]], signal=(k == DC - 1))
                if it % 2 == 0:
                    P.op("act", "copy", dict(out=ob[i][:, 0:W], in_=ps[i][:, 0:W]), reads=[ps_t[i]], writes=[ob_t[i]])
                else:
                    P.op("dve", "tensor_copy", dict(out=ob[i][:, 0:W], in_=ps[i][:, 0:W]), reads=[ps_t[i]], writes=[ob_t[i]])
                if j < DC:
                    dst = qT[j * 128:(j + 1) * 128, t0:t0 + W]
                    key = (qT_out, (j, ti))
                else:
                    dst = kT[(j - DC) * 128:(j - DC + 1) * 128, t0 - HALO:t0 - HALO + W]
                    key = (kT_out, (j - DC, ti))
                P.dma("sp", [(dst, ob[i][:, 0:W])], ob_ds[i], reads=[ob_t[i]], writes=[cx.trk(*key)])
        cnt = 0
        for ti, (t0, W) in enumerate(cfg.tiles):
            if ti == 0:
                continue
            for s in range(W // 128):
                i = cnt % 2
                cnt += 1
                for h in range(2):
                    for k in range(DC):
                        P.op("pe", "matmul", dict(out=vps[i][:, h * 512:(h + 1) * 512], lhsT=xT[ti][:, k, s * 128:(s + 1) * 128],
                                                  rhs=wv[:, k, h * 512:(h + 1) * 512], start=(k == 0), stop=(k == DC - 1)),
                             reads=[xT_t[ti], wv_t], writes=[vps_t[i]], signal=(k == DC - 1))
                if cnt % 2 == 0:
                    P.op("act", "copy", dict(out=vb[i][:], in_=vps[i][:]), reads=[vps_t[i]], writes=[vb_t[i]])
                else:
                    P.op("dve", "tensor_copy", dict(out=vb[i][:], in_=vps[i][:]), reads=[vps_t[i]], writes=[vb_t[i]])
                o0 = t0 - HALO + s * 128
                P.dma("sp", [(v[o0:o0 + 128, :], vb[i][:])], vb_ds[i], reads=[vb_t[i]], writes=[cx.trk(v_out, (ti, s))])


def stage_a2(cx, li, qT_in, kT_own, kT_past, v_own, v_past, attn_out):
    cfg = cx.cfg
    P = cx.P
    nown = cfg.nown
    NKP = nown // 128
    lam_init = 0.8 - 0.6 * math.exp(-0.3 * li)
    with Stage(cx, "a2") as st:
        qT = cx.dt(qT_in, [D, cfg.nt], BF16)
        kTo = cx.dt(kT_own, [D, nown], BF16)
        vo = cx.dt(v_own, [nown, D], BF16)
        if "kpast_fn" in cx.views:
            kpast_fn, vpast_fn = cx.views["kpast_fn"], cx.views["vpast_fn"]
        else:
            kTp = cx.dt(kT_past, [D, nown], BF16)
            vp = cx.dt(v_past, [nown, D], BF16)
            kpast_fn = lambda h: kTp[h * 128:(h + 1) * 128, :]
            vpast_fn = lambda k0, k1, h: vp[k0 * 128:k1 * 128, h * 128:(h + 1) * 128]
        att = cx.dt(attn_out, [cfg.nt, D], BF16)
        (lamin, gsub_raw, pbias), c_t = load_consts(cx, st, [("b0_lam", [128, 4, 64]), ("b0_gsub", [128, 128]), ("pbias", [128, 1])])
        tri = st.sb([128, 4, 512], BF16, "tri")
        tri_t = Trk()
        P.dma("sp", [(tri[:], cx.dt("trimask", [128, 4, 512], BF16))], P.dsem(), writes=[tri_t])
        mh = st.sb([128, 1], F32, "mh")
        mh_t = Trk()
        P.op("pool", "memset", dict(ap=mh[:], constant=-0.5), writes=[mh_t])
        prod = st.sb([128, 2, 64], F32, "prod")
        sm0 = st.sb([128, 8], F32, "sm0")
        l_t = Trk()
        for c in range(2):
            P.op("dve", "tensor_tensor", dict(out=prod[:, c, :], in0=lamin[:, 2 * c, :], in1=lamin[:, 2 * c + 1, :], op=ALU.mult),
                 reads=[c_t], writes=[l_t])
            P.op("dve", "tensor_reduce", dict(out=sm0[:, c:c + 1], in_=prod[:, c, :], axis=mybir.AxisListType.X, op=ALU.add),
                 reads=[l_t], writes=[l_t])
        P.op("act", "activation", dict(out=sm0[:, 2:4], in_=sm0[:, 0:2], func=AF.Exp), reads=[l_t], writes=[l_t])
        nlam = st.sb([128, 1], F32, "nlam")
        gsub = st.sb([128, 128], F32, "gsub")
        P.op("dve", "tensor_tensor", dict(out=nlam[:], in0=sm0[:, 3:4], in1=sm0[:, 2:3], op=ALU.subtract), reads=[l_t], writes=[l_t])
        P.op("dve", "tensor_scalar", dict(out=nlam[:], in0=nlam[:], scalar1=float(-lam_init), scalar2=None, op0=ALU.add),
             reads=[l_t], writes=[l_t])
        P.op("dve", "tensor_scalar", dict(out=gsub[:], in0=gsub_raw[:], scalar1=float(1.0 - lam_init), scalar2=None, op0=ALU.mult),
             reads=[c_t, l_t], writes=[l_t])
        kp_sb = [st.sb([128, nown], BF16, "kp") for _ in range(2)]
        ko_sb = [st.sb([128, nown], BF16, "ko") for _ in range(2)]
        vp_sb = [st.sb([128, NKP, 129], BF16, "vp") for _ in range(2)]
        vo_sb = [st.sb([128, NKP, 129], BF16, "vo") for _ in range(2)]
        kv_t = [Trk() for _ in range(2)]
        kv_ds = [P.dsem() for _ in range(2)]
        for b in range(2):
            P.op("pool", "memset", dict(ap=vp_sb[b][:, :, 128:129], constant=1.0), writes=[kv_t[b]])
            P.op("pool", "memset", dict(ap=vo_sb[b][:, :, 128:129], constant=1.0), writes=[kv_t[b]])

        def load_head(h):
            b = h % 2
            pairs = [(kp_sb[b][:], kpast_fn(h)), (ko_sb[b][:], kTo[h * 128:(h + 1) * 128, :])]
            for k0 in range(0, NKP, 8):
                k1 = min(NKP, k0 + 8)
                pairs.append((vp_sb[b][:, k0:k1, 0:128], vpast_fn(k0, k1, h).rearrange("(kt p) e -> p kt e", p=128)))
                pairs.append((vo_sb[b][:, k0:k1, 0:128],
                              vo[k0 * 128:k1 * 128, h * 128:(h + 1) * 128].rearrange("(kt p) e -> p kt e", p=128)))
            reads = [cx.trk(kT_own, (h, ti)) for ti in range(len(cfg.tiles))] + \
                    [cx.trk(v_own, (ti, s)) for ti in range(len(cfg.tiles)) for s in range(4)] + [cx.trk(kT_past, 0), cx.trk(v_past, 0)]
            P.dma("sp", pairs, kv_ds[b], reads=reads, writes=[kv_t[b]])
        qb = [st.sb([128, 512], BF16, "qb") for _ in range(2)]
        qb_t = [Trk() for _ in range(2)]
        qb_ds = [P.dsem() for _ in range(2)]
        sps = [st.ps([128, 512], F32, "sps") for _ in range(4)]
        sps_t = [Trk() for _ in range(4)]
        acc = [st.ps([128, 512], F32, "acc") for _ in range(4)]
        acc_t = [Trk() for _ in range(4)]
        NF = 2
        ob = [st.sb([128, 128], F32, "o") for _ in range(NF)]
        junk = [st.sb([128, 128], F32, "junk") for _ in range(NF)]
        sm = [st.sb([128, 8], F32, "sm") for _ in range(NF)]
        onb = [st.sb([128, 128], BF16, "onb") for _ in range(NF)]
        f_t = [Trk() for _ in range(NF)]
        onb_t = [Trk() for _ in range(NF)]
        onb_ds = [P.dsem() for _ in range(NF)]
        NPT = 6
        pT = [st.sb([128, 512], BF16, "pT") for _ in range(NPT)]
        pT_t = [Trk() for _ in range(NPT)]
        units = []
        for h in range(8):
            for ti, (t0, W) in enumerate(cfg.tiles):
                q0 = nown - HALO + t0
                for kt in range((q0 + W) // 128):
                    units.append((h, ti, kt))
        state = {"qcnt": 0, "fcnt": 0, "scnt": 0, "pcnt": 0, "heads": set(), "q": {}}
        unit_bufs = {}

        def stage_s(u):
            h, ti, kt = units[u]
            t0, W = cfg.tiles[ti]
            q0 = nown - HALO + t0
            b = h % 2
            if (h, ti) not in state["q"]:
                qi = state["qcnt"] % 2
                state["qcnt"] += 1
                state["q"] = {(h, ti): qi}
                P.dma("sp", [(qb[qi][:, 0:W], qT[h * 128:(h + 1) * 128, t0:t0 + W])], qb_ds[qi],
                      reads=[cx.trk(qT_in, (h, ti))], writes=[qb_t[qi]])
            qi = state["q"][(h, ti)]
            k0 = kt * 128
            past = kt < NKP
            ksb = kp_sb[b] if past else ko_sb[b]
            kcol = k0 if past else k0 - nown
            d = k0 - q0
            rs = []
            for c in range(2):
                r = state["scnt"] % 4
                state["scnt"] += 1
                pr = state["pcnt"] % NPT
                state["pcnt"] += 1
                P.op("pe", "matmul", dict(out=sps[r][:, 0:W], lhsT=ksb[c * 64:(c + 1) * 64, kcol:kcol + 128],
                                          rhs=qb[qi][c * 64:(c + 1) * 64, 0:W], start=True, stop=True),
                     reads=[kv_t[b], qb_t[qi]], writes=[sps_t[r]])
                P.op("act", "activation", dict(out=pT[pr][:, 0:W], in_=sps[r][:, 0:W], func=AF.Exp, scale=0.125,
                                               bias=(pbias[:, 0:1] if past else 0.0)),
                     reads=[sps_t[r], c_t], writes=[pT_t[pr]])
                if d >= 0:
                    P.op("dve" if c == 0 else "pool", "tensor_tensor",
                         dict(out=pT[pr][:, 0:W], in0=pT[pr][:, 0:W], in1=tri[:, d // 128, 0:W], op=ALU.mult),
                         reads=[pT_t[pr], tri_t], writes=[pT_t[pr]])
                rs.append(pr)
            unit_bufs[u] = rs

        def stage_pv(u):
            h, ti, kt = units[u]
            t0, W = cfg.tiles[ti]
            q0 = nown - HALO + t0
            NS = W // 128
            b = h % 2
            k0 = kt * 128
            past = kt < NKP
            vsb = vp_sb[b] if past else vo_sb[b]
            vkt = kt if past else kt - NKP
            d = k0 - q0
            rs = unit_bufs.pop(u)
            if h not in state["heads"]:
                state["heads"].add(h)
                if h + 1 < 8:
                    load_head(h + 1)
            for c in range(2):
                pr = rs[c]
                js = [j for j in range(NS) if not (d >= 0 and j < d // 128)]
                for j in js:
                    lastkt = (q0 + j * 128) // 128
                    P.op("pe", "matmul", dict(out=acc[j][:, c * 129:(c + 1) * 129], lhsT=pT[pr][:, j * 128:(j + 1) * 128],
                                              rhs=vsb[:, vkt, 0:129], start=(kt == 0 and c == 0), stop=(kt == lastkt),
                                              skip_group_check=True),
                         reads=[pT_t[pr], kv_t[b]], writes=[acc_t[j]],
                         signal=(j == js[-1] or (kt == lastkt and c == 1)))
            for j in range(NS):
                if (q0 + j * 128) // 128 != kt:
                    continue
                fi = state["fcnt"] % NF
                state["fcnt"] += 1
                a = acc[j]
                o, s_, ft = ob[fi], sm[fi], f_t[fi]
                P.op("dve", "reciprocal", dict(out=s_[:, 0:1], in_=a[:, 128:129]), reads=[acc_t[j]], writes=[ft])
                P.op("dve", "reciprocal", dict(out=s_[:, 1:2], in_=a[:, 257:258]), reads=[acc_t[j]], writes=[ft])
                P.op("dve", "tensor_tensor", dict(out=s_[:, 2:3], in0=s_[:, 1:2], in1=nlam[:, 0:1], op=ALU.mult),
                     reads=[ft, l_t], writes=[ft])
                P.op("dve", "tensor_scalar", dict(out=o[:], in0=a[:, 0:128], scalar1=s_[:, 0:1], scalar2=None, op0=ALU.mult),
                     reads=[acc_t[j], ft], writes=[ft])
                P.op("dve", "scalar_tensor_tensor", dict(out=o[:], in0=a[:, 129:257], scalar=s_[:, 2:3], in1=o[:],
                                                         op0=ALU.mult, op1=ALU.add), reads=[acc_t[j], ft], writes=[ft])
                P.op("act", "activation", dict(out=junk[fi][:], in_=o[:], func=AF.Square, accum_out=s_[:, 3:4]),
                     reads=[ft], writes=[ft])
                P.op("pool", "tensor_scalar", dict(out=s_[:, 4:5], in0=s_[:, 3:4], scalar1=1.0 / 128.0, scalar2=float(RMS_EPS),
                                                   op0=ALU.mult, op1=ALU.add), reads=[ft], writes=[ft])
                P.op("pool", "tensor_tensor", dict(out=s_[:, 4:5], in0=s_[:, 4:5], in1=mh[:], op=ALU.pow),
                     reads=[ft, mh_t], writes=[ft])
                P.op("dve", "scalar_tensor_tensor", dict(out=onb[fi][:], in0=o[:], scalar=s_[:, 4:5], in1=gsub[:],
                                                         op0=ALU.mult, op1=ALU.mult), reads=[ft, l_t], writes=[onb_t[fi]])
                r0 = t0 + j * 128
                P.dma("sp", [(att[r0:r0 + 128, h * 128:(h + 1) * 128], onb[fi][:])], onb_ds[fi], reads=[onb_t[fi]],
                      writes=[cx.trk(attn_out, (ti, h))])

        load_head(0)
        stage_s(0)
        for u in range(len(units)):
            if u + 1 < len(units):
                stage_s(u + 1)
            stage_pv(u)


def stage_g1a(cx, xT_in, gu_out):
    cfg = cx.cfg
    P = cx.P
    with Stage(cx, "g1a") as st:
        gu = cx.dt(gu_out, [GH, cfg.nt], BF16)
        (cols,), cols_t = load_consts(cx, st, [("c0_ucols", [128, GC])])
        xT, xT_t = load_xT_resident(cx, st, xT_in)
        wl = ChunkLoader(cx, st, cx.dt("c0_wu", [GC, 128, DC, 128], F32))
        ps = [st.ps([128, 512], F32, "ps") for _ in range(2)]
        ps_t = [Trk() for _ in range(2)]
        ub = [st.sb([128, 512], BF16, "ub") for _ in range(2)]
        ub_t = [Trk() for _ in range(2)]
        ub_ds = [P.dsem() for _ in range(2)]
        pending = [wl.load(0)]
        it = 0
        for f in range(GC):
            if f + 1 < GC:
                pending.append(wl.load(f + 1))
            slot = pending.pop(0)
            for ti, (t0, W) in enumerate(cfg.tiles):
                i = it % 2
                it += 1
                for k in range(DC):
                    P.op("pe", "matmul", dict(out=ps[i][:, 0:W], lhsT=wl.wb[slot][:, k, :], rhs=xT[ti][:, k, 0:W],
                                              start=(k == 0), stop=(k == DC - 1)),
                         reads=[wl.wb_t[slot], xT_t[ti]], writes=[ps_t[i]], signal=(k == DC - 1))
                P.op("act", "activation", dict(out=ub[i][:, 0:W], in_=ps[i][:, 0:W], func=AF.Gelu, bias=cols[:, f:f + 1], scale=1.0),
                     reads=[ps_t[i], cols_t], writes=[ub_t[i]])
                P.dma("sp", [(gu[f * 128:(f + 1) * 128, t0:t0 + W], ub[i][:, 0:W])], ub_ds[i], reads=[ub_t[i]],
                      writes=[cx.trk(gu_out, (f, ti))])


def stage_g1b(cx, xT_in, gu_in, go_out):
    cfg = cx.cfg
    P = cx.P
    with Stage(cx, "g1b") as st:
        xTd = cx.dt(xT_in, [D, cfg.nt], BF16)
        gu = cx.dt(gu_in, [GH, cfg.nt], BF16)
        go = cx.dt(go_out, [GH, cfg.nt], BF16)
        (bvb, gcols, wsr, trim, bsb), c_t = load_consts(cx, st, [("c0_bvb", [128, GH]), ("c0_gcols", [128, GC, 2]),
                                                                ("c0_wsT", [128, 4, 128]), ("c0_trim", [128, 128]),
                                                                ("c0_bsb", [128, 4, 128])])
        wv = st.sb([128, DC, GH], BF16, "wv")
        wv_t = Trk()
        load_w_rows(cx, st, wv, wv_t, cx.dt("c0_wv", [128, DC, GH], F32), DC, width=GH)
        mh = st.sb([128, 1], F32, "mh")
        mh_t = Trk()
        P.op("pool", "memset", dict(ap=mh[:], constant=-0.5), writes=[mh_t])
        ones = st.sb([128, 128], BF16, "ones")
        s_t = Trk()
        P.op("pool", "memset", dict(ap=ones[:], constant=1.0), writes=[s_t])
        wsT = st.sb([128, 4, 128], BF16, "wsT")
        for g in range(4):
            P.op("dve", "tensor_tensor", dict(out=wsT[:, g, :], in0=wsr[:, g, :], in1=trim[:], op=ALU.mult), reads=[c_t], writes=[s_t])
        rs = st.ps([128, 512], F32, "rs")
        rs_t = Trk()
        for g in range(4):
            P.op("pe", "matmul", dict(out=rs[:, g * 128:(g + 1) * 128], lhsT=ones[:], rhs=wsT[:, g, :], start=True, stop=True),
                 reads=[s_t], writes=[rs_t], signal=(g == 3))
        E = st.sb([128, GC, 128], F32, "E")
        E_t = Trk()
        for cc in range(GC):
            g = cc // 6
            P.op("dve", "scalar_tensor_tensor", dict(out=E[:, cc, :], in0=rs[:, g * 128:(g + 1) * 128], scalar=gcols[:, cc, 1:2],
                                                     in1=bsb[:, g, :], op0=ALU.mult, op1=ALU.add), reads=[rs_t, c_t], writes=[E_t])
        xb = [st.sb([128, DC, 512], BF16, "xTt") for _ in range(2)]
        xb_t = [Trk() for _ in range(2)]
        xb_ds = [P.dsem() for _ in range(2)]
        vps = [st.ps([128, 512], F32, "vps") for _ in range(2)]
        vps_t = [Trk() for _ in range(2)]
        svps = [st.ps([128, 512], F32, "svps") for _ in range(2)]
        svps_t = [Trk() for _ in range(2)]
        vbuf = st.sb([128, GH], F32, "vbuf")
        vbuf_t = [Trk() for _ in range(6)]
        stt = st.sb([128, 48], F32, "stt")
        stt_t = Trk()
        vhat = [st.sb([128, GH], BF16, "vhat") for _ in range(2)]
        vhat_t = [Trk() for _ in range(2)]
        uTn = [st.sb([128, GC, 128], BF16, "uTn") for _ in range(2)]
        uTn_t = [Trk() for _ in range(2)]
        uTn_ds = [P.dsem() for _ in range(2)]
        tmp = st.sb([128, GC, 128], F32, "tmp")
        tmp_t = [Trk() for _ in range(6)]
        oTb = st.sb([128, GC, 512], BF16, "oTb")
        oTb_t = Trk()
        oTb_ds = P.dsem()

        def load_x(ti):
            t0, W = cfg.tiles[ti]
            P.dma("sp", [(xb[ti % 2][:, :, 0:W], xTd[:, t0:t0 + W].rearrange("(k p) t -> p k t", p=128))], xb_ds[ti % 2],
                  reads=[cx.trk(xT_in, ti)], writes=[xb_t[ti % 2]])
        load_x(0)
        cnt = 0
        vcnt = 0
        scnt = 0
        for ti, (t0, W) in enumerate(cfg.tiles):
            if ti + 1 < len(cfg.tiles):
                load_x(ti + 1)
            xt = xb[ti % 2]
            for s in range(W // 128):
                i = cnt % 2
                cnt += 1
                c0 = t0 + s * 128
                P.dma("sp", [(uTn[i][:], gu[:, c0:c0 + 128].rearrange("(k p) t -> p k t", p=128))], uTn_ds[i],
                      reads=[cx.trk(gu_in, (f, ti)) for f in range(GC)], writes=[uTn_t[i]])
                for fc in range(6):
                    vi = vcnt % 2
                    vcnt += 1
                    sl = slice(fc * 512, (fc + 1) * 512)
                    for k in range(DC):
                        P.op("pe", "matmul", dict(out=vps[vi][:], lhsT=xt[:, k, s * 128:(s + 1) * 128], rhs=wv[:, k, sl],
                                                  start=(k == 0), stop=(k == DC - 1)),
                             reads=[xb_t[ti % 2], wv_t], writes=[vps_t[vi]], signal=(k == DC - 1))
                    P.op("dve", "tensor_tensor", dict(out=vbuf[:, sl], in0=vps[vi][:], in1=bvb[:, sl], op=ALU.add),
                         reads=[vps_t[vi], c_t], writes=[vbuf_t[fc]])
                    P.op("act", "activation", dict(out=vbuf[:, sl], in_=vbuf[:, sl], func=AF.Gelu), reads=[vbuf_t[fc]], writes=[vbuf_t[fc]])
                    P.op("dve", "bn_stats", dict(out=stt[:, fc * 6:(fc + 1) * 6], in_=vbuf[:, sl]), reads=[vbuf_t[fc]], writes=[stt_t])
                P.op("dve", "bn_aggr", dict(out=stt[:, 36:38], in_=stt[:, 0:36]), reads=[stt_t], writes=[stt_t])
                P.op("pool", "tensor_scalar", dict(out=stt[:, 38:39], in0=stt[:, 37:38], scalar1=float(LN_EPS), scalar2=None, op0=ALU.add),
                     reads=[stt_t], writes=[stt_t])
                P.op("pool", "tensor_tensor", dict(out=stt[:, 38:39], in0=stt[:, 38:39], in1=mh[:], op=ALU.pow),
                     reads=[stt_t, mh_t], writes=[stt_t])
                vh = vhat[i]
                for hh in range(2):
                    sl = slice(hh * 1536, (hh + 1) * 1536)
                    P.op("dve", "tensor_scalar", dict(out=vh[:, sl], in0=vbuf[:, sl], scalar1=stt[:, 36:37], scalar2=stt[:, 38:39],
                                                      op0=ALU.subtract, op1=ALU.mult),
                         reads=[stt_t] + vbuf_t[3 * hh:3 * hh + 3], writes=[vhat_t[i]])
                for cg in range(6):
                    si = scnt % 2
                    scnt += 1
                    for q in range(4):
                        cc = cg * 4 + q
                        g = cc // 6
                        P.op("pe", "matmul", dict(out=svps[si][:, q * 128:(q + 1) * 128], lhsT=vh[:, cc * 128:(cc + 1) * 128],
                                                  rhs=wsT[:, g, :], start=True, stop=True),
                             reads=[vhat_t[i], s_t], writes=[svps_t[si]], signal=(q == 3))
                    for q in range(4):
                        cc = cg * 4 + q
                        P.op("dve", "scalar_tensor_tensor", dict(out=tmp[:, cc, :], in0=svps[si][:, q * 128:(q + 1) * 128],
                                                                 scalar=gcols[:, cc, 0:1], in1=E[:, cc, :], op0=ALU.mult, op1=ALU.add),
                             reads=[svps_t[si], c_t, E_t], writes=[tmp_t[cg]])
                    P.op("pool", "tensor_tensor", dict(out=oTb[:, cg * 4:cg * 4 + 4, s * 128:(s + 1) * 128], in0=tmp[:, cg * 4:cg * 4 + 4, :],
                                                       in1=uTn[i][:, cg * 4:cg * 4 + 4, :], op=ALU.mult),
                         reads=[tmp_t[cg], uTn_t[i]], writes=[oTb_t])
            P.dma("sp", [(go[:, t0:t0 + W].rearrange("(k p) t -> p k t", p=128), oTb[:, :, 0:W])], oTb_ds, reads=[oTb_t],
                  writes=[cx.trk(go_out, (k, ti)) for k in range(GC)])


def ffn_stages(li, xT_in, xres_in, xres_out, xT_out, final_out=None):
    return [
        lambda cx: stage_f1(cx, li, xT_in, "uT"),
        lambda cx: stage_proj(cx, "f2_%d" % li, "f%d" % li, li, "uT", FC, "f%d_wdn" % li, xres_in, xres_out, xT_out,
                              final_out=final_out),
    ]


def stages_part1():
    st = [lambda cx: stage_prep(cx, "x_in", "xT_a")]
    st += [lambda cx: stage_c1(cx, 0, "xT_a", "cv"),
           lambda cx: stage_tm_proj(cx, "c2_0", "m0", "cv", F32, True, "a0_lcols", "a0_wout", "x_in", "xr_a", "xT_b")]
    st += ffn_stages(0, "xT_b", "xr_a", "xr_b", "xT_a")
    st += [lambda cx: stage_a1(cx, "xT_a", "qT", "kT_own", "v_own")]
    return st


def stages_part2():
    st = [lambda cx: stage_a2(cx, 1, "qT", "kT_own", "kT_past", "v_own", "v_past", "att"),
          lambda cx: stage_tm_proj(cx, "a3", "m1", "att", BF16, False, None, "b0_wo", "xr_b", "xr_a", "xT_b")]
    st += ffn_stages(1, "xT_b", "xr_a", "xr_b", "xT_a")
    st += [lambda cx: stage_g1a(cx, "xT_a", "gu"),
           lambda cx: stage_g1b(cx, "xT_a", "gu", "go"),
           lambda cx: stage_proj(cx, "g2", "m2", 2, "go", GC, "c0_wo", "xr_b", "xr_a", "xT_b")]
    st += ffn_stages(2, "xT_b", "xr_a", "xr_b", "xT_a")
    st += [lambda cx: stage_c1(cx, 1, "xT_a", "cv"),
           lambda cx: stage_tm_proj(cx, "c2_1", "m3", "cv", F32, True, "a1_lcols", "a1_wout", "xr_b", "xr_a", "xT_b")]
    st += ffn_stages(3, "xT_b", "xr_a", None, None, final_out="out")
    return st


def stage_xchg(cx):
    cfg = cx.cfg
    P = cx.P
    nown = cfg.nown
    kT = cx.dt("kT_own", [D, nown], BF16)
    v = cx.dt("v_own", [nown, D], BF16)
    groups = [[0, 1], [2, 3], [4, 5], [6, 7]]
    KR = 256
    VR = min(1024, nown)
    P.barrier()
    t = Trk()
    kall, vall = [], []
    for p in range(D // KR):
        dst = cx.nc.dram_tensor("kT_all%d" % p, [2 * KR, nown], BF16).ap()
        P.op("pool", "collective_compute", dict(kind="AllGather", op=ALU.bypass, replica_groups=groups,
                                                ins=[kT[p * KR:(p + 1) * KR, :].opt()], outs=[dst.opt()]), writes=[t])
        kall.append(dst)
    for p in range(nown // VR):
        dst = cx.nc.dram_tensor("v_all%d" % p, [2 * VR, D], BF16).ap()
        P.op("pool", "collective_compute", dict(kind="AllGather", op=ALU.bypass, replica_groups=groups,
                                                ins=[v[p * VR:(p + 1) * VR, :].opt()], outs=[dst.opt()]), writes=[t])
        vall.append(dst)

    def kpast_fn(h):
        r0 = (h % 2) * 128
        return kall[h // 2][r0:r0 + 128, :]

    def vpast_fn(k0, k1, h):
        p = (k0 * 128) // VR
        assert (k1 * 128 - 1) // VR == p
        r0 = k0 * 128 - p * VR
        return vall[p][r0:r0 + (k1 - k0) * 128, h * 128:(h + 1) * 128]
    cx.views["kpast_fn"] = kpast_fn
    cx.views["vpast_fn"] = vpast_fn
    P.barrier()


SCRATCH = {"xT_a", "xT_b", "xr_a", "xr_b", "cv", "uT", "qT", "kT_own", "v_own", "kT_past", "v_past", "att", "gu", "go",
           "kT_all", "v_all"}
FUSED = True
_PROG_CACHE = {}


def _get_prog(key, cfg, stages, ext_out, internal):
    if key not in _PROG_CACHE:
        _PROG_CACHE[key] = build_program(cfg, stages, None, ext_out, internal=internal, want_names=True)
    return _PROG_CACHE[key]


def kernel(**inputs):
    inputs = {k: np.asarray(v) for k, v in inputs.items()}
    x = inputs["x"].astype(np.float32, copy=False)
    B, S, _ = x.shape
    ncore = 8
    nown = S // 2
    cfg = Cfg(nown)
    consts = host_consts(inputs)
    per_core = []
    for c in range(ncore):
        b, half = c // 2, c % 2
        if half == 0:
            xin = np.concatenate([np.zeros((HALO, D), np.float32), x[b, 0:nown]], axis=0)
        else:
            xin = x[b, nown - HALO:2 * nown]
        per_core.append({
            "x_in": np.ascontiguousarray(xin),
            "hm": np.full((128, 1), float(half), np.float32),
            "pbias": np.full((128, 1), 0.0 if half == 1 else -80.0, np.float32),
        })
    if FUSED:
        nc, _, names = _get_prog(("fused", nown), cfg, stages_part1() + [stage_xchg] + stages_part2(), {"out"}, SCRATCH)
        maps = []
        for c in range(ncore):
            maps.append({k: (per_core[c][k] if k in per_core[c] else consts[k]) for k in names})
        res = run_bass_kernel_spmd(nc, maps, core_ids=list(range(ncore))).results
        out = np.empty((B, S, D), np.float32)
        for c in range(ncore):
            b, half = c // 2, c % 2
            out[b, half * nown:(half + 1) * nown] = res[c]["out"]
        return out
    out1 = {"xr_b", "qT", "kT_own", "v_own"}
    nc1, _, names1 = _get_prog(("p1", nown), cfg, stages_part1(), out1, SCRATCH - out1)
    maps1 = []
    for c in range(ncore):
        m = {}
        for k in names1:
            m[k] = per_core[c][k] if k in per_core[c] else consts[k]
        maps1.append(m)
    res1 = run_bass_kernel_spmd(nc1, maps1, core_ids=list(range(ncore))).results
    out2 = {"out"}
    in2 = {"xr_b", "qT", "kT_own", "v_own", "kT_past", "v_past"}
    nc2, _, names2 = _get_prog(("p2", nown), cfg, stages_part2(), out2, SCRATCH - in2)
    maps2 = []
    for c in range(ncore):
        m = {}
        src = res1[c - (c % 2)]
        for k in names2:
            if k == "kT_past":
                m[k] = src["kT_own"]
            elif k == "v_past":
                m[k] = src["v_own"]
            elif k in ("xr_b", "qT", "kT_own", "v_own"):
                m[k] = res1[c][k]
            elif k in per_core[c]:
                m[k] = per_core[c][k]
            else:
                m[k] = consts[k]
        maps2.append(m)
    res2 = run_bass_kernel_spmd(nc2, maps2, core_ids=list(range(ncore))).results
    out = np.empty((B, S, D), np.float32)
    for c in range(ncore):
        b, half = c // 2, c % 2
        out[b, half * nown:(half + 1) * nown] = res2[c]["out"]
    return out
```

```python
import contextlib
import math
import numpy as np
import ml_dtypes
import concourse.bass as bass
import concourse.mybir as mybir
from concourse.bass_utils import run_bass_kernel_spmd

F32 = mybir.dt.float32
BF16 = mybir.dt.bfloat16
BF16_NP = ml_dtypes.bfloat16
AF = mybir.ActivationFunctionType
ALU = mybir.AluOpType

D = 1024
DC = 8
DEPTH = 4
HALO = 256
NOWN_FULL = 4096
FF = 2816
FC = 22
CONVK = 31
GH = 3072
GC = 24
ALPHA = (2 * DEPTH) ** 0.25
LN_EPS = 1e-5
RMS_EPS = 1e-5
SEM_LIMIT = 60000
SELF_SYNC = True


class Tok:
    __slots__ = ("sem", "val")

    def __init__(self, sem, val):
        self.sem = sem
        self.val = val


class Trk:
    __slots__ = ("w", "r")

    def __init__(self):
        self.w = []
        self.r = []


class DSem:
    __slots__ = ("sem", "cnt")

    def __init__(self, sem):
        self.sem = sem
        self.cnt = 0


class Prog:
    def __init__(self, nc, es):
        self.nc = nc
        self.es = es
        self.engs = {"pe": nc.tensor, "act": nc.scalar, "dve": nc.vector, "pool": nc.gpsimd, "sp": nc.sync}
        self.q = {k: [] for k in self.engs}
        self.pool_sems = []
        n = 0
        while n < 96:
            try:
                self.pool_sems.append(es.enter_context(nc.semaphore("s%d" % n)))
            except Exception:
                break
            n += 1
        self.esem = {k: self.pool_sems.pop() for k in self.engs}
        self.dpool = [DSem(x) for x in self.pool_sems[8:]]
        self.pool_sems = self.pool_sems[:8]
        self.stage_ds = []
        self.ecnt = {k: 0 for k in self.engs}
        self.unsig = {k: False for k in self.engs}
        self.seen = {k: {} for k in self.engs}
        self.stage_dma = {}
        self.dsems = []
        self.ninst = 0

    def dsem(self):
        d = self.dpool.pop()
        self.stage_ds.append(d)
        return d

    def release_dsems(self):
        self.dpool.extend(self.stage_ds)
        self.stage_ds = []

    def _collect(self, eng, reads, writes, extra):
        need = {}

        def add(t):
            if t is None:
                return
            k = id(t.sem)
            if k not in need or need[k].val < t.val:
                need[k] = t
        for b in reads:
            for t in b.w:
                add(t)
        for b in writes:
            for t in b.w:
                add(t)
            for t in b.r:
                add(t)
        for t in extra:
            add(t)
        waits = []
        seen = self.seen[eng]
        own = self.esem[eng]
        for k, t in need.items():
            if t.sem is own and (eng == "pe" or not SELF_SYNC):
                continue
            if seen.get(k, 0) >= t.val:
                continue
            seen[k] = t.val
            waits.append(t)
        return waits

    @staticmethod
    def _commit(tok, reads, writes):
        for b in reads:
            b.r.append(tok)
            if len(b.r) > 64:
                best = {}
                for t in b.r:
                    k = id(t.sem)
                    if k not in best or best[k].val < t.val:
                        best[k] = t
                b.r = list(best.values())
        for b in writes:
            b.w = [tok]
            b.r = []

    def op(self, eng, name, kw, reads=(), writes=(), extra=(), signal=True):
        waits = self._collect(eng, reads, writes, extra)
        sem = self.esem[eng]
        if SELF_SYNC and eng != "pe":
            signal = True
        self.unsig[eng] = not signal
        if signal:
            self.ecnt[eng] += 1
            tok = Tok(sem, self.ecnt[eng])
        else:
            tok = Tok(sem, self.ecnt[eng] + 1)

        def emit(e):
            for t in waits:
                e.wait_ge(t.sem, t.val)
            ins = getattr(e, name)(**kw)
            if signal:
                ins.then_inc(sem, 1)
        self.q[eng].append(emit)
        self.ninst += 1
        self._commit(tok, reads, writes)
        if signal and self.ecnt[eng] >= SEM_LIMIT:
            self.esem[eng] = self.pool_sems.pop()
            self.ecnt[eng] = 0
        return tok

    def dma(self, eng, pairs, ds, reads=(), writes=(), extra=()):
        if ds.cnt + 16 * len(pairs) >= SEM_LIMIT:
            ds.sem = self.pool_sems.pop()
            ds.cnt = 0
        waits = self._collect(eng, reads, writes, extra)
        ds.cnt += 16 * len(pairs)
        sem = ds.sem
        tok = Tok(sem, ds.cnt)

        def emit(e):
            for t in waits:
                e.wait_ge(t.sem, t.val)
            for (o, i) in pairs:
                e.dma_start(out=o, in_=i).then_inc(sem, 16)
        self.q[eng].append(emit)
        self.ninst += len(pairs)
        self._commit(tok, reads, writes)
        self.stage_dma[id(sem)] = tok
        return tok

    def barrier(self):
        toks = []
        for k in self.engs:
            assert not self.unsig[k], k
            if self.ecnt[k] > 0:
                toks.append(Tok(self.esem[k], self.ecnt[k]))
        toks += list(self.stage_dma.values())
        self.stage_dma = {}
        for k in self.engs:
            waits = self._collect(k, (), (), toks)
            if waits:
                def emit(e, waits=waits):
                    for t in waits:
                        e.wait_ge(t.sem, t.val)
                self.q[k].append(emit)

    def finish(self):
        self.barrier()
        block = self.es.enter_context(self.nc.Block())
        q = self.q

        @block.tensor
        def _(e):
            for f in q["pe"]:
                f(e)

        @block.scalar
        def _(e):
            for f in q["act"]:
                f(e)

        @block.vector
        def _(e):
            for f in q["dve"]:
                f(e)

        @block.gpsimd
        def _(e):
            for f in q["pool"]:
                f(e)

        @block.sync
        def _(e):
            for f in q["sp"]:
                f(e)


class Cfg:
    def __init__(self, nown):
        self.nown = nown
        self.nt = HALO + nown
        self.tiles = [(0, HALO)] + [(HALO + 512 * i, 512) for i in range(nown // 512)]


class Ctx:
    def __init__(self, nc, es, cfg, ext_in, ext_out):
        self.nc = nc
        self.es = es
        self.cfg = cfg
        self.P = Prog(nc, es)
        self.ext_in = ext_in
        self.ext_out = ext_out
        self.dram = {}
        self.dtrk = {}
        self.kinds = {}
        self.internal = set()
        self.views = {}

    def dt(self, name, shape=None, dtype=None):
        if name in self.views:
            return self.views[name]
        if name not in self.dram:
            kind = "Internal"
            if name in self.ext_out:
                kind = "ExternalOutput"
            elif self.ext_in is None:
                if name not in self.internal:
                    kind = "ExternalInput"
            elif name in self.ext_in:
                kind = "ExternalInput"
            self.kinds[name] = kind
            self.dram[name] = self.nc.dram_tensor(name, list(shape), dtype, kind=kind).ap()
        return self.dram[name]

    def trk(self, name, idx):
        key = (name, idx)
        if key not in self.dtrk:
            self.dtrk[key] = Trk()
        return self.dtrk[key]


class Stage:
    def __init__(self, cx, name):
        self.cx = cx
        self.name = name
        self.es = contextlib.ExitStack()
        self.n = 0

    def __enter__(self):
        self.es.__enter__()
        return self

    def __exit__(self, *a):
        self.cx.P.barrier()
        self.cx.P.release_dsems()
        return self.es.__exit__(*a)

    def sb(self, shape, dtype, tag="t"):
        self.n += 1
        return self.es.enter_context(self.cx.nc.sbuf_tensor("%s_%s%d" % (self.name, tag, self.n), list(shape), dtype))

    def ps(self, shape, dtype, tag="p"):
        self.n += 1
        return self.es.enter_context(self.cx.nc.psum_tensor("%s_%s%d" % (self.name, tag, self.n), list(shape), dtype))


def load_consts(cx, st, specs, eng="sp"):
    P = cx.P
    ds = P.dsem()
    trk = Trk()
    ts = []
    pairs = []
    for (name, shape) in specs:
        src = cx.dt(name, shape, F32)
        t = st.sb(shape, F32, tag=name)
        ts.append(t)
        pairs.append((t[:], src))
    P.dma(eng, pairs, ds, writes=[trk])
    return ts, trk


class Epi:
    def __init__(self, cx, st, prefix, li, xres_in, xres_out, xT_out, final_out=None):
        P = cx.P
        self.cx, self.st = cx, st
        cfg = cx.cfg
        self.xres_in = cx.dt(xres_in, [cfg.nt, D], F32)
        self.xres_in_name = xres_in
        self.final_out = final_out
        if final_out is None:
            self.xres_out = cx.dt(xres_out, [cfg.nt, D], F32)
            self.xT_out = cx.dt(xT_out, [D, cfg.nt], BF16)
        else:
            self.xres_out = cx.dt(final_out, [cfg.nown, D], F32)
            self.xT_out = None
        self.xres_out_name = xres_out if final_out is None else final_out
        self.xT_out_name = xT_out
        (self.bb, self.gam, self.bet, self.ident), ct = load_consts(
            cx, st, [(prefix + "_bb", [128, D]), (prefix + "_g", [128, D]), (prefix + "_b", [128, D]),
                     ("ident_f32", [128, 128])])
        self.bb_t = self.gam_t = self.bet_t = self.ident_t = ct
        self.mh = st.sb([128, 1], F32, "mh")
        self.mh_t = Trk()
        P.op("pool", "memset", dict(ap=self.mh[:], constant=-0.5), writes=[self.mh_t])
        NR = 2
        self.xold = [st.sb([128, 4, D], F32, "xold") for _ in range(NR)]
        self.xold_t = [Trk() for _ in range(NR)]
        self.xold_ds = [P.dsem() for _ in range(NR)]
        self.s_ = [st.sb([128, D], F32, "s") for _ in range(2)]
        self.s_t = [Trk() for _ in range(2)]
        self.xn = [st.sb([128, D], F32, "xn") for _ in range(2)]
        self.xn_t = [Trk() for _ in range(2)]
        self.xnew = [st.sb([128, D], F32, "xnew") for _ in range(2)]
        self.xnew_t = [Trk() for _ in range(2)]
        self.xnew_ds = [P.dsem() for _ in range(2)]
        self.stat = [st.sb([128, 16], F32, "stat") for _ in range(2)]
        self.stat_t = [Trk() for _ in range(2)]
        self.xTo = [st.sb([128, DC, 512], BF16, "xTo") for _ in range(2)]
        self.xTo_t = [Trk() for _ in range(2)]
        self.xTo_ds = [P.dsem() for _ in range(2)]
        self.tp = [st.ps([128, D], F32, "tp") for _ in range(1)]
        self.tp_t = [Trk() for _ in range(1)]
        self.cnt = 0
        self.tile_cnt = 0

    def load_xold(self, ti):
        P = self.cx.P
        t0, W = self.cx.cfg.tiles[ti]
        slot = ti % len(self.xold)
        src = self.xres_in[t0:t0 + W, :].rearrange("(s p) d -> p s d", p=128)
        P.dma("sp", [(self.xold[slot][:, 0:W // 128, :], src)], self.xold_ds[slot],
              reads=[self.cx.trk(self.xres_in_name, (ti, s)) for s in range(W // 128)], writes=[self.xold_t[slot]])

    def sub_a(self, ti, s, y_ps, y_t):
        cx, P = self.cx, self.cx.P
        t0, W = cx.cfg.tiles[ti]
        slot = ti % len(self.xold)
        i = self.cnt % 2
        self.cnt += 1
        xold = self.xold[slot][:, s, :]
        if y_ps is not None:
            xnew = self.xnew[i]
            s_ = self.s_[i]
            P.op("dve", "scalar_tensor_tensor", dict(out=s_[:], in0=xold, scalar=float(ALPHA), in1=self.bb[:],
                                                         op0=ALU.mult, op1=ALU.add),
                 reads=[self.xold_t[slot], self.bb_t], writes=[self.s_t[i]])
            for h in range(2):
                P.op("dve", "tensor_tensor", dict(out=s_[:, h * 512:(h + 1) * 512], in0=y_ps[:, h * 512:(h + 1) * 512],
                                                          in1=s_[:, h * 512:(h + 1) * 512], op=ALU.add),
                     reads=[y_t], writes=[self.s_t[i]], signal=(h == 1))
            stt = self.stat[i]
            for h in range(2):
                P.op("dve", "bn_stats", dict(out=stt[:, h * 6:(h + 1) * 6], in_=s_[:, h * 512:(h + 1) * 512]),
                     reads=[self.s_t[i]], writes=[self.stat_t[i]], signal=False)
            P.op("dve", "bn_aggr", dict(out=stt[:, 12:14], in_=stt[:, 0:12]), reads=[self.stat_t[i]], writes=[self.stat_t[i]])
            P.op("pool", "tensor_scalar", dict(out=stt[:, 14:15], in0=stt[:, 13:14], scalar1=float(LN_EPS), scalar2=None,
                                                   op0=ALU.add), reads=[self.stat_t[i]], writes=[self.stat_t[i]], signal=False)
            P.op("pool", "tensor_tensor", dict(out=stt[:, 14:15], in0=stt[:, 14:15], in1=self.mh[:], op=ALU.pow),
                 reads=[self.stat_t[i], self.mh_t], writes=[self.stat_t[i]])
            xn = self.xn[i]
            P.op("dve", "scalar_tensor_tensor", dict(out=xn[:], in0=s_[:], scalar=stt[:, 12:13], in1=self.gam[:],
                                                     op0=ALU.subtract, op1=ALU.mult),
                 reads=[self.s_t[i], self.stat_t[i], self.gam_t], writes=[self.xn_t[i]])
            P.op("dve", "scalar_tensor_tensor", dict(out=xnew[:], in0=xn[:], scalar=stt[:, 14:15], in1=self.bet[:],
                                                     op0=ALU.mult, op1=ALU.add),
                 reads=[self.xn_t[i], self.stat_t[i], self.bet_t], writes=[self.xnew_t[i]])
            src_for_T = xnew
            src_t = self.xnew_t[i]
            if self.final_out is None:
                dst = self.xres_out[t0 + s * 128:t0 + (s + 1) * 128, :]
                P.dma("sp", [(dst, xnew[:])], self.xnew_ds[i], reads=[self.xnew_t[i]],
                      writes=[cx.trk(self.xres_out_name, (ti, s))])
            else:
                if t0 >= HALO:
                    o0 = t0 - HALO + s * 128
                    dst = self.xres_out[o0:o0 + 128, :]
                    P.dma("sp", [(dst, xnew[:])], self.xnew_ds[i], reads=[self.xnew_t[i]],
                          writes=[cx.trk(self.xres_out_name, (ti, s))])
            src_ap = xnew
        else:
            src_ap = None
            src_t = self.xold_t[slot]
        return (ti, s, src_ap, src_t, slot)

    def sub_b(self, state):
        cx, P = self.cx, self.cx.P
        ti, s, src_ap, src_t, slot = state
        t0, W = cx.cfg.tiles[ti]
        if self.xT_out is None:
            return
        j = self.tile_cnt % 2
        tp = self.tp[0]
        for k in range(DC):
            if src_ap is not None:
                inp = src_ap[:, k * 128:(k + 1) * 128]
            else:
                inp = self.xold[slot][:, s, k * 128:(k + 1) * 128]
            P.op("pe", "transpose", dict(out=tp[:, k * 128:(k + 1) * 128], in_=inp, identity=self.ident[:]),
                 reads=[src_t, self.ident_t], writes=[self.tp_t[0]], signal=(k == DC - 1))
        xTo = self.xTo[j]
        P.op("act", "copy", dict(out=xTo[:, :, s * 128:(s + 1) * 128], in_=tp[:].rearrange("p (k t) -> p k t", k=DC)),
             reads=[self.tp_t[0]], writes=[self.xTo_t[j]])
        if s == W // 128 - 1:
            dst = self.xT_out[:, t0:t0 + W].rearrange("(k p) t -> p k t", p=128)
            P.dma("sp", [(dst, xTo[:, :, 0:W])], self.xTo_ds[j], reads=[self.xTo_t[j]],
                  writes=[cx.trk(self.xT_out_name, ti)])
            self.tile_cnt += 1


    def sub(self, ti, s, y_ps, y_t):
        self.sub_b(self.sub_a(ti, s, y_ps, y_t))


def stage_prep(cx, x_in, xT_out):
    cfg = cx.cfg
    with Stage(cx, "prep") as st:
        ep = Epi.__new__(Epi)
        P = cx.P
        ep.cx, ep.st = cx, st
        ep.xres_in = cx.dt(x_in, [cfg.nt, D], F32)
        ep.xres_in_name = x_in
        ep.final_out = None
        ep.xT_out = cx.dt(xT_out, [D, cfg.nt], BF16)
        ep.xT_out_name = xT_out
        (ep.ident,), ep.ident_t = load_consts(cx, st, [("ident_f32", [128, 128])])
        ep.xold = [st.sb([128, 4, D], F32, "xold") for _ in range(2)]
        ep.xold_t = [Trk() for _ in range(2)]
        ep.xold_ds = [P.dsem() for _ in range(2)]
        ep.xTo = [st.sb([128, DC, 512], BF16, "xTo") for _ in range(2)]
        ep.xTo_t = [Trk() for _ in range(2)]
        ep.xTo_ds = [P.dsem() for _ in range(2)]
        ep.tp = [st.ps([128, D], F32, "tp")]
        ep.tp_t = [Trk()]
        ep.cnt = 0
        ep.tile_cnt = 0
        for ti, (t0, W) in enumerate(cfg.tiles):
            ep.load_xold(ti)
            for s in range(W // 128):
                ep.sub(ti, s, None, None)


def load_xT_resident(cx, st, xT_in):
    cfg = cx.cfg
    P = cx.P
    src = cx.dt(xT_in, [D, cfg.nt], BF16)
    xT = []
    trks = []
    for ti, (t0, W) in enumerate(cfg.tiles):
        ds = P.dsem()
        trk = Trk()
        xt = st.sb([128, DC, W], BF16, "xTres")
        P.dma("sp", [(xt[:], src[:, t0:t0 + W].rearrange("(k p) t -> p k t", p=128))], ds,
              reads=[cx.trk(xT_in, ti)], writes=[trk])
        trks.append(trk)
        xT.append(xt)
    return xT, trks


def stage_f1(cx, li, xT_in, uT_out):
    cfg = cx.cfg
    P = cx.P
    with Stage(cx, "f1_%d" % li) as st:
        uT = cx.dt(uT_out, [FF, cfg.nt], BF16)
        (cols, hm), cols_t = load_consts(cx, st, [("f%d_cols" % li, [128, 2 * FC, 5]), ("hm", [128, 1])])
        hm_t = cols_t
        xT, xT_t = load_xT_resident(cx, st, xT_in)
        wsrc = cx.dt("f%d_wup" % li, [2 * FC, 128, DC, 128], F32)
        NW = 4
        wb = [st.sb([128, DC, 128], BF16, "w") for _ in range(NW)]
        wb_t = [Trk() for _ in range(NW)]
        wst = [st.sb([128, DC, 128], F32, "wst") for _ in range(NW)]
        wst_t = [Trk() for _ in range(NW)]
        wb_ds = [P.dsem() for _ in range(NW)]
        gps = [st.ps([128, 512], F32, "g") for _ in range(2)]
        gps_t = [Trk() for _ in range(2)]
        vps = [st.ps([128, 512], F32, "v") for _ in range(2)]
        vps_t = [Trk() for _ in range(2)]
        hg = [st.sb([128, 514], F32, "hg") for _ in range(2)]
        hg_t = [Trk() for _ in range(2)]
        hv = [st.sb([128, 514], F32, "hv") for _ in range(2)]
        hv_t = [Trk() for _ in range(2)]
        cg = [st.sb([128, 512], F32, "cg") for _ in range(2)]
        cg_t = [Trk() for _ in range(2)]
        cv = [st.sb([128, 512], F32, "cv") for _ in range(2)]
        cv_t = [Trk() for _ in range(2)]
        sg = [st.sb([128, 512], F32, "sg") for _ in range(2)]
        sg_t = [Trk() for _ in range(2)]
        ub = [st.sb([128, 512], BF16, "u") for _ in range(2)]
        ub_t = [Trk() for _ in range(2)]
        ub_ds = [P.dsem() for _ in range(2)]
        wcnt = 0

        def load_w(j):
            nonlocal wcnt
            slot = wcnt % NW
            wcnt += 1
            P.dma("sp", [(wst[slot][:], wsrc[j])], wb_ds[slot], writes=[wst_t[slot]])
            P.op("pool", "tensor_copy", dict(out=wb[slot][:], in_=wst[slot][:]), reads=[wst_t[slot]], writes=[wb_t[slot]])
            return slot

        pending = [(load_w(0), load_w(FC))]
        it = 0
        for f in range(FC):
            if f + 1 < FC:
                pending.append((load_w(f + 1), load_w(FC + f + 1)))
            sg_slot, sv_slot = pending.pop(0)
            for ti, (t0, W) in enumerate(cfg.tiles):
                i = it % 2
                it += 1
                for (slot, pst, pstt, jj) in ((sg_slot, gps[i], gps_t[i], f), (sv_slot, vps[i], vps_t[i], FC + f)):
                    for k in range(DC):
                        P.op("pe", "matmul", dict(
                            out=pst[:, 0:W], lhsT=wb[slot][:, k, :], rhs=xT[ti][:, k, 0:W], start=(k == 0), stop=(k == DC - 1)),
                            reads=[wb_t[slot], xT_t[ti]], writes=[pstt], signal=(k == DC - 1))
                for (hb, hbt, pst, pstt, jj) in ((hg, hg_t, gps[i], gps_t[i], f), (hv, hv_t, vps[i], vps_t[i], FC + f)):
                    h = hb[i]
                    bcol = cols[:, jj, 0:1]
                    if ti == 0:
                        P.op("pool", "memset", dict(ap=h[:, 0:2], constant=0.0), writes=[hbt[i]])
                        P.op("dve", "tensor_scalar", dict(
                            out=h[:, 2:2 + W], in0=pst[:, 0:W], scalar1=bcol, scalar2=hm[:, 0:1], op0=ALU.add, op1=ALU.mult),
                            reads=[pstt, cols_t, hm_t], writes=[hbt[i]])
                    else:
                        hp = hb[1 - i]
                        Wp = cfg.tiles[ti - 1][1]
                        P.op("pool", "tensor_copy", dict(out=h[:, 0:2], in_=hp[:, Wp:Wp + 2]),
                             reads=[hbt[1 - i]], writes=[hbt[i]])
                        P.op("act", "activation", dict(
                            out=h[:, 2:2 + W], in_=pst[:, 0:W], func=AF.Identity, bias=bcol, scale=1.0),
                            reads=[pstt, cols_t], writes=[hbt[i]])
                br = ((hg, hg_t, cg, cg_t, f), (hv, hv_t, cv, cv_t, FC + f))
                for (hb, hbt, cb, cbt, jj) in br:
                    P.op("act", "activation", dict(out=cb[i][:, 0:W], in_=hb[i][:, 2:2 + W], func=AF.Identity, bias=cols[:, jj, 4:5],
                                                   scale=cols[:, jj, 3:4]), reads=[hbt[i], cols_t], writes=[cbt[i]])
                for (off, tap) in ((1, 2), (0, 1)):
                    for (hb, hbt, cb, cbt, jj) in br:
                        P.op("dve", "scalar_tensor_tensor", dict(out=cb[i][:, 0:W], in0=hb[i][:, off:off + W], scalar=cols[:, jj, tap:tap + 1],
                                                                 in1=cb[i][:, 0:W], op0=ALU.mult, op1=ALU.add),
                             reads=[hbt[i], cbt[i]], writes=[cbt[i]])
                P.op("act", "activation", dict(out=sg[i][:, 0:W], in_=cg[i][:, 0:W], func=AF.Silu),
                     reads=[cg_t[i]], writes=[sg_t[i]])
                P.op("dve", "tensor_tensor", dict(out=ub[i][:, 0:W], in0=sg[i][:, 0:W], in1=cv[i][:, 0:W], op=ALU.mult),
                     reads=[sg_t[i], cv_t[i]], writes=[ub_t[i]])
                P.dma("sp", [(uT[f * 128:(f + 1) * 128, t0:t0 + W], ub[i][:, 0:W])], ub_ds[i], reads=[ub_t[i]],
                      writes=[cx.trk(uT_out, (f, ti))])


def load_w_rows(cx, st, wd, wd_t, wsrc, KC, width=D):
    P = cx.P
    stg = [st.sb([128, 2048], F32, "wstg") for _ in range(2)]
    stg_t = [Trk() for _ in range(2)]
    stg_ds = [P.dsem() for _ in range(2)]
    n = 0
    if width <= 1024:
        pieces = [(k0, min(KC, k0 + 2048 // width), 0, width) for k0 in range(0, KC, 2048 // width)]
    else:
        pieces = [(k, k + 1, w0, min(width, w0 + 2048)) for k in range(KC) for w0 in range(0, width, 2048)]
    for (k0, k1, w0, w1) in pieces:
        i = n % 2
        n += 1
        nk, nw = k1 - k0, w1 - w0
        sview = stg[i][:, 0:nk * nw].rearrange("p (k w) -> p k w", k=nk)
        P.dma("sp", [(sview, wsrc[:, k0:k1, w0:w1])], stg_ds[i], writes=[stg_t[i]])
        if i == 0:
            P.op("pool", "tensor_copy", dict(out=wd[:, k0:k1, w0:w1], in_=sview), reads=[stg_t[i]], writes=[wd_t])
        else:
            P.op("act", "copy", dict(out=wd[:, k0:k1, w0:w1], in_=sview), reads=[stg_t[i]], writes=[wd_t])


def stage_proj(cx, name, prefix, li, aT_in, KC, w_name, xres_in, xres_out, xT_out, final_out=None):
    cfg = cx.cfg
    P = cx.P
    with Stage(cx, name) as st:
        aT = cx.dt(aT_in, [KC * 128, cfg.nt], BF16)
        wsrc = cx.dt(w_name, [128, KC, D], F32)
        wd = st.sb([128, KC, D], BF16, "wd")
        wd_t = Trk()
        load_w_rows(cx, st, wd, wd_t, wsrc, KC)
        ep = Epi(cx, st, prefix, li, xres_in, xres_out, xT_out, final_out=final_out)
        NA = 2
        ab = [st.sb([128, KC, 512], BF16, "a") for _ in range(NA)]
        ab_t = [Trk() for _ in range(NA)]
        ab_ds = [P.dsem() for _ in range(NA)]
        yps = [st.ps([128, D], F32, "y") for _ in range(2)]
        yps_t = [Trk() for _ in range(2)]
        tiles = list(enumerate(cfg.tiles))
        if final_out is not None:
            tiles = tiles[1:]

        def load_a(ti):
            t0, W = cfg.tiles[ti]
            slot = ti % NA
            reads = [cx.trk(aT_in, (k, ti)) for k in range(KC)]
            P.dma("sp", [(ab[slot][:, :, 0:W], aT[:, t0:t0 + W].rearrange("(k p) t -> p k t", p=128))], ab_ds[slot],
                  reads=reads, writes=[ab_t[slot]])
            ep.load_xold(ti)
        load_a(tiles[0][0])
        cnt = 0
        pend = None
        for n, (ti, (t0, W)) in enumerate(tiles):
            if n + 1 < len(tiles):
                load_a(tiles[n + 1][0])
            slot = ti % NA
            for s in range(W // 128):
                i = cnt % 2
                cnt += 1
                for h in range(2):
                    for k in range(KC):
                        P.op("pe", "matmul", dict(
                            out=yps[i][:, h * 512:(h + 1) * 512], lhsT=ab[slot][:, k, s * 128:(s + 1) * 128],
                            rhs=wd[:, k, h * 512:(h + 1) * 512], start=(k == 0), stop=(k == KC - 1)),
                            reads=[ab_t[slot], wd_t], writes=[yps_t[i]], signal=(k == KC - 1))
                st_a = ep.sub_a(ti, s, yps[i], yps_t[i])
                if pend is not None:
                    ep.sub_b(pend)
                pend = st_a
        if pend is not None:
            ep.sub_b(pend)


def build_program(cfg, stages, ext_in, ext_out, internal=(), want_names=False):
    nc = bass.Bass("TRN2", target_bir_lowering=False)
    es = contextlib.ExitStack()
    with es:
        cx = Ctx(nc, es, cfg, None if ext_in is None else set(ext_in), set(ext_out))
        cx.internal = set(internal)
        for fn in stages:
            fn(cx)
        cx.P.finish()
        ninst = cx.P.ninst
    if want_names:
        return nc, ninst, [k for k, v in cx.kinds.items() if v == "ExternalInput"]
    return nc, ninst


def lay_cols(v):
    return np.ascontiguousarray(v.reshape(-1, 128).T)


def lay_bcast(v):
    return np.ascontiguousarray(np.broadcast_to(v[None, :], (128, v.shape[0])))


def lay_w_chunks(w):
    K, N = w.shape
    return np.ascontiguousarray(w.reshape(K // 128, 128, N // 128, 128).transpose(2, 1, 0, 3))


def lay_w_rows(w):
    K, N = w.shape
    return np.ascontiguousarray(w.reshape(K // 128, 128, N).transpose(1, 0, 2))


def host_consts(inp):
    c = {}
    c["ident_f32"] = np.eye(128, dtype=np.float32)
    for li in range(DEPTH):
        bup = inp["f_b_up"][li]
        wdw = inp["f_w_dw"][li]
        bdw = inp["f_b_dw"][li]
        cols = np.stack([lay_cols(bup), lay_cols(wdw[0]), lay_cols(wdw[1]), lay_cols(wdw[2]), lay_cols(bdw)], axis=-1)
        c["f%d_cols" % li] = np.ascontiguousarray(cols.astype(np.float32))
        c["f%d_wup" % li] = lay_w_chunks(inp["f_w_up"][li])
        c["f%d_wdn" % li] = lay_w_rows(inp["f_w_down"][li])
        c["f%d_bb" % li] = lay_bcast(inp["f_b_down"][li])
        c["f%d_g" % li] = lay_bcast(inp["ln_ffn_g"][li])
        c["f%d_b" % li] = lay_bcast(inp["ln_ffn_b"][li])
    for j in range(inp["a_w_in"].shape[0]):
        bi = inp["a_b_in"][j]
        wdw = inp["a_w_dw"][j]
        cols = np.concatenate([lay_cols(bi[:D])[:, :, None], lay_cols(bi[D:])[:, :, None], lay_cols(inp["a_b_dw"][j])[:, :, None],
                               np.stack([lay_cols(wdw[k]) for k in range(CONVK)], axis=-1)], axis=-1)
        c["a%d_cols" % j] = np.ascontiguousarray(cols.astype(np.float32))
        c["a%d_win" % j] = lay_w_chunks(inp["a_w_in"][j])
        c["a%d_wout" % j] = lay_w_rows(inp["a_w_out"][j])
        c["a%d_lcols" % j] = np.ascontiguousarray(np.stack([lay_cols(inp["a_ln_g"][j]), lay_cols(inp["a_ln_b"][j])], axis=-1))
    c["ident_bf16"] = np.eye(128, dtype=np.float32).astype(BF16_NP)
    wqkv = inp["b_w_qkv"][0]
    c["b0_wqk"] = lay_w_chunks(wqkv[:, :2 * D])
    c["b0_wv"] = lay_w_rows(wqkv[:, 2 * D:])
    c["b0_wo"] = lay_w_rows(inp["b_w_o"][0])
    lam = np.stack([inp["b_lq1"][0], inp["b_lk1"][0], inp["b_lq2"][0], inp["b_lk2"][0]], axis=0)
    c["b0_lam"] = np.ascontiguousarray(np.broadcast_to(lam[None], (128, 4, 64))).astype(np.float32)
    c["b0_gsub"] = lay_bcast(inp["b_subln_g"][0])
    win = inp["c_w_in"][0]
    c["c0_wu"] = lay_w_chunks(win[:, :GH])
    c["c0_wv"] = lay_w_rows(win[:, GH:])
    c["c0_wo"] = lay_w_rows(inp["c_w_out"][0])
    c["c0_ucols"] = lay_cols(inp["c_b_in"][0][:GH])
    c["c0_bvb"] = lay_bcast(inp["c_b_in"][0][GH:])
    c["c0_gcols"] = np.ascontiguousarray(np.stack([lay_cols(inp["c_ln_g"][0]), lay_cols(inp["c_ln_b"][0])], axis=-1))
    c["c0_wsT"] = np.ascontiguousarray(inp["c_w_s"][0].transpose(2, 0, 1))
    c["c0_trim"] = (np.arange(128)[None, :] >= np.arange(128)[:, None]).astype(np.float32)
    c["c0_bsb"] = np.ascontiguousarray(np.broadcast_to(inp["c_b_s"][0][None], (128, 4, 128))).astype(np.float32)
    pp = np.arange(128)[:, None, None]
    mm = np.arange(4)[None, :, None]
    cc = np.arange(512)[None, None, :]
    c["trimask"] = (cc >= pp + 128 * mm).astype(np.float32).astype(BF16_NP)
    for li in range(DEPTH):
        kind, j = li % 3, li // 3
        bo = [inp["a_b_out"], None, inp["c_b_out"]][kind]
        c["m%d_bb" % li] = lay_bcast(bo[j]) if bo is not None else np.zeros((128, D), np.float32)
        c["m%d_g" % li] = lay_bcast(inp["ln_mix_g"][li])
        c["m%d_b" % li] = lay_bcast(inp["ln_mix_b"][li])
    return c


class ChunkLoader:
    def __init__(self, cx, st, wsrc, nslots=4, cast_eng="pool"):
        P = cx.P
        self.cx, self.wsrc = cx, wsrc
        self.n = nslots
        self.wb = [st.sb([128, DC, 128], BF16, "w") for _ in range(nslots)]
        self.wb_t = [Trk() for _ in range(nslots)]
        self.wst = [st.sb([128, DC, 128], F32, "wst") for _ in range(nslots)]
        self.wst_t = [Trk() for _ in range(nslots)]
        self.ds = [P.dsem() for _ in range(nslots)]
        self.cnt = 0
        self.cast_eng = cast_eng

    def load(self, j):
        P = self.cx.P
        slot = self.cnt % self.n
        self.cnt += 1
        P.dma("sp", [(self.wst[slot][:], self.wsrc[j])], self.ds[slot], writes=[self.wst_t[slot]])
        if self.cast_eng == "pool":
            P.op("pool", "tensor_copy", dict(out=self.wb[slot][:], in_=self.wst[slot][:]),
                 reads=[self.wst_t[slot]], writes=[self.wb_t[slot]])
        else:
            P.op("act", "copy", dict(out=self.wb[slot][:], in_=self.wst[slot][:]),
                 reads=[self.wst_t[slot]], writes=[self.wb_t[slot]])
        return slot


def stage_c1(cx, j_idx, xT_in, cv_out):
    cfg = cx.cfg
    P = cx.P
    with Stage(cx, "c1_%d" % j_idx) as st:
        cv = cx.dt(cv_out, [cfg.nt, D], F32)
        (cols, hm, ident), cols_t = load_consts(cx, st, [("a%d_cols" % j_idx, [128, DC, 34]), ("hm", [128, 1]),
                                                         ("ident_f32", [128, 128])])
        hb = st.sb([128, DC, 1], F32, "hb")
        wdh = st.sb([128, DC, CONVK], F32, "wdh")
        d_t = Trk()
        P.op("pool", "tensor_scalar", dict(out=hb[:], in0=cols[:, :, 1:2], scalar1=0.5, scalar2=None, op0=ALU.mult),
             reads=[cols_t], writes=[d_t])
        d2_t = Trk()
        P.op("pool", "tensor_scalar", dict(out=wdh[:], in0=cols[:, :, 3:34], scalar1=0.5, scalar2=None, op0=ALU.mult),
             reads=[cols_t], writes=[d2_t])
        xT, xT_t = load_xT_resident(cx, st, xT_in)
        wl = ChunkLoader(cx, st, cx.dt("a%d_win" % j_idx, [2 * DC, 128, DC, 128], F32))
        aps = [st.ps([128, 512], F32, "a") for _ in range(2)]
        aps_t = [Trk() for _ in range(2)]
        gps = [st.ps([128, 512], F32, "g") for _ in range(2)]
        gps_t = [Trk() for _ in range(2)]
        tps = [st.ps([128, 512], F32, "tp") for _ in range(2)]
        tps_t = [Trk() for _ in range(2)]
        th = [st.sb([128, 512], F32, "th") for _ in range(2)]
        th_t = [Trk() for _ in range(2)]
        asb = [st.sb([128, 512], F32, "asb") for _ in range(2)]
        asb_t = [Trk() for _ in range(2)]
        hbuf = [st.sb([128, 30 + 512], F32, "h") for _ in range(2)]
        hbuf_t = [Trk() for _ in range(2)]
        acc1 = [st.sb([128, 512], F32, "acc1") for _ in range(2)]
        acc1_t = [Trk() for _ in range(2)]
        acc2 = [st.sb([128, 512], F32, "acc2") for _ in range(2)]
        acc2_t = [Trk() for _ in range(2)]
        ct = [st.sb([128, 4, 128], F32, "ct") for _ in range(2)]
        ct_t = [Trk() for _ in range(2)]
        ct_ds = [P.dsem() for _ in range(2)]
        NPE = 16
        cps = [st.ps([128, 512], F32, "cps") for _ in range(2)]
        cps_t = [Trk() for _ in range(2)]
        h16 = [st.sb([128, 30 + 512], BF16, "h16") for _ in range(2)]
        h16_t = [Trk() for _ in range(2)]
        dg = [st.sb([128, NPE, 128], BF16, "dg") for _ in range(2)]
        dg_t = [Trk() for _ in range(2)]
        pending = [(wl.load(0), wl.load(DC))]
        it = 0
        for j in range(DC):
            if j + 1 < DC:
                pending.append((wl.load(j + 1), wl.load(DC + j + 1)))
            sa, sg = pending.pop(0)
            dgj, dgj_t = dg[j % 2], dg_t[j % 2]
            for k in range(NPE):
                P.op("pool", "tensor_scalar", dict(out=dgj[:, k, :], in0=ident[:], scalar1=wdh[:, j, k:k + 1], scalar2=None,
                                                   op0=ALU.mult), reads=[cols_t, d2_t], writes=[dgj_t])
            for ti, (t0, W) in enumerate(cfg.tiles):
                i = it % 2
                it += 1
                for (slot, pst, pstt) in ((sa, aps[i], aps_t[i]), (sg, gps[i], gps_t[i])):
                    for k in range(DC):
                        P.op("pe", "matmul", dict(out=pst[:, 0:W], lhsT=wl.wb[slot][:, k, :], rhs=xT[ti][:, k, 0:W],
                                                  start=(k == 0), stop=(k == DC - 1)),
                             reads=[wl.wb_t[slot], xT_t[ti]], writes=[pstt], signal=(k == DC - 1))
                P.op("act", "activation", dict(out=th[i][:, 0:W], in_=gps[i][:, 0:W], func=AF.Tanh, bias=hb[:, j, :], scale=0.5),
                     reads=[gps_t[i], d_t], writes=[th_t[i]])
                P.op("act", "activation", dict(out=asb[i][:, 0:W], in_=aps[i][:, 0:W], func=AF.Identity, bias=cols[:, j, 0:1],
                                               scale=1.0), reads=[aps_t[i], cols_t], writes=[asb_t[i]])
                h = hbuf[i]
                if ti == 0:
                    P.op("pool", "memset", dict(ap=h[:, 0:30], constant=0.0), writes=[hbuf_t[i]])
                else:
                    Wp = cfg.tiles[ti - 1][1]
                    P.op("pool", "tensor_copy", dict(out=h[:, 0:30], in_=hbuf[1 - i][:, Wp:Wp + 30]),
                         reads=[hbuf_t[1 - i]], writes=[hbuf_t[i]])
                P.op("dve", "scalar_tensor_tensor", dict(out=h[:, 30:30 + W], in0=th[i][:, 0:W], scalar=1.0, in1=asb[i][:, 0:W],
                                                         op0=ALU.add, op1=ALU.mult),
                     reads=[th_t[i], asb_t[i]], writes=[hbuf_t[i]])
                if ti == 0:
                    P.op("dve", "tensor_scalar", dict(out=h[:, 30:30 + W], in0=h[:, 30:30 + W], scalar1=hm[:, 0:1], scalar2=None,
                                                      op0=ALU.mult), reads=[cols_t], writes=[hbuf_t[i]])
                P.op("act", "copy", dict(out=h16[i][:, 0:30 + W], in_=h[:, 0:30 + W]), reads=[hbuf_t[i]], writes=[h16_t[i]])
                for k in range(NPE):
                    P.op("pe", "matmul", dict(out=cps[i][:, 0:W], lhsT=dgj[:, k, :], rhs=h16[i][:, k:k + W],
                                              start=(k == 0), stop=(k == NPE - 1)),
                         reads=[dgj_t, h16_t[i]], writes=[cps_t[i]], signal=(k == NPE - 1))
                a1, a2 = acc1[i], acc2[i]
                P.op("dve", "tensor_scalar", dict(out=a1[:, 0:W], in0=h[:, 30:30 + W], scalar1=wdh[:, j, 30:31],
                                                  scalar2=cols[:, j, 2:3], op0=ALU.mult, op1=ALU.add),
                     reads=[hbuf_t[i], d2_t, cols_t], writes=[acc1_t[i]])
                P.op("dve", "tensor_scalar", dict(out=a2[:, 0:W], in0=h[:, 29:29 + W], scalar1=wdh[:, j, 29:30], scalar2=None,
                                                  op0=ALU.mult), reads=[hbuf_t[i], d2_t], writes=[acc2_t[i]])
                for k in range(28, NPE - 1, -1):
                    a, at = (a1, acc1_t[i]) if k % 2 == 0 else (a2, acc2_t[i])
                    P.op("dve", "scalar_tensor_tensor", dict(out=a[:, 0:W], in0=h[:, k:k + W], scalar=wdh[:, j, k:k + 1],
                                                             in1=a[:, 0:W], op0=ALU.mult, op1=ALU.add),
                         reads=[hbuf_t[i], at], writes=[at])
                P.op("dve", "tensor_tensor", dict(out=a1[:, 0:W], in0=a1[:, 0:W], in1=a2[:, 0:W], op=ALU.add),
                     reads=[acc1_t[i], acc2_t[i]], writes=[acc1_t[i]])
                P.op("dve", "tensor_tensor", dict(out=a1[:, 0:W], in0=cps[i][:, 0:W], in1=a1[:, 0:W], op=ALU.add),
                     reads=[acc1_t[i], cps_t[i]], writes=[acc1_t[i]])
                ns = W // 128
                for s in range(ns):
                    P.op("pe", "transpose", dict(out=tps[i][:, s * 128:(s + 1) * 128], in_=a1[:, s * 128:(s + 1) * 128],
                                                 identity=ident[:]),
                         reads=[acc1_t[i], cols_t], writes=[tps_t[i]], signal=(s == ns - 1))
                P.op("act", "copy", dict(out=ct[i][:, 0:ns, :], in_=tps[i][:, 0:W].rearrange("p (s c) -> p s c", s=ns)),
                     reads=[tps_t[i]], writes=[ct_t[i]])
                P.dma("sp", [(cv[t0:t0 + W, j * 128:(j + 1) * 128].rearrange("(s p) c -> p s c", p=128), ct[i][:, 0:ns, :])],
                      ct_ds[i], reads=[ct_t[i]], writes=[cx.trk(cv_out, (ti, j))])


def stage_tm_proj(cx, name, prefix, src_name, src_dtype, prenorm, cols_name, w_name, xres_in, xres_out, xT_out):
    cfg = cx.cfg
    P = cx.P
    with Stage(cx, name) as st:
        src = cx.dt(src_name, [cfg.nt, D], src_dtype)
        wsrc = cx.dt(w_name, [128, DC, D], F32)
        wd = st.sb([128, DC, D], BF16, "wd")
        wd_t = Trk()
        load_w_rows(cx, st, wd, wd_t, wsrc, DC)
        ep = Epi(cx, st, prefix, 0, xres_in, xres_out, xT_out)
        ds = P.dsem()
        identb = st.sb([128, 128], BF16, "identb")
        identb_t = Trk()
        pairs = [(identb[:], cx.dt("ident_bf16", [128, 128], BF16))]
        if prenorm:
            lcols = st.sb([128, DC, 2], F32, "lcols")
            pairs.append((lcols[:], cx.dt(cols_name, [128, DC, 2], F32)))
        P.dma("sp", pairs, ds, writes=[identb_t])
        NA = 2
        ib = [st.sb([128, 4, D], src_dtype, "in") for _ in range(NA)]
        ib_t = [Trk() for _ in range(NA)]
        ib_ds = [P.dsem() for _ in range(NA)]
        xb = [st.sb([128, D], BF16, "xb") for _ in range(2)]
        xb_t = [Trk() for _ in range(2)]
        stat = [st.sb([128, 16], F32, "pstat") for _ in range(2)]
        stat_t = [Trk() for _ in range(2)]
        zT = [st.sb([128, DC, 128], BF16, "zT") for _ in range(2)]
        zT_t = [Trk() for _ in range(2)]
        tpb = [st.ps([128, D], BF16, "tpb") for _ in range(2)]
        tpb_t = [Trk() for _ in range(2)]
        yps = [st.ps([128, D], F32, "y") for _ in range(2)]
        yps_t = [Trk() for _ in range(2)]

        def load_in(ti):
            t0, W = cfg.tiles[ti]
            slot = ti % NA
            reads = [cx.trk(src_name, (ti, j)) for j in range(DC)]
            P.dma("sp", [(ib[slot][:, 0:W // 128, :], src[t0:t0 + W, :].rearrange("(s p) d -> p s d", p=128))], ib_ds[slot],
                  reads=reads, writes=[ib_t[slot]])
            ep.load_xold(ti)
        subs = [(ti, s) for ti, (t0, W) in enumerate(cfg.tiles) for s in range(W // 128)]
        loaded = set()

        def ensure_loaded(ti):
            if ti < len(cfg.tiles) and ti not in loaded:
                load_in(ti)
                loaded.add(ti)

        def phase_a(n):
            ti, s = subs[n]
            slot = ti % NA
            i = n % 2
            ensure_loaded(ti)
            if s == 1:
                ensure_loaded(ti + 1)
            xin = ib[slot][:, s, :]
            if prenorm:
                stt = stat[i]
                for h in range(2):
                    P.op("dve", "bn_stats", dict(out=stt[:, h * 6:(h + 1) * 6], in_=xin[:, h * 512:(h + 1) * 512]),
                         reads=[ib_t[slot]], writes=[stat_t[i]])
                P.op("dve", "bn_aggr", dict(out=stt[:, 12:14], in_=stt[:, 0:12]), reads=[stat_t[i]], writes=[stat_t[i]])
                P.op("pool", "tensor_scalar", dict(out=stt[:, 14:15], in0=stt[:, 13:14], scalar1=float(LN_EPS), scalar2=None,
                                                   op0=ALU.add), reads=[stat_t[i]], writes=[stat_t[i]])
                P.op("pool", "tensor_tensor", dict(out=stt[:, 14:15], in0=stt[:, 14:15], in1=ep.mh[:], op=ALU.pow),
                     reads=[stat_t[i], ep.mh_t], writes=[stat_t[i]])
                P.op("dve", "tensor_scalar", dict(out=xb[i][:], in0=xin, scalar1=stt[:, 12:13], scalar2=stt[:, 14:15],
                                                  op0=ALU.subtract, op1=ALU.mult),
                     reads=[ib_t[slot], stat_t[i]], writes=[xb_t[i]])
                tin_t = xb_t[i]
                tin_ap = lambda k: xb[i][:, k * 128:(k + 1) * 128]
            else:
                tin_t = ib_t[slot]
                tin_ap = lambda k: ib[slot][:, s, k * 128:(k + 1) * 128]
            for k in range(DC):
                P.op("pe", "transpose", dict(out=tpb[i][:, k * 128:(k + 1) * 128], in_=tin_ap(k), identity=identb[:]),
                     reads=[tin_t, identb_t], writes=[tpb_t[i]], signal=(k == DC - 1))
            if prenorm:
                for k in range(DC):
                    P.op("act", "activation", dict(out=zT[i][:, k, :], in_=tpb[i][:, k * 128:(k + 1) * 128], func=AF.Silu,
                                                   bias=lcols[:, k, 1:2], scale=lcols[:, k, 0:1]),
                         reads=[tpb_t[i], identb_t], writes=[zT_t[i]])
            else:
                P.op("act", "copy", dict(out=zT[i][:], in_=tpb[i][:].rearrange("p (k t) -> p k t", k=DC)),
                     reads=[tpb_t[i]], writes=[zT_t[i]])

        def phase_b(n):
            ti, s = subs[n]
            i = n % 2
            for h in range(2):
                for k in range(DC):
                    P.op("pe", "matmul", dict(out=yps[i][:, h * 512:(h + 1) * 512], lhsT=zT[i][:, k, :],
                                              rhs=wd[:, k, h * 512:(h + 1) * 512], start=(k == 0), stop=(k == DC - 1)),
                         reads=[zT_t[i], wd_t], writes=[yps_t[i]], signal=(k == DC - 1))
            return ep.sub_a(ti, s, yps[i], yps_t[i])

        phase_a(0)
        pend = None
        for n in range(len(subs)):
            if n + 1 < len(subs):
                phase_a(n + 1)
            st_a = phase_b(n)
            if pend is not None:
                ep.sub_b(pend)
            pend = st_a
        ep.sub_b(pend)


def stage_a1(cx, xT_in, qT_out, kT_out, v_out):
    cfg = cx.cfg
    P = cx.P
    with Stage(cx, "a1") as st:
        qT = cx.dt(qT_out, [D, cfg.nt], BF16)
        kT = cx.dt(kT_out, [D, cfg.nown], BF16)
        v = cx.dt(v_out, [cfg.nown, D], BF16)
        xT, xT_t = load_xT_resident(cx, st, xT_in)
        wl = ChunkLoader(cx, st, cx.dt("b0_wqk", [2 * DC, 128, DC, 128], F32))
        wv = st.sb([128, DC, D], BF16, "wv")
        wv_t = Trk()
        load_w_rows(cx, st, wv, wv_t, cx.dt("b0_wv", [128, DC, D], F32), DC)
        ps = [st.ps([128, 512], F32, "ps") for _ in range(2)]
        ps_t = [Trk() for _ in range(2)]
        ob = [st.sb([128, 512], BF16, "ob") for _ in range(2)]
        ob_t = [Trk() for _ in range(2)]
        ob_ds = [P.dsem() for _ in range(2)]
        vps = [st.ps([128, D], F32, "vps") for _ in range(2)]
        vps_t = [Trk() for _ in range(2)]
        vb = [st.sb([128, D], BF16, "vb") for _ in range(2)]
        vb_t = [Trk() for _ in range(2)]
        vb_ds = [P.dsem() for _ in range(2)]
        pending = [wl.load(0)]
        it = 0
        for j in range(2 * DC):
            if j + 1 < 2 * DC:
                pending.append(wl.load(j + 1))
            slot = pending.pop(0)
            for ti, (t0, W) in enumerate(cfg.tiles):
                if j >= DC and ti == 0:
                    continue
                i = it % 2
                it += 1
                for k in range(DC):
                    P.op("pe", "matmul", dict(out=ps[i][:, 0:W], lhsT=wl.wb[slot][:, k, :], rhs=xT[ti][:, k, 0:W],
                                              start=(k == 0), stop=(k == DC - 1)),
                         reads=[wl.wb_t[slot], xT_t[ti]], writes=[ps_t[i]], signal=(k == DC - 1))
                if it % 2 == 0:
                    P.op("act", "copy", dict(out=ob[i][:, 0:W], in_=ps[i][:, 0:W]), reads=[ps_t[i]], writes=[ob_t[i]])
                else:
                    P.op("dve", "tensor_copy", dict(out=ob[i][:, 0:W], in_=ps[i][:, 0:W]), reads=[ps_t[i]], writes=[ob_t[i]])
                if j < DC:
                    dst = qT[j * 128:(j + 1) * 128, t0:t0 + W]
                    key = (qT_out, (j, ti))
                else:
                    dst = kT[(j - DC) * 128:(j - DC + 1) * 128, t0 - HALO:t0 - HALO + W]
                    key = (kT_out, (j - DC, ti))
                P.dma("sp", [(dst, ob[i][:, 0:W])], ob_ds[i], reads=[ob_t[i]], writes=[cx.trk(*key)])
        cnt = 0
        for ti, (t0, W) in enumerate(cfg.tiles):
            if ti == 0:
                continue
            for s in range(W // 128):
                i = cnt % 2
                cnt += 1
                for h in range(2):
                    for k in range(DC):
                        P.op("pe", "matmul", dict(out=vps[i][:, h * 512:(h + 1) * 512], lhsT=xT[ti][:, k, s * 128:(s + 1) * 128],
                                                  rhs=wv[:, k, h * 512:(h + 1) * 512], start=(k == 0), stop=(k == DC - 1)),
                             reads=[xT_t[ti], wv_t], writes=[vps_t[i]], signal=(k == DC - 1))
                if cnt % 2 == 0:
                    P.op("act", "copy", dict(out=vb[i][:], in_=vps[i][:]), reads=[vps_t[i]], writes=[vb_t[i]])
                else:
                    P.op("dve", "tensor_copy", dict(out=vb[i][:], in_=vps[i][:]), reads=[vps_t[i]], writes=[vb_t[i]])
                o0 = t0 - HALO + s * 128
                P.dma("sp", [(v[o0:o0 + 128, :], vb[i][:])], vb_ds[i], reads=[vb_t[i]], writes=[cx.trk(v_out, (ti, s))])


def stage_a2(cx, li, qT_in, kT_own, kT_past, v_own, v_past, attn_out):
    cfg = cx.cfg
    P = cx.P
    nown = cfg.nown
    NKP = nown // 128
    lam_init = 0.8 - 0.6 * math.exp(-0.3 * li)
    with Stage(cx, "a2") as st:
        qT = cx.dt(qT_in, [D, cfg.nt], BF16)
        kTo = cx.dt(kT_own, [D, nown], BF16)
        vo = cx.dt(v_own, [nown, D], BF16)
        if "kpast_fn" in cx.views:
            kpast_fn, vpast_fn = cx.views["kpast_fn"], cx.views["vpast_fn"]
        else:
            kTp = cx.dt(kT_past, [D, nown], BF16)
            vp = cx.dt(v_past, [nown, D], BF16)
            kpast_fn = lambda h: kTp[h * 128:(h + 1) * 128, :]
            vpast_fn = lambda k0, k1, h: vp[k0 * 128:k1 * 128, h * 128:(h + 1) * 128]
        att = cx.dt(attn_out, [cfg.nt, D], BF16)
        (lamin, gsub_raw, pbias), c_t = load_consts(cx, st, [("b0_lam", [128, 4, 64]), ("b0_gsub", [128, 128]), ("pbias", [128, 1])])
        tri = st.sb([128, 4, 512], BF16, "tri")
        tri_t = Trk()
        P.dma("sp", [(tri[:], cx.dt("trimask", [128, 4, 512], BF16))], P.dsem(), writes=[tri_t])
        mh = st.sb([128, 1], F32, "mh")
        mh_t = Trk()
        P.op("pool", "memset", dict(ap=mh[:], constant=-0.5), writes=[mh_t])
        prod = st.sb([128, 2, 64], F32, "prod")
        sm0 = st.sb([128, 8], F32, "sm0")
        l_t = Trk()
        for c in range(2):
            P.op("dve", "tensor_tensor", dict(out=prod[:, c, :], in0=lamin[:, 2 * c, :], in1=lamin[:, 2 * c + 1, :], op=ALU.mult),
                 reads=[c_t], writes=[l_t])
            P.op("dve", "tensor_reduce", dict(out=sm0[:, c:c + 1], in_=prod[:, c, :], axis=mybir.AxisListType.X, op=ALU.add),
                 reads=[l_t], writes=[l_t])
        P.op("act", "activation", dict(out=sm0[:, 2:4], in_=sm0[:, 0:2], func=AF.Exp), reads=[l_t], writes=[l_t])
        nlam = st.sb([128, 1], F32, "nlam")
        gsub = st.sb([128, 128], F32, "gsub")
        P.op("dve", "tensor_tensor", dict(out=nlam[:], in0=sm0[:, 3:4], in1=sm0[:, 2:3], op=ALU.subtract), reads=[l_t], writes=[l_t])
        P.op("dve", "tensor_scalar", dict(out=nlam[:], in0=nlam[:], scalar1=float(-lam_init), scalar2=None, op0=ALU.add),
             reads=[l_t], writes=[l_t])
        P.op("dve", "tensor_scalar", dict(out=gsub[:], in0=gsub_raw[:], scalar1=float(1.0 - lam_init), scalar2=None, op0=ALU.mult),
             reads=[c_t, l_t], writes=[l_t])
        kp_sb = [st.sb([128, nown], BF16, "kp") for _ in range(2)]
        ko_sb = [st.sb([128, nown], BF16, "ko") for _ in range(2)]
        vp_sb = [st.sb([128, NKP, 129], BF16, "vp") for _ in range(2)]
        vo_sb = [st.sb([128, NKP, 129], BF16, "vo") for _ in range(2)]
        kv_t = [Trk() for _ in range(2)]
        kv_ds = [P.dsem() for _ in range(2)]
        for b in range(2):
            P.op("pool", "memset", dict(ap=vp_sb[b][:, :, 128:129], constant=1.0), writes=[kv_t[b]])
            P.op("pool", "memset", dict(ap=vo_sb[b][:, :, 128:129], constant=1.0), writes=[kv_t[b]])

        def load_head(h):
            b = h % 2
            pairs = [(kp_sb[b][:], kpast_fn(h)), (ko_sb[b][:], kTo[h * 128:(h + 1) * 128, :])]
            for k0 in range(0, NKP, 8):
                k1 = min(NKP, k0 + 8)
                pairs.append((vp_sb[b][:, k0:k1, 0:128], vpast_fn(k0, k1, h).rearrange("(kt p) e -> p kt e", p=128)))
                pairs.append((vo_sb[b][:, k0:k1, 0:128],
                              vo[k0 * 128:k1 * 128, h * 128:(h + 1) * 128].rearrange("(kt p) e -> p kt e", p=128)))
            reads = [cx.trk(kT_own, (h, ti)) for ti in range(len(cfg.tiles))] + \
                    [cx.trk(v_own, (ti, s)) for ti in range(len(cfg.tiles)) for s in range(4)] + [cx.trk(kT_past, 0), cx.trk(v_past, 0)]
            P.dma("sp", pairs, kv_ds[b], reads=reads, writes=[kv_t[b]])
        qb = [st.sb([128, 512], BF16, "qb") for _ in range(2)]
        qb_t = [Trk() for _ in range(2)]
        qb_ds = [P.dsem() for _ in range(2)]
        sps = [st.ps([128, 512], F32, "sps") for _ in range(4)]
        sps_t = [Trk() for _ in range(4)]
        acc = [st.ps([128, 512], F32, "acc") for _ in range(4)]
        acc_t = [Trk() for _ in range(4)]
        NF = 2
        ob = [st.sb([128, 128], F32, "o") for _ in range(NF)]
        junk = [st.sb([128, 128], F32, "junk") for _ in range(NF)]
        sm = [st.sb([128, 8], F32, "sm") for _ in range(NF)]
        onb = [st.sb([128, 128], BF16, "onb") for _ in range(NF)]
        f_t = [Trk() for _ in range(NF)]
        onb_t = [Trk() for _ in range(NF)]
        onb_ds = [P.dsem() for _ in range(NF)]
        NPT = 6
        pT = [st.sb([128, 512], BF16, "pT") for _ in range(NPT)]
        pT_t = [Trk() for _ in range(NPT)]
        units = []
        for h in range(8):
            for ti, (t0, W) in enumerate(cfg.tiles):
                q0 = nown - HALO + t0
                for kt in range((q0 + W) // 128):
                    units.append((h, ti, kt))
        state = {"qcnt": 0, "fcnt": 0, "scnt": 0, "pcnt": 0, "heads": set(), "q": {}}
        unit_bufs = {}

        def stage_s(u):
            h, ti, kt = units[u]
            t0, W = cfg.tiles[ti]
            q0 = nown - HALO + t0
            b = h % 2
            if (h, ti) not in state["q"]:
                qi = state["qcnt"] % 2
                state["qcnt"] += 1
                state["q"] = {(h, ti): qi}
                P.dma("sp", [(qb[qi][:, 0:W], qT[h * 128:(h + 1) * 128, t0:t0 + W])], qb_ds[qi],
                      reads=[cx.trk(qT_in, (h, ti))], writes=[qb_t[qi]])
            qi = state["q"][(h, ti)]
            k0 = kt * 128
            past = kt < NKP
            ksb = kp_sb[b] if past else ko_sb[b]
            kcol = k0 if past else k0 - nown
            d = k0 - q0
            rs = []
            for c in range(2):
                r = state["scnt"] % 4
                state["scnt"] += 1
                pr = state["pcnt"] % NPT
                state["pcnt"] += 1
                P.op("pe", "matmul", dict(out=sps[r][:, 0:W], lhsT=ksb[c * 64:(c + 1) * 64, kcol:kcol + 128],
                                          rhs=qb[qi][c * 64:(c + 1) * 64, 0:W], start=True, stop=True),
                     reads=[kv_t[b], qb_t[qi]], writes=[sps_t[r]])
                P.op("act", "activation", dict(out=pT[pr][:, 0:W], in_=sps[r][:, 0:W], func=AF.Exp, scale=0.125,
                                               bias=(pbias[:, 0:1] if past else 0.0)),
                     reads=[sps_t[r], c_t], writes=[pT_t[pr]])
                if d >= 0:
                    P.op("dve" if c == 0 else "pool", "tensor_tensor",
                         dict(out=pT[pr][:, 0:W], in0=pT[pr][:, 0:W], in1=tri[:, d // 128, 0:W], op=ALU.mult),
                         reads=[pT_t[pr], tri_t], writes=[pT_t[pr]])
                rs.append(pr)
            unit_bufs[u] = rs

        def stage_pv(u):
            h, ti, kt = units[u]
            t0, W = cfg.tiles[ti]
            q0 = nown - HALO + t0
            NS = W // 128
            b = h % 2
            k0 = kt * 128
            past = kt < NKP
            vsb = vp_sb[b] if past else vo_sb[b]
            vkt = kt if past else kt - NKP
            d = k0 - q0
            rs = unit_bufs.pop(u)
            if h not in state["heads"]:
                state["heads"].add(h)
                if h + 1 < 8:
                    load_head(h + 1)
            for c in range(2):
                pr = rs[c]
                js = [j for j in range(NS) if not (d >= 0 and j < d // 128)]
                for j in js:
                    lastkt = (q0 + j * 128) // 128
                    P.op("pe", "matmul", dict(out=acc[j][:, c * 129:(c + 1) * 129], lhsT=pT[pr][:, j * 128:(j + 1) * 128],
                                              rhs=vsb[:, vkt, 0:129], start=(kt == 0 and c == 0), stop=(kt == lastkt),
                                              skip_group_check=True),
                         reads=[pT_t[pr], kv_t[b]], writes=[acc_t[j]],
                         signal=(j == js[-1] or (kt == lastkt and c == 1)))
            for j in range(NS):
                if (q0 + j * 128) // 128 != kt:
                    continue
                fi = state["fcnt"] % NF
                state["fcnt"] += 1
                a = acc[j]
                o, s_, ft = ob[fi], sm[fi], f_t[fi]
                P.op("dve", "reciprocal", dict(out=s_[:, 0:1], in_=a[:, 128:129]), reads=[acc_t[j]], writes=[ft])
                P.op("dve", "reciprocal", dict(out=s_[:, 1:2], in_=a[:, 257:258]), reads=[acc_t[j]], writes=[ft])
                P.op("dve", "tensor_tensor", dict(out=s_[:, 2:3], in0=s_[:, 1:2], in1=nlam[:, 0:1], op=ALU.mult),
                     reads=[ft, l_t], writes=[ft])
                P.op("dve", "tensor_scalar", dict(out=o[:], in0=a[:, 0:128], scalar1=s_[:, 0:1], scalar2=None, op0=ALU.mult),
                     reads=[acc_t[j], ft], writes=[ft])
                P.op("dve", "scalar_tensor_tensor", dict(out=o[:], in0=a[:, 129:257], scalar=s_[:, 2:3], in1=o[:],
                                                         op0=ALU.mult, op1=ALU.add), reads=[acc_t[j], ft], writes=[ft])
                P.op("act", "activation", dict(out=junk[fi][:], in_=o[:], func=AF.Square, accum_out=s_[:, 3:4]),
                     reads=[ft], writes=[ft])
                P.op("pool", "tensor_scalar", dict(out=s_[:, 4:5], in0=s_[:, 3:4], scalar1=1.0 / 128.0, scalar2=float(RMS_EPS),
                                                   op0=ALU.mult, op1=ALU.add), reads=[ft], writes=[ft])
                P.op("pool", "tensor_tensor", dict(out=s_[:, 4:5], in0=s_[:, 4:5], in1=mh[:], op=ALU.pow),
                     reads=[ft, mh_t], writes=[ft])
                P.op("dve", "scalar_tensor_tensor", dict(out=onb[fi][:], in0=o[:], scalar=s_[:, 4:5], in1=gsub[:],
                                                         op0=ALU.mult, op1=ALU.mult), reads=[ft, l_t], writes=[onb_t[fi]])
                r0 = t0 + j * 128
                P.dma("sp", [(att[r0:r0 + 128, h * 128:(h + 1) * 128], onb[fi][:])], onb_ds[fi], reads=[onb_t[fi]],
                      writes=[cx.trk(attn_out, (ti, h))])

        load_head(0)
        stage_s(0)
        for u in range(len(units)):
            if u + 1 < len(units):
                stage_s(u + 1)
            stage_pv(u)


def stage_g1a(cx, xT_in, gu_out):
    cfg = cx.cfg
    P = cx.P
    with Stage(cx, "g1a") as st:
        gu = cx.dt(gu_out, [GH, cfg.nt], BF16)
        (cols,), cols_t = load_consts(cx, st, [("c0_ucols", [128, GC])])
        xT, xT_t = load_xT_resident(cx, st, xT_in)
        wl = ChunkLoader(cx, st, cx.dt("c0_wu", [GC, 128, DC, 128], F32))
        ps = [st.ps([128, 512], F32, "ps") for _ in range(2)]
        ps_t = [Trk() for _ in range(2)]
        ub = [st.sb([128, 512], BF16, "ub") for _ in range(2)]
        ub_t = [Trk() for _ in range(2)]
        ub_ds = [P.dsem() for _ in range(2)]
        pending = [wl.load(0)]
        it = 0
        for f in range(GC):
            if f + 1 < GC:
                pending.append(wl.load(f + 1))
            slot = pending.pop(0)
            for ti, (t0, W) in enumerate(cfg.tiles):
                i = it % 2
                it += 1
                for k in range(DC):
                    P.op("pe", "matmul", dict(out=ps[i][:, 0:W], lhsT=wl.wb[slot][:, k, :], rhs=xT[ti][:, k, 0:W],
                                              start=(k == 0), stop=(k == DC - 1)),
                         reads=[wl.wb_t[slot], xT_t[ti]], writes=[ps_t[i]], signal=(k == DC - 1))
                P.op("act", "activation", dict(out=ub[i][:, 0:W], in_=ps[i][:, 0:W], func=AF.Gelu, bias=cols[:, f:f + 1], scale=1.0),
                     reads=[ps_t[i], cols_t], writes=[ub_t[i]])
                P.dma("sp", [(gu[f * 128:(f + 1) * 128, t0:t0 + W], ub[i][:, 0:W])], ub_ds[i], reads=[ub_t[i]],
                      writes=[cx.trk(gu_out, (f, ti))])


def stage_g1b(cx, xT_in, gu_in, go_out):
    cfg = cx.cfg
    P = cx.P
    with Stage(cx, "g1b") as st:
        xTd = cx.dt(xT_in, [D, cfg.nt], BF16)
        gu = cx.dt(gu_in, [GH, cfg.nt], BF16)
        go = cx.dt(go_out, [GH, cfg.nt], BF16)
        (bvb, gcols, wsr, trim, bsb), c_t = load_consts(cx, st, [("c0_bvb", [128, GH]), ("c0_gcols", [128, GC, 2]),
                                                                ("c0_wsT", [128, 4, 128]), ("c0_trim", [128, 128]),
                                                                ("c0_bsb", [128, 4, 128])])
        wv = st.sb([128, DC, GH], BF16, "wv")
        wv_t = Trk()
        load_w_rows(cx, st, wv, wv_t, cx.dt("c0_wv", [128, DC, GH], F32), DC, width=GH)
        mh = st.sb([128, 1], F32, "mh")
        mh_t = Trk()
        P.op("pool", "memset", dict(ap=mh[:], constant=-0.5), writes=[mh_t])
        ones = st.sb([128, 128], BF16, "ones")
        s_t = Trk()
        P.op("pool", "memset", dict(ap=ones[:], constant=1.0), writes=[s_t])
        wsT = st.sb([128, 4, 128], BF16, "wsT")
        for g in range(4):
            P.op("dve", "tensor_tensor", dict(out=wsT[:, g, :], in0=wsr[:, g, :], in1=trim[:], op=ALU.mult), reads=[c_t], writes=[s_t])
        rs = st.ps([128, 512], F32, "rs")
        rs_t = Trk()
        for g in range(4):
            P.op("pe", "matmul", dict(out=rs[:, g * 128:(g + 1) * 128], lhsT=ones[:], rhs=wsT[:, g, :], start=True, stop=True),
                 reads=[s_t], writes=[rs_t], signal=(g == 3))
        E = st.sb([128, GC, 128], F32, "E")
        E_t = Trk()
        for cc in range(GC):
            g = cc // 6
            P.op("dve", "scalar_tensor_tensor", dict(out=E[:, cc, :], in0=rs[:, g * 128:(g + 1) * 128], scalar=gcols[:, cc, 1:2],
                                                     in1=bsb[:, g, :], op0=ALU.mult, op1=ALU.add), reads=[rs_t, c_t], writes=[E_t])
        xb = [st.sb([128, DC, 512], BF16, "xTt") for _ in range(2)]
        xb_t = [Trk() for _ in range(2)]
        xb_ds = [P.dsem() for _ in range(2)]
        vps = [st.ps([128, 512], F32, "vps") for _ in range(2)]
        vps_t = [Trk() for _ in range(2)]
        svps = [st.ps([128, 512], F32, "svps") for _ in range(2)]
        svps_t = [Trk() for _ in range(2)]
        vbuf = st.sb([128, GH], F32, "vbuf")
        vbuf_t = [Trk() for _ in range(6)]
        stt = st.sb([128, 48], F32, "stt")
        stt_t = Trk()
        vhat = [st.sb([128, GH], BF16, "vhat") for _ in range(2)]
        vhat_t = [Trk() for _ in range(2)]
        uTn = [st.sb([128, GC, 128], BF16, "uTn") for _ in range(2)]
        uTn_t = [Trk() for _ in range(2)]
        uTn_ds = [P.dsem() for _ in range(2)]
        tmp = st.sb([128, GC, 128], F32, "tmp")
        tmp_t = [Trk() for _ in range(6)]
        oTb = st.sb([128, GC, 512], BF16, "oTb")
        oTb_t = Trk()
        oTb_ds = P.dsem()

        def load_x(ti):
            t0, W = cfg.tiles[ti]
            P.dma("sp", [(xb[ti % 2][:, :, 0:W], xTd[:, t0:t0 + W].rearrange("(k p) t -> p k t", p=128))], xb_ds[ti % 2],
                  reads=[cx.trk(xT_in, ti)], writes=[xb_t[ti % 2]])
        load_x(0)
        cnt = 0
        vcnt = 0
        scnt = 0
        for ti, (t0, W) in enumerate(cfg.tiles):
            if ti + 1 < len(cfg.tiles):
                load_x(ti + 1)
            xt = xb[ti % 2]
            for s in range(W // 128):
                i = cnt % 2
                cnt += 1
                c0 = t0 + s * 128
                P.dma("sp", [(uTn[i][:], gu[:, c0:c0 + 128].rearrange("(k p) t -> p k t", p=128))], uTn_ds[i],
                      reads=[cx.trk(gu_in, (f, ti)) for f in range(GC)], writes=[uTn_t[i]])
                for fc in range(6):
                    vi = vcnt % 2
                    vcnt += 1
                    sl = slice(fc * 512, (fc + 1) * 512)
                    for k in range(DC):
                        P.op("pe", "matmul", dict(out=vps[vi][:], lhsT=xt[:, k, s * 128:(s + 1) * 128], rhs=wv[:, k, sl],
                                                  start=(k == 0), stop=(k == DC - 1)),
                             reads=[xb_t[ti % 2], wv_t], writes=[vps_t[vi]], signal=(k == DC - 1))
                    P.op("dve", "tensor_tensor", dict(out=vbuf[:, sl], in0=vps[vi][:], in1=bvb[:, sl], op=ALU.add),
                         reads=[vps_t[vi], c_t], writes=[vbuf_t[fc]])
                    P.op("act", "activation", dict(out=vbuf[:, sl], in_=vbuf[:, sl], func=AF.Gelu), reads=[vbuf_t[fc]], writes=[vbuf_t[fc]])
                    P.op("dve", "bn_stats", dict(out=stt[:, fc * 6:(fc + 1) * 6], in_=vbuf[:, sl]), reads=[vbuf_t[fc]], writes=[stt_t])
                P.op("dve", "bn_aggr", dict(out=stt[:, 36:38], in_=stt[:, 0:36]), reads=[stt_t], writes=[stt_t])
                P.op("pool", "tensor_scalar", dict(out=stt[:, 38:39], in0=stt[:, 37:38], scalar1=float(LN_EPS), scalar2=None, op0=ALU.add),
                     reads=[stt_t], writes=[stt_t])
                P.op("pool", "tensor_tensor", dict(out=stt[:, 38:39], in0=stt[:, 38:39], in1=mh[:], op=ALU.pow),
                     reads=[stt_t, mh_t], writes=[stt_t])
                vh = vhat[i]
                for hh in range(2):
                    sl = slice(hh * 1536, (hh + 1) * 1536)
                    P.op("dve", "tensor_scalar", dict(out=vh[:, sl], in0=vbuf[:, sl], scalar1=stt[:, 36:37], scalar2=stt[:, 38:39],
                                                      op0=ALU.subtract, op1=ALU.mult),
                         reads=[stt_t] + vbuf_t[3 * hh:3 * hh + 3], writes=[vhat_t[i]])
                for cg in range(6):
                    si = scnt % 2
                    scnt += 1
                    for q in range(4):
                        cc = cg * 4 + q
                        g = cc // 6
                        P.op("pe", "matmul", dict(out=svps[si][:, q * 128:(q + 1) * 128], lhsT=vh[:, cc * 128:(cc + 1) * 128],
                                                  rhs=wsT[:, g, :], start=True, stop=True),
                             reads=[vhat_t[i], s_t], writes=[svps_t[si]], signal=(q == 3))
                    for q in range(4):
                        cc = cg * 4 + q
                        P.op("dve", "scalar_tensor_tensor", dict(out=tmp[:, cc, :], in0=svps[si][:, q * 128:(q + 1) * 128],
                                                                 scalar=gcols[:, cc, 0:1], in1=E[:, cc, :], op0=ALU.mult, op1=ALU.add),
                             reads=[svps_t[si], c_t, E_t], writes=[tmp_t[cg]])
                    P.op("pool", "tensor_tensor", dict(out=oTb[:, cg * 4:cg * 4 + 4, s * 128:(s + 1) * 128], in0=tmp[:, cg * 4:cg * 4 + 4, :],
                                                       in1=uTn[i][:, cg * 4:cg * 4 + 4, :], op=ALU.mult),
                         reads=[tmp_t[cg], uTn_t[i]], writes=[oTb_t])
            P.dma("sp", [(go[:, t0:t0 + W].rearrange("(k p) t -> p k t", p=128), oTb[:, :, 0:W])], oTb_ds, reads=[oTb_t],
                  writes=[cx.trk(go_out, (k, ti)) for k in range(GC)])


def ffn_stages(li, xT_in, xres_in, xres_out, xT_out, final_out=None):
    return [
        lambda cx: stage_f1(cx, li, xT_in, "uT"),
        lambda cx: stage_proj(cx, "f2_%d" % li, "f%d" % li, li, "uT", FC, "f%d_wdn" % li, xres_in, xres_out, xT_out,
                              final_out=final_out),
    ]


def stages_part1():
    st = [lambda cx: stage_prep(cx, "x_in", "xT_a")]
    st += [lambda cx: stage_c1(cx, 0, "xT_a", "cv"),
           lambda cx: stage_tm_proj(cx, "c2_0", "m0", "cv", F32, True, "a0_lcols", "a0_wout", "x_in", "xr_a", "xT_b")]
    st += ffn_stages(0, "xT_b", "xr_a", "xr_b", "xT_a")
    st += [lambda cx: stage_a1(cx, "xT_a", "qT", "kT_own", "v_own")]
    return st


def stages_part2():
    st = [lambda cx: stage_a2(cx, 1, "qT", "kT_own", "kT_past", "v_own", "v_past", "att"),
          lambda cx: stage_tm_proj(cx, "a3", "m1", "att", BF16, False, None, "b0_wo", "xr_b", "xr_a", "xT_b")]
    st += ffn_stages(1, "xT_b", "xr_a", "xr_b", "xT_a")
    st += [lambda cx: stage_g1a(cx, "xT_a", "gu"),
           lambda cx: stage_g1b(cx, "xT_a", "gu", "go"),
           lambda cx: stage_proj(cx, "g2", "m2", 2, "go", GC, "c0_wo", "xr_b", "xr_a", "xT_b")]
    st += ffn_stages(2, "xT_b", "xr_a", "xr_b", "xT_a")
    st += [lambda cx: stage_c1(cx, 1, "xT_a", "cv"),
           lambda cx: stage_tm_proj(cx, "c2_1", "m3", "cv", F32, True, "a1_lcols", "a1_wout", "xr_b", "xr_a", "xT_b")]
    st += ffn_stages(3, "xT_b", "xr_a", None, None, final_out="out")
    return st


def stage_xchg(cx):
    cfg = cx.cfg
    P = cx.P
    nown = cfg.nown
    kT = cx.dt("kT_own", [D, nown], BF16)
    v = cx.dt("v_own", [nown, D], BF16)
    groups = [[0, 1], [2, 3], [4, 5], [6, 7]]
    KR = 256
    VR = min(1024, nown)
    P.barrier()
    t = Trk()
    kall, vall = [], []
    for p in range(D // KR):
        dst = cx.nc.dram_tensor("kT_all%d" % p, [2 * KR, nown], BF16).ap()
        P.op("pool", "collective_compute", dict(kind="AllGather", op=ALU.bypass, replica_groups=groups,
                                                ins=[kT[p * KR:(p + 1) * KR, :].opt()], outs=[dst.opt()]), writes=[t])
        kall.append(dst)
    for p in range(nown // VR):
        dst = cx.nc.dram_tensor("v_all%d" % p, [2 * VR, D], BF16).ap()
        P.op("pool", "collective_compute", dict(kind="AllGather", op=ALU.bypass, replica_groups=groups,
                                                ins=[v[p * VR:(p + 1) * VR, :].opt()], outs=[dst.opt()]), writes=[t])
        vall.append(dst)

    def kpast_fn(h):
        r0 = (h % 2) * 128
        return kall[h // 2][r0:r0 + 128, :]

    def vpast_fn(k0, k1, h):
        p = (k0 * 128) // VR
        assert (k1 * 128 - 1) // VR == p
        r0 = k0 * 128 - p * VR
        return vall[p][r0:r0 + (k1 - k0) * 128, h * 128:(h + 1) * 128]
    cx.views["kpast_fn"] = kpast_fn
    cx.views["vpast_fn"] = vpast_fn
    P.barrier()


SCRATCH = {"xT_a", "xT_b", "xr_a", "xr_b", "cv", "uT", "qT", "kT_own", "v_own", "kT_past", "v_past", "att", "gu", "go",
           "kT_all", "v_all"}
FUSED = True
_PROG_CACHE = {}


def _get_prog(key, cfg, stages, ext_out, internal):
    if key not in _PROG_CACHE:
        _PROG_CACHE[key] = build_program(cfg, stages, None, ext_out, internal=internal, want_names=True)
    return _PROG_CACHE[key]


def kernel(**inputs):
    inputs = {k: np.asarray(v) for k, v in inputs.items()}
    x = inputs["x"].astype(np.float32, copy=False)
    B, S, _ = x.shape
    ncore = 8
    nown = S // 2
    cfg = Cfg(nown)
    consts = host_consts(inputs)
    per_core = []
    for c in range(ncore):
        b, half = c // 2, c % 2
        if half == 0:
            xin = np.concatenate([np.zeros((HALO, D), np.float32), x[b, 0:nown]], axis=0)
        else:
            xin = x[b, nown - HALO:2 * nown]
        per_core.append({
            "x_in": np.ascontiguousarray(xin),
            "hm": np.full((128, 1), float(half), np.float32),
            "pbias": np.full((128, 1), 0.0 if half == 1 else -80.0, np.float32),
        })
    if FUSED:
        nc, _, names = _get_prog(("fused", nown), cfg, stages_part1() + [stage_xchg] + stages_part2(), {"out"}, SCRATCH)
        maps = []
        for c in range(ncore):
            maps.append({k: (per_core[c][k] if k in per_core[c] else consts[k]) for k in names})
        res = run_bass_kernel_spmd(nc, maps, core_ids=list(range(ncore))).results
        out = np.empty((B, S, D), np.float32)
        for c in range(ncore):
            b, half = c // 2, c % 2
            out[b, half * nown:(half + 1) * nown] = res[c]["out"]
        return out
    out1 = {"xr_b", "qT", "kT_own", "v_own"}
    nc1, _, names1 = _get_prog(("p1", nown), cfg, stages_part1(), out1, SCRATCH - out1)
    maps1 = []
    for c in range(ncore):
        m = {}
        for k in names1:
            m[k] = per_core[c][k] if k in per_core[c] else consts[k]
        maps1.append(m)
    res1 = run_bass_kernel_spmd(nc1, maps1, core_ids=list(range(ncore))).results
    out2 = {"out"}
    in2 = {"xr_b", "qT", "kT_own", "v_own", "kT_past", "v_past"}
    nc2, _, names2 = _get_prog(("p2", nown), cfg, stages_part2(), out2, SCRATCH - in2)
    maps2 = []
    for c in range(ncore):
        m = {}
        src = res1[c - (c % 2)]
        for k in names2:
            if k == "kT_past":
                m[k] = src["kT_own"]
            elif k == "v_past":
                m[k] = src["v_own"]
            elif k in ("xr_b", "qT", "kT_own", "v_own"):
                m[k] = res1[c][k]
            elif k in per_core[c]:
                m[k] = per_core[c][k]
            else:
                m[k] = consts[k]
        maps2.append(m)
    res2 = run_bass_kernel_spmd(nc2, maps2, core_ids=list(range(ncore))).results
    out = np.empty((B, S, D), np.float32)
    for c in range(ncore):
        b, half = c // 2, c % 2
        out[b, half * nown:(half + 1) * nown] = res2[c]["out"]
    return out
```
